# Optimizing a Trainium2 kernel written in Bass

```python
import jax, jax.numpy as jnp
from jax import lax
import numpy as np

D_MODEL = 1024
BATCH = 4
SEQ = 4096
DEPTH = 1

N_META = 16
NORM_EPS = 1e-6
M_HEADS = 4
M_V = 2 * D_MODEL
M_V_HEAD = M_V // M_HEADS
M_QK_HEAD = M_V_HEAD // 2
M_QK = M_HEADS * M_QK_HEAD
CHUNK = 64
GATE_SOFTCAP = 15.0
S_WIDTH = 2 * D_MODEL
CONV_W = 3
IN_SPLITS = (M_QK, M_QK, M_V, M_V, M_V, M_HEADS, M_HEADS,
             S_WIDTH, S_WIDTH, S_WIDTH, S_WIDTH, D_MODEL, D_MODEL)
N_IN = 2 * M_QK + 3 * M_V + 2 * M_HEADS + 4 * S_WIDTH + 2 * D_MODEL

kernel_name = "meta_mlstm_shortconv_gated_hybrid"


def rmsnorm(x, w):
    xf = x.astype(jnp.float32)
    y = xf * lax.rsqrt(jnp.mean(xf * xf, axis=-1, keepdims=True) + NORM_EPS)
    return (y * w.astype(jnp.float32)).astype(x.dtype)


def softcap(x, cap):
    return cap * jnp.tanh(x / cap)


def causal_depthwise_conv(u, w):
    k, c = w.shape
    return lax.conv_general_dilated(
        u, w[:, None, :].astype(u.dtype), window_strides=(1,), padding=[(k - 1, 0)],
        dimension_numbers=("NWC", "WIO", "NWC"), feature_group_count=c)


def mlstm_chunkwise(q, k, v, li, lf):
    bsz, seq_len, n_heads, dk = q.shape
    dv = v.shape[-1]
    pad = CHUNK - N_META

    def padt(a, val):
        return jnp.pad(a, [(0, 0), (pad, 0)] + [(0, 0)] * (a.ndim - 2), constant_values=val)

    q = padt(q * (dk ** -0.5), 0.0)
    k = padt(k, 0.0)
    v = padt(v, 0.0)
    lf = padt(lf, 0.0)
    li = padt(li, -jnp.inf)
    n_chunks = (seq_len + pad) // CHUNK

    def to_chunks(a):
        a = a.reshape((bsz, n_chunks, CHUNK) + a.shape[2:])
        return jnp.moveaxis(a, (1, 3), (0, 2))

    xs = tuple(to_chunks(a) for a in (q, k, v, li, lf))
    causal = jnp.tril(jnp.ones((CHUNK, CHUNK), dtype=bool))

    def step(carry, inp):
        c_prev, n_prev, m_prev = carry
        qc, kc, vc, lic, lfc = inp
        b = jnp.cumsum(lfc, axis=-1)
        dlog = jnp.where(causal, b[..., :, None] - b[..., None, :] + lic[..., None, :], -jnp.inf)
        inter = b + m_prev[..., None]
        m_t = jnp.maximum(inter, jnp.max(dlog, axis=-1))
        w = jnp.exp(dlog - m_t[..., None]) * jnp.einsum("bhtd,bhsd->bhts", qc, kc)
        wi = jnp.exp(inter - m_t)
        num = (wi[..., None] * jnp.einsum("bhtd,bhde->bhte", qc, c_prev)
               + jnp.einsum("bhts,bhse->bhte", w, vc))
        den = wi * jnp.einsum("bhtd,bhd->bht", qc, n_prev) + jnp.sum(w, axis=-1)
        h = num / jnp.maximum(jnp.abs(den), jnp.exp(-m_t))[..., None]
        b_last = b[..., -1]
        wlog = b_last[..., None] - b + lic
        m_new = jnp.maximum(b_last + m_prev, jnp.max(wlog, axis=-1))
        decay = jnp.exp(b_last + m_prev - m_new)
        kw = jnp.exp(wlog - m_new[..., None])[..., None] * kc
        c_new = decay[..., None, None] * c_prev + jnp.einsum("bhsd,bhse->bhde", kw, vc)
        n_new = decay[..., None] * n_prev + jnp.sum(kw, axis=-2)
        return (c_new, n_new, m_new), h

    init = (jnp.zeros((bsz, n_heads, dk, dv), jnp.float32),
            jnp.zeros((bsz, n_heads, dk), jnp.float32),
            jnp.zeros((bsz, n_heads), jnp.float32))
    _, h = lax.scan(step, init, xs)
    h = jnp.moveaxis(h, (0, 2), (1, 3)).reshape(bsz, n_chunks * CHUNK, n_heads, dv)
    return h[:, pad:]


def hybrid_layer(x, norm_w, w_in, b_igate, b_fgate, mh_norm_w, conv_w,
                 w_proj_a, w_proj_b, w_out):
    bsz, seq_len, _ = x.shape
    f32 = jnp.float32
    xn = rmsnorm(x, norm_w)
    proj = xn @ w_in
    split_at = np.cumsum(IN_SPLITS)[:-1].tolist()
    (q, k, v, o_pre, z_a, i_pre, f_pre,
     s_h, s_b, s_c, z_s, g_a, g_b) = jnp.split(proj, split_at, axis=-1)

    qh = q.reshape(bsz, seq_len, M_HEADS, M_QK_HEAD).astype(f32)
    kh = k.reshape(bsz, seq_len, M_HEADS, M_QK_HEAD).astype(f32)
    vh = v.reshape(bsz, seq_len, M_HEADS, M_V_HEAD).astype(f32)
    li = softcap(i_pre.astype(f32) + b_igate.astype(f32), GATE_SOFTCAP)
    lf = jax.nn.log_sigmoid(softcap(f_pre.astype(f32) + b_fgate.astype(f32), GATE_SOFTCAP))
    h = mlstm_chunkwise(qh, kh, vh, li, lf)
    h = h * lax.rsqrt(jnp.mean(h * h, axis=-1, keepdims=True) + NORM_EPS)
    h = (h.reshape(bsz, seq_len, M_V) * mh_norm_w.astype(f32)).astype(x.dtype)
    a = jax.nn.silu(z_a) * (jax.nn.sigmoid(o_pre) * h)

    y_s = jax.nn.silu(z_s) * (s_b * causal_depthwise_conv(s_c * s_h, conv_w))

    merged = (jax.nn.sigmoid(g_a) * (a @ w_proj_a)
              + jax.nn.sigmoid(g_b) * (y_s @ w_proj_b))
    return x + merged @ w_out


def setup_inputs(seed: int = 0) -> dict:
    key = jax.random.key(seed)
    ks = jax.random.split(key, 13)
    nrm = jax.random.normal
    return {
        "x": nrm(ks[0], (BATCH, SEQ, D_MODEL), jnp.float32),
        "meta_tokens": nrm(ks[1], (N_META, D_MODEL), jnp.float32),
        "norm_w": 1.0 + 0.02 * nrm(ks[2], (DEPTH, D_MODEL), jnp.float32),
        "w_in": nrm(ks[3], (DEPTH, D_MODEL, N_IN), jnp.float32) * D_MODEL ** -0.5,
        "b_igate": 0.1 * nrm(ks[4], (DEPTH, M_HEADS), jnp.float32),
        "b_fgate": jnp.linspace(3.0, 6.0, M_HEADS, dtype=jnp.float32)[None, :]
                   + 0.1 * nrm(ks[5], (DEPTH, M_HEADS), jnp.float32),
        "mh_norm_w": 1.0 + 0.02 * nrm(ks[6], (DEPTH, M_V), jnp.float32),
        "conv_w": nrm(ks[7], (DEPTH, CONV_W, S_WIDTH), jnp.float32) * CONV_W ** -0.5,
        "w_proj_a": nrm(ks[8], (DEPTH, M_V, D_MODEL), jnp.float32) * M_V ** -0.5,
        "w_proj_b": nrm(ks[9], (DEPTH, S_WIDTH, D_MODEL), jnp.float32) * S_WIDTH ** -0.5,
        "w_out": nrm(ks[10], (DEPTH, D_MODEL, D_MODEL), jnp.float32) * D_MODEL ** -0.5,
        "final_norm_w": 1.0 + 0.02 * nrm(ks[11], (D_MODEL,), jnp.float32),
    }


def reference(x, meta_tokens, norm_w, w_in, b_igate, b_fgate, mh_norm_w, conv_w,
              w_proj_a, w_proj_b, w_out, final_norm_w):
    bsz = x.shape[0]
    meta = jnp.broadcast_to(meta_tokens[None].astype(x.dtype), (bsz, N_META, D_MODEL))
    h = jnp.concatenate([meta, x], axis=1)
    for layer in range(DEPTH):
        h = hybrid_layer(h, norm_w[layer], w_in[layer], b_igate[layer], b_fgate[layer],
                         mh_norm_w[layer], conv_w[layer], w_proj_a[layer],
                         w_proj_b[layer], w_out[layer])
    return rmsnorm(h, final_norm_w)[:, N_META:]
```

```python
import numpy as np
import concourse.bass as bass
import concourse.mybir as mybir
from concourse.bass_utils import run_bass_kernel_spmd

F32 = mybir.dt.float32
BF16 = mybir.dt.bfloat16
ALU = mybir.AluOpType
AF = mybir.ActivationFunctionType

D = 1024
NIN = 18440
EPS = 1e-6
OFF_Q, OFF_K, OFF_V, OFF_O, OFF_ZA, OFF_I = 0, 1024, 2048, 4096, 6144, 8192
OFF_SH, OFF_SB, OFF_SC, OFF_ZS, OFF_GA, OFF_GB = 8200, 10248, 12296, 14344, 16392, 17416

SELF_SYNC = {"pool", "dve", "act"}


class Tok:
    __slots__ = ("w", "r", "excl")

    def __init__(self, excl=False):
        self.w = None
        self.r = []
        self.excl = excl


class SemC:
    __slots__ = ("sem", "val", "owner")

    def __init__(self, sem, owner=None):
        self.sem = sem
        self.val = 0
        self.owner = owner


class EngQ:
    def __init__(self, name, semc):
        self.name = name
        self.semc = semc
        semc.owner = self
        self.seen = {}
        self.prog = []


class Sched:
    def __init__(self, nc):
        self.nc = nc
        self.q = {}
        for n in ("pe", "act", "dve", "pool", "sp"):
            self.q[n] = EngQ(n, SemC(nc.alloc_semaphore(name="q_" + n)))

    def new_sem(self, name):
        self.nsem = getattr(self, "nsem", 0) + 1
        return SemC(self.nc.alloc_semaphore(name=f"{name}_{self.nsem}"))

    def _waits(self, q, reads, writes):
        need = {}

        def add(d):
            if d is None:
                return
            s, v = d
            if s.owner is q and q.name not in SELF_SYNC:
                return
            if q.seen.get(s, 0) >= v:
                return
            if need.get(s, 0) < v:
                need[s] = v

        for t in reads:
            add(t.w)
        for t in writes:
            add(t.w)
            for d in t.r:
                add(d)
        for s, v in need.items():
            q.seen[s] = v
        return list(need.items())

    def op(self, qn, fn, reads=(), writes=()):
        q = self.q[qn]
        writes = list(dict.fromkeys(list(writes) + [t for t in reads if t.excl]))
        reads = [t for t in reads if not t.excl]
        waits = self._waits(q, reads, writes)
        q.semc.val += 1
        me = (q.semc, q.semc.val)
        for t in writes:
            t.w = me
            t.r = []
        for t in reads:
            t.r.append(me)
        sem = q.semc.sem

        def thunk(eng):
            for s, v in waits:
                eng.wait_ge(s.sem, v)
            fn(eng).then_inc(sem, 1)
        q.prog.append(thunk)

    def dma(self, qn, fn, semc, reads=(), writes=(), n=1):
        q = self.q[qn]
        waits = self._waits(q, reads, writes)
        semc.val += 16 * n
        me = (semc, semc.val)
        for t in writes:
            t.w = me
            t.r = []
        for t in reads:
            t.r.append(me)
        sem = semc.sem

        def thunk(eng):
            for s, v in waits:
                eng.wait_ge(s.sem, v)
            fn(eng, sem)
        q.prog.append(thunk)

    def wait_all(self, qn, toks):
        q = self.q[qn]
        waits = self._waits(q, toks, ())

        def thunk(eng):
            for s, v in waits:
                eng.wait_ge(s.sem, v)
        q.prog.append(thunk)

    def emit(self):
        nc = self.nc
        progs = self.q
        with nc.Block() as block:
            @block.tensor
            def _(e):
                for t in progs["pe"].prog:
                    t(e)

            @block.scalar
            def _(e):
                for t in progs["act"].prog:
                    t(e)

            @block.vector
            def _(e):
                for t in progs["dve"].prog:
                    t(e)

            @block.gpsimd
            def _(e):
                for t in progs["pool"].prog:
                    t(e)

            @block.sync
            def _(e):
                for t in progs["sp"].prog:
                    t(e)


def build(TM, PC):
    nc = bass.Bass("TRN2", target_bir_lowering=False)
    S = Sched(nc)
    NT = TM * 512
    NP = PC * 128

    def din(name, shape):
        return nc.dram_tensor(name, shape, F32, kind="ExternalInput").ap()

    xm = din("xm", [NT, D])
    xp = din("xp", [NP, D])
    w_in = din("w_in", [D, NIN])
    wa = din("wa", [2048, D])
    wb = din("wb", [2048, D])
    wo = din("wo", [D, D])
    norm_w = din("norm_w", [128, 8])
    bias_d = din("bias8", [128, 8])
    mhw_d = din("mhw", [128, 16])
    cw_d = din("cw", [128, 48])
    fnw_d = din("fnw", [128, D])
    y_out = nc.dram_tensor("y", [NT, D], F32, kind="ExternalOutput").ap()

    def sb(name, shape, dt):
        return nc.alloc_sbuf_tensor(name, shape, dt).ap()

    PE = lambda fn, r=(), w=(): S.op("pe", fn, r, w)
    ACT = lambda fn, r=(), w=(): S.op("act", fn, r, w)
    DVE = lambda fn, r=(), w=(): S.op("dve", fn, r, w)
    POOL = lambda fn, r=(), w=(): S.op("pool", fn, r, w)

    NSLOT = 6
    ring = [sb(f"ring{i}", [128, 4096], BF16) for i in range(NSLOT)]
    t_ring = [Tok() for _ in range(NSLOT)]
    sem_ring = [S.new_sem(f"ring{i}") for i in range(NSLOT)]
    Cst = sb("Cst", [128, 4, 2, 512], F32)
    Cbf = sb("Cbf", [128, 4, 2, 512], BF16)
    t_C = [[Tok() for _ in range(2)] for _ in range(4)]
    t_Cbf = [[Tok() for _ in range(2)] for _ in range(4)]
    nst = sb("nst", [128, 8], F32)
    nbc = sb("nbc", [128, 8, 128], BF16)
    t_n, t_nbc = Tok(), Tok()
    xnTs = [sb(f"xnT{i}", [128, 8, 512], BF16) for i in range(2)]
    t_xnTs = [[Tok() for _ in range(4)] for _ in range(2)]

    class XC:
        x = xnTs[1]
        t = t_xnTs[1]
    xh = sb("xh", [128, 8, 2], BF16)
    t_xh = Tok()
    xst = [sb(f"xst{i}", [128, D], F32) for i in range(2)]
    t_xst = [Tok() for _ in range(2)]
    sem_x = [S.new_sem(f"x{i}") for i in range(2)]
    sem_xs = [S.new_sem(f"xs{i}") for i in range(2)]
    xs = [sb(f"xs{i}", [128, D], BF16) for i in range(2)]
    t_xs = [Tok() for _ in range(2)]
    xsc = sb("xsc", [128, 8], F32)
    t_xsc = [Tok() for _ in range(2)]
    xk = [0]
    qT = sb("qT", [128, 8, 512], BF16)
    kT = sb("kT", [128, 8, 512], BF16)
    t_qT = [Tok() for _ in range(8)]
    t_kT = [Tok() for _ in range(8)]
    ktok = [sb(f"ktok{i}", [128, 1024], BF16) for i in range(2)]
    vtok = [sb(f"vtok{i}", [128, 2048], BF16) for i in range(4)]
    t_ktok = [[Tok() for _ in range(4)] for _ in range(4)]
    t_vtok = [[Tok() for _ in range(4)] for _ in range(4)]
    hnT = sb("hnT", [128, 16, 512], BF16)
    t_hn = [[Tok() for _ in range(4)] for _ in range(16)]
    t_a = [Tok() for _ in range(16)]
    ysT = sb("ysT", [128, 16, 512], BF16)
    t_ys = [Tok() for _ in range(16)]
    t_mg = [Tok() for _ in range(8)]
    NTMP = 6
    tmp = [sb(f"tmp{i}", [128, 514] if i < 4 else [128, 2], F32) for i in range(NTMP)]
    t_tmp = [Tok() for _ in range(NTMP)]
    uh = sb("uh", [128, 16, 2], F32)
    t_uh = [Tok() for _ in range(16)]
    xsc2 = sb("xsc2", [128, 4], F32)
    t_xsc2 = [Tok() for _ in range(2)]
    sem_out = [S.new_sem("out0"), S.new_sem("out1")]
    def dbl(name, shape, dt):
        return [sb(f"{name}{i}", shape, dt) for i in range(2)], [Tok() for _ in range(2)]
    A_ = [sb(f"A{i}", [128, 128], F32) for i in range(4)]
    t_A = [Tok() for _ in range(4)]
    E_, t_E = dbl("E", [128, 128], F32)
    G_, t_G = dbl("G", [128, 128], F32)
    ST_, t_ST = dbl("ST", [128, 128], BF16)
    qG_, t_qG = dbl("qG", [128, 2, 128], BF16)
    mall = sb("mall", [128, 4, 128], F32)
    t_mall = [Tok() for _ in range(4)]
    hball = sb("hball", [128, 4, 4, 128], F32)
    t_hball = [Tok() for _ in range(4)]
    v5all = sb("v5all", [128, 4, 128], F32)
    t_v5all = [Tok() for _ in range(4)]
    sq_, t_sq = dbl("sq", [128, 4, 128], BF16)
    class _NS:
        pass
    GS = []
    for gi in range(2):
        g_ = _NS()
        g_.gpre = sb(f"gpre{gi}", [128, 4, 8], F32)
        g_.gth = sb(f"gth{gi}", [128, 4, 8], F32)
        g_.li = sb(f"li{gi}", [128, 4, 4], F32)
        g_.lf = sb(f"lf{gi}", [128, 4, 4], F32)
        g_.gex = sb(f"gex{gi}", [128, 4, 4], F32)
        g_.t_gpre, g_.t_gth, g_.t_li, g_.t_lf, g_.t_gex = Tok(), Tok(), Tok(), Tok(), Tok()
        GS.append(g_)
    CSs = []
    for gi in range(2):
        c_ = _NS()
        c_.gsum = sb(f"gsum{gi}", [128, 4], F32)
        c_.gg = sb(f"gg{gi}", [128, 4], F32)
        c_.dec = sb(f"dec{gi}", [128, 4], F32)
        c_.t_gsum, c_.t_gg, c_.t_dec = Tok(), Tok(), Tok()
        CSs.append(c_)
    gctr = [0, 0]
    sc1 = sb("sc1", [128, 4], F32)
    t_sc1 = Tok()
    identf = sb("identf", [128, 128], F32)
    ident = sb("ident", [128, 128], BF16)
    maskA = sb("maskA", [128, 128], F32)
    triB = sb("triB", [128, 128], F32)
    negm = sb("negm", [128, 128], BF16)
    onesf = sb("onesf", [128, 128], F32)
    onesb = sb("onesb", [128, 128], BF16)
    mhalf = sb("mhalf", [128, 1], F32)
    wif = sb("wif", [128, 8, 8], BF16)
    nw32 = sb("nw32", [128, 8], F32)
    mhw = sb("mhwc", [128, 16], F32)
    cwt = sb("cwt", [128, 16, 3], F32)
    fnw = sb("fnwbc", [128, D], F32)
    bias8 = sb("bias8sb", [128, 8], F32)
    t_const = Tok()
    t_wif = Tok()

    big = [nc.alloc_psum_tensor(f"big{i}", [128, 512], F32).ap() for i in range(4)]
    t_big = [Tok(True) for _ in range(4)]
    psm = nc.alloc_psum_tensor("psm", [128, 512], F32).ap()
    t_psm = [Tok(True)] * 4
    psh = nc.alloc_psum_tensor("psh", [128, 4, 128], F32).ap()
    t_psh = Tok(True)
    psx = nc.alloc_psum_tensor("psx", [128, 512], F32).ap()
    t_pss = t_pnu = t_pg = t_psf = t_ptot = t_phalo = Tok(True)
    psT = nc.alloc_psum_tensor("psT", [128, 8, 128], BF16).ap()
    t_psT = Tok(True)
    big += [psm, psh.rearrange("p a b -> p (a b)")]
    t_big += [t_psm[0], t_psh]
    bigctr = [0]
    nbig = [6]

    def nextbig():
        i = bigctr[0] % nbig[0]
        bigctr[0] += 1
        return i

    s_in = nc.dram_tensor("s_in", [D, NIN], BF16).ap()
    s_a = nc.dram_tensor("s_a", [2048, D], BF16).ap()
    s_b = nc.dram_tensor("s_b", [2048, D], BF16).ap()
    s_o = nc.dram_tensor("s_o", [D, D], BF16).ap()
    t_scr = {}

    def convert(name, dst, src, cols=None):
        t = Tok()
        sc = S.new_sem("cv_" + name)
        t_scr.setdefault(name, [])
        if cols is None:
            S.dma("pool", lambda e, sem: e.dma_start(out=dst, in_=src).then_inc(sem, 16), sc, writes=[t])
        else:
            c0, c1 = cols
            S.dma("pool", lambda e, sem: e.dma_start(out=dst[:, c0:c1], in_=src[:, c0:c1]).then_inc(sem, 16), sc, writes=[t])
        t_scr[name].append(t)

    w_in_v = w_in.rearrange("(kc p) f -> p kc f", p=128)
    S.dma("pool", lambda e, sem: e.dma_start(out=wif, in_=w_in_v[:, :, OFF_I:OFF_I + 8]).then_inc(sem, 16),
          S.new_sem("wif"), writes=[t_wif])
    for i in range(NSLOT):
        off = OFF_K + 512 * i
        S.dma("pool", lambda e, sem, i=i, off=off: e.dma_start(out=ring[i].rearrange("p (kc f) -> p kc f", kc=8),
                                                                in_=w_in_v[:, :, off:off + 512]).then_inc(sem, 16),
              S.new_sem(f"pre{i}"), writes=[t_ring[i]])
    conv_list = [("q", s_in, w_in, (OFF_Q, OFF_K)), ("cv0", s_in, w_in, (OFF_SH, OFF_SC)), ("cv1", s_in, w_in, (OFF_SC, OFF_GA)),
                 ("oz", s_in, w_in, (OFF_O, OFF_I)), ("g", s_in, w_in, (OFF_GA, NIN)), ("a", s_a, wa, None), ("b", s_b, wb, None),
                 ("o", s_o, wo, None), ("kv", s_in, w_in, (OFF_K, OFF_O))]
    _cl = []
    for (name, dst, src, cols) in conv_list:
        if cols is None or cols[1] - cols[0] <= 1024:
            _cl.append((name, dst, src, cols))
        else:
            for c0 in range(cols[0], cols[1], 1024):
                _cl.append((name, dst, src, (c0, min(c0 + 1024, cols[1]))))
    conv_list = [c_ for c_ in _cl if c_[0] in ("q", "cv0", "cv1")]
    conv_late = {"t0a": [c_ for c_ in _cl if c_[0] == "oz"],
                 "t0b": [c_ for c_ in _cl if c_[0] in ("g", "a", "b")],
                 "t0c": [c_ for c_ in _cl if c_[0] in ("o", "kv")]}
    n_blocks = {}
    for c_ in _cl:
        t_scr.setdefault(c_[0], [])
        n_blocks[c_[0]] = n_blocks.get(c_[0], 0) + 1
    convert(*conv_list.pop(0))
    s_in_v = s_in.rearrange("(kc p) f -> p kc f", p=128)
    s_a_v = s_a.rearrange("(kc p) c -> p kc c", p=128)
    s_b_v = s_b.rearrange("(kc p) c -> p kc c", p=128)
    s_o_v = s_o.rearrange("(kc p) c -> p kc c", p=128)

    def slab_in(off):
        name = ("q" if off < OFF_K else "kv" if off < OFF_O else "oz" if off < OFF_I else
                "cv0" if off < OFF_SC else "cv1" if off < OFF_GA else "g")
        return (name, s_in_v[:, :, off:off + 512], 8, 512)

    def slab_ab(which, cb):
        v = s_a_v if which == "a" else s_b_v
        return (which, v[:, :, cb * 256:(cb + 1) * 256], 16, 256)

    def slab_o(nb):
        return ("o", s_o_v[:, :, nb * 512:(nb + 1) * 512], 8, 512)

    kv_slabs = [slab_in(OFF_K), slab_in(OFF_K + 512)] + [slab_in(OFF_V + 512 * i) for i in range(4)]
    tile_seq = kv_slabs + [slab_in(OFF_Q), slab_in(OFF_Q + 512)]
    for g in range(4):
        tile_seq += [slab_in(OFF_SH + 512 * g), slab_in(OFF_SC + 512 * g), slab_in(OFF_ZS + 512 * g), slab_in(OFF_SB + 512 * g)]
    for g in range(4):
        tile_seq += [slab_in(OFF_O + 512 * g), slab_in(OFF_ZA + 512 * g)]
    for cb in range(4):
        tile_seq += [slab_ab("a", cb), slab_ab("b", cb)]
        if cb % 2 == 0:
            tile_seq += [slab_in(OFF_GA + 512 * (cb // 2)), slab_in(OFF_GB + 512 * (cb // 2))]
    tile_seq += [slab_o(0), slab_o(1)]
    seq = list(tile_seq)
    for _ in range(TM - 1):
        seq += tile_seq
    st = {"loaded": NSLOT, "used": 0, "free": 0}

    def _load(pos):
        name, src, nkc, nf = seq[pos]
        slot = pos % NSLOT
        dst = ring[slot].rearrange("p (kc f) -> p kc f", kc=nkc)
        assert len(t_scr[name]) == n_blocks[name], f"slab load of {name} recorded before its conversion"
        S.dma("sp", lambda e, sem: e.dma_start(out=dst, in_=src).then_inc(sem, 16), sem_ring[slot],
              reads=(t_scr[name] if isinstance(t_scr[name], list) else [t_scr[name]]), writes=[t_ring[slot]])

    def _prefetch():
        while st["loaded"] < min(len(seq), st["free"] + NSLOT):
            _load(st["loaded"])
            st["loaded"] += 1

    def get_slabs(n):
        out = []
        first = st["used"]
        st["used"] += n
        assert st["used"] - st["free"] <= NSLOT
        _prefetch()
        for pos in range(first, first + n):
            name, src, nkc, nf = seq[pos]
            slot = pos % NSLOT
            out.append((ring[slot].rearrange("p (kc f) -> p kc f", kc=nkc), t_ring[slot]))
        return out

    def release(n):
        st["free"] += n
        assert st["free"] <= st["used"]
        _prefetch()

    def cload(dst, src, slow=False):
        S.dma("sp", lambda e, sem: e.dma_start(out=dst, in_=src, allow_slow_non_contiguous=slow).then_inc(sem, 16),
              S.new_sem("c"), writes=[t_const])
    t_cs = [Tok() for _ in range(12)]
    POOL(lambda e: e.memset(identf, 1.0), w=[t_cs[0]])
    POOL(lambda e: e.affine_select(out=identf, in_=identf, pattern=[[-1, 128]], compare_op=ALU.is_equal, fill=0.0, base=0, channel_multiplier=1), w=[t_cs[0]])
    POOL(lambda e: e.tensor_copy(out=ident, in_=identf), r=[t_cs[0]], w=[t_cs[1]])
    POOL(lambda e: e.memset(maskA, 1.0), w=[t_cs[2]])
    POOL(lambda e: e.affine_select(out=maskA, in_=maskA, pattern=[[-1, 128]], compare_op=ALU.is_gt, fill=0.0, base=0, channel_multiplier=1), w=[t_cs[2]])
    POOL(lambda e: e.memset(triB, 1.0), w=[t_cs[3]])
    POOL(lambda e: e.affine_select(out=triB, in_=triB, pattern=[[1, 128]], compare_op=ALU.is_ge, fill=0.0, base=0, channel_multiplier=-1), w=[t_cs[3]])
    POOL(lambda e: e.memset(negm, -30000.0), w=[t_cs[4]])
    POOL(lambda e: e.affine_select(out=negm, in_=negm, pattern=[[-1, 128]], compare_op=ALU.is_gt, fill=0.0, base=0, channel_multiplier=1), w=[t_cs[4]])
    POOL(lambda e: e.memset(onesf, 1.0), w=[t_cs[5]])
    POOL(lambda e: e.memset(onesb, 1.0), w=[t_cs[6]])
    POOL(lambda e: e.memset(mhalf, -0.5), w=[t_cs[7]])
    POOL(lambda e: e.memset(Cst.rearrange("p a b c -> p (a b c)"), 0.0), w=[t_C[h][d] for h in range(4) for d in range(2)])
    POOL(lambda e: e.memset(Cbf.rearrange("p a b c -> p (a b c)"), 0.0), w=[t_Cbf[h][d] for h in range(4) for d in range(2)])
    POOL(lambda e: e.memset(nst, 0.0), w=[t_n])
    POOL(lambda e: e.memset(nbc.rearrange("p a b -> p (a b)"), 0.0), w=[t_nbc])
    t_all_const = [t_cs[i] for i in range(8)]
    t_nw, t_mh, t_cw, t_fn, t_b8 = Tok(), Tok(), Tok(), Tok(), Tok()
    def pl(dst, src, tk, slow=True):
        S.dma("sp", lambda e, sem: e.dma_start(out=dst, in_=src, allow_slow_non_contiguous=slow).then_inc(sem, 16),
              S.new_sem("p"), writes=[tk])
    pl(nw32, norm_w, t_nw, slow=False)
    pl(mhw, mhw_d, t_mh, slow=False)
    pl(cwt.rearrange("p a b -> p (a b)"), cw_d, t_cw, slow=False)
    pl(fnw, fnw_d, t_fn, slow=False)
    pl(bias8, bias_d, t_b8, slow=False)
    DVE(lambda e: e.tensor_scalar(out=nw32, in0=nw32, scalar1=32.0, scalar2=None, op0=ALU.mult), r=[], w=[t_nw])
    DVE(lambda e: e.tensor_scalar(out=fnw, in0=fnw, scalar1=32.0, scalar2=None, op0=ALU.mult), r=[], w=[t_fn])
    DVE(lambda e: e.tensor_scalar(out=mhw, in0=mhw, scalar1=float(0.25 * np.sqrt(512.0)), scalar2=None, op0=ALU.mult), r=[], w=[t_mh])
    DVE(lambda e: e.tensor_scalar(out=cwt.rearrange("p a b -> p (a b)"), in0=cwt.rearrange("p a b -> p (a b)"), scalar1=0.5, scalar2=None, op0=ALU.mult), r=[], w=[t_cw])

    xctr = [0]

    def xload(src_rows, qn="act"):
        i = xctr[0] % 2
        xctr[0] += 1
        S.dma(qn, lambda e, sem: e.dma_start(out=xst[i], in_=src_rows).then_inc(sem, 16), (sem_x if qn == "act" else sem_xs)[i], writes=[t_xst[i]])
        return i

    def xprep_a(src_rows, qn="act", i=None):
        if i is None:
            i = xload(src_rows, qn)
        k = xk[0] % 2
        xk[0] += 1
        POOL(lambda e: e.memset(xsc[:, 4 * k:4 * k + 1], 0.0), w=[t_xsc[k]])
        ACT(lambda e: e.activation(out=xs[k], in_=xst[i], func=AF.Square, accum_out=xsc[:, 4 * k:4 * k + 1]), r=[t_xst[i]], w=[t_xs[k], t_xsc[k]])
        DVE(lambda e: e.tensor_scalar(out=xsc[:, 4 * k + 1:4 * k + 2], in0=xsc[:, 4 * k:4 * k + 1], scalar1=D * EPS, scalar2=None, op0=ALU.add),
            r=[], w=[t_xsc[k]])
        POOL(lambda e: e.tensor_tensor(out=xsc[:, 4 * k + 2:4 * k + 3], in0=xsc[:, 4 * k + 1:4 * k + 2], in1=mhalf[:, 0:1], op=ALU.pow),
             r=[t_cs[7]], w=[t_xsc[k]])
        ACT(lambda e: e.activation(out=xs[k], in_=xst[i], func=AF.Copy, scale=xsc[:, 4 * k + 2:4 * k + 3]), r=[t_xst[i], t_xsc[k]], w=[t_xs[k]])
        return k

    def xprep_b(k, dst_cols, tk_dst, also_halo=False, X=None):
        X = XC.x if X is None else X
        def tr(e):
            for kc in range(8):
                ins = e.transpose(out=psT[:, kc, :], in_=xs[k][:, kc * 128:(kc + 1) * 128], identity=ident)
            return ins
        PE(tr, r=[t_xs[k], t_cs[1]], w=[t_psT])
        DVE(lambda e: e.tensor_tensor(out=X[:, :, dst_cols], in0=psT, in1=nw32.unsqueeze(2).to_broadcast([128, 8, 128]), op=ALU.mult),
            r=[t_psT, t_nw], w=[tk_dst])
        if also_halo:
            DVE(lambda e: e.tensor_tensor(out=xh, in0=psT[:, :, 126:128], in1=nw32.unsqueeze(2).to_broadcast([128, 8, 2]), op=ALU.mult),
                r=[t_psT, t_nw], w=[t_xh])

    def xprep4_loads(row0):
        return [xload(xm[row0:row0 + 128, :]), xload(xm[row0 + 128:row0 + 256, :])]

    def xprep4_head(row0, li=None):
        if li is None:
            li = [None, None]
        return {0: xprep_a(xm[row0:row0 + 128, :], i=li[0]), 1: xprep_a(xm[row0 + 128:row0 + 256, :], i=li[1])}

    def xprep4(row0, ks=None, X=None, T=None):
        T = XC.t if T is None else T
        if ks is None:
            ks = xprep4_head(row0)
        for c in range(4):
            xprep_b(ks[c], slice(c * 128, (c + 1) * 128), T[c], X=X)
            if c + 2 < 4:
                ks[c + 2] = xprep_a(xm[row0 + (c + 2) * 128:row0 + (c + 3) * 128, :])

    def tokmajor_proj(slab, tslab, xcols, t_x, dst, t_dst, on_dve=False):
        b = nextbig()
        X = XC.x

        def mm(e):
            for kc in range(8):
                ins = e.matmul(big[b], lhsT=X[:, kc, xcols], rhs=slab[:, kc, :], start=(kc == 0), stop=(kc == 7))
            return ins
        PE(mm, r=[tslab, t_x], w=[t_big[b]])
        if on_dve:
            DVE(lambda e: e.tensor_copy(out=dst, in_=big[b]), r=[t_big[b]], w=list(t_dst))
        else:
            ACT(lambda e: e.activation(out=dst, in_=big[b], func=AF.Copy), r=[t_big[b]], w=list(t_dst))

    def gates(nch, xcol_fn, t_xs_list):
        g_ = GS[gctr[0] % 2]
        gctr[0] += 1
        X = XC.x

        def mm(e):
            for c in range(nch):
                for kc in range(8):
                    ins = e.matmul(psx[:, 160 + 8 * c:168 + 8 * c], lhsT=X[:, kc, xcol_fn(c)], rhs=wif[:, kc, :],
                                   start=(kc == 0), stop=(kc == 7))
            return ins
        PE(mm, r=[t_wif] + t_xs_list, w=[t_pg])
        pg = psx[:, 160:160 + 8 * nch].rearrange("p (c g) -> p c g", g=8)
        DVE(lambda e: e.tensor_tensor(out=g_.gpre[:, 0:nch, :], in0=pg, in1=bias8.unsqueeze(1).to_broadcast([128, nch, 8]), op=ALU.add),
            r=[t_pg, t_b8], w=[g_.t_gpre])
        ACT(lambda e: e.activation(out=g_.gth[:, 0:nch, :], in_=g_.gpre[:, 0:nch, :], func=AF.Tanh, scale=1.0 / 15.0), r=[g_.t_gpre], w=[g_.t_gth])
        DVE(lambda e: e.tensor_scalar(out=g_.li[:, 0:nch, :], in0=g_.gth[:, 0:nch, 0:4], scalar1=15.0, scalar2=None, op0=ALU.mult), r=[g_.t_gth], w=[g_.t_li])
        ACT(lambda e: e.activation(out=g_.gex[:, 0:nch, :], in_=g_.gth[:, 0:nch, 4:8], func=AF.Exp, scale=-15.0), r=[g_.t_gth], w=[g_.t_gex])
        ACT(lambda e: e.activation(out=g_.gex[:, 0:nch, :], in_=g_.gex[:, 0:nch, :], func=AF.Ln, bias=1.0, scale=1.0), r=[], w=[g_.t_gex])
        DVE(lambda e: e.tensor_scalar(out=g_.lf[:, 0:nch, :], in0=g_.gex[:, 0:nch, :], scalar1=-1.0, scalar2=None, op0=ALU.mult), r=[g_.t_gex], w=[g_.t_lf])
        return g_

    def chunk_gate_scalars(g_, c):
        c_ = CSs[gctr[1] % 2]
        gctr[1] += 1

        def mm(e):
            e.matmul(psx[:, 192:196], lhsT=maskA, rhs=g_.lf[:, c, :], start=True, stop=True)
            return e.matmul(psx[:, 200:204], lhsT=onesf, rhs=g_.lf[:, c, :], start=True, stop=True)
        PE(mm, r=[g_.t_lf, t_cs[2], t_cs[5]], w=[t_psf, t_ptot])
        DVE(lambda e: e.tensor_tensor(out=c_.gsum, in0=psx[:, 192:196], in1=g_.li[:, c, :], op=ALU.add), r=[t_psf, g_.t_li], w=[c_.t_gsum])
        ACT(lambda e: e.activation(out=c_.gg, in_=c_.gsum, func=AF.Exp), r=[c_.t_gsum], w=[c_.t_gg])
        ACT(lambda e: e.activation(out=c_.dec, in_=psx[:, 200:204], func=AF.Exp), r=[t_ptot], w=[c_.t_dec])
        return c_

    def state_update(c_, kt, tkt, vt, tvt, cast=True, kcols=None):
        if kcols is None:
            for h in range(4):
                DVE(lambda e, h=h: e.tensor_scalar(out=kt[:, h * 256:(h + 1) * 256], in0=kt[:, h * 256:(h + 1) * 256],
                                                   scalar1=c_.gg[:, h:h + 1], scalar2=None, op0=ALU.mult),
                    r=[c_.t_gg], w=[tkt[h]])
        else:
            def trk(e):
                for hd in range(8):
                    ins = e.transpose(out=psT[:, hd, :], in_=kT[:, hd, kcols], identity=ident)
                return ins
            PE(trk, r=list(t_kT) + [t_cs[1]], w=[t_psT])
            for h in range(4):
                DVE(lambda e, h=h: e.tensor_scalar(out=kt[:, h * 256:(h + 1) * 256].rearrange("p (a b) -> p a b", a=2), in0=psT[:, 2 * h:2 * h + 2, :],
                                                   scalar1=c_.gg[:, h:h + 1], scalar2=None, op0=ALU.mult),
                    r=[t_psT, c_.t_gg], w=[tkt[h]])
        yield
        for h in range(4):
            for dc in range(2):
                b = nextbig()
                PE(lambda e, h=h, dc=dc, b=b: e.matmul(big[b], lhsT=kt[:, h * 256 + dc * 128:h * 256 + (dc + 1) * 128],
                                                        rhs=vt[:, h * 512:(h + 1) * 512], start=True, stop=True),
                   r=[tkt[h], tvt[h]], w=[t_big[b]])
                DVE(lambda e, h=h, dc=dc, b=b: e.scalar_tensor_tensor(out=Cst[:, h, dc, :], in0=Cst[:, h, dc, :], scalar=c_.dec[:, h:h + 1],
                                                                      in1=big[b], op0=ALU.mult, op1=ALU.add),
                    r=[c_.t_dec, t_big[b]], w=[t_C[h][dc]])
                if cast:
                    ACT(lambda e, h=h, dc=dc: e.activation(out=Cbf[:, h, dc, :], in_=Cst[:, h, dc, :], func=AF.Copy), r=[t_C[h][dc]], w=[t_Cbf[h][dc]])
            yield

        def nmm(e):
            for h in range(4):
                for dc in range(2):
                    ins = e.matmul(psx[:, 128 + 2 * h + dc:129 + 2 * h + dc], lhsT=kt[:, h * 256 + dc * 128:h * 256 + (dc + 1) * 128],
                                   rhs=onesb[:, 0:1], start=True, stop=True)
            return ins
        PE(nmm, r=list(tkt) + [t_cs[6]], w=[t_pnu])
        DVE(lambda e: e.tensor_tensor(out=nst.rearrange("p (h d) -> p h d", d=2), in0=nst.rearrange("p (h d) -> p h d", d=2),
                                      in1=c_.dec.unsqueeze(2).to_broadcast([128, 4, 2]), op=ALU.mult), r=[c_.t_dec], w=[t_n])
        DVE(lambda e: e.tensor_tensor(out=nst, in0=nst, in1=psx[:, 128:136], op=ALU.add), r=[t_pnu], w=[t_n])
        if cast:
            DVE(lambda e: e.tensor_tensor(out=nbc, in0=onesb.unsqueeze(1).to_broadcast([128, 8, 128]),
                                          in1=nst.unsqueeze(2).to_broadcast([128, 8, 128]), op=ALU.mult), r=[t_n, t_cs[6]], w=[t_nbc])
        yield

    kvs = get_slabs(6)

    xsc3 = sb("xsc3", [128, 16], F32)
    t_xsc3 = [Tok() for _ in range(4)]

    def xprep_t0_chunk(c):
        stg = hnT[:, 4 * c:4 * c + 4, :].rearrange("p a b -> p (a b)").bitcast(F32)
        xsa = ysT[:, 2 * c:2 * c + 2, :].rearrange("p a b -> p (a b)")
        tk_stg = [t_hn[jj][cc] for jj in range(4 * c, 4 * c + 4) for cc in range(4)]
        tk_xsa = [t_ys[2 * c], t_ys[2 * c + 1]]
        S.dma("sp", lambda e, sem: e.dma_start(out=stg, in_=xm[c * 128:(c + 1) * 128, :]).then_inc(sem, 16), S.new_sem(f"t0x{c}"), writes=tk_stg)
        c0 = 4 * c
        POOL(lambda e: e.memset(xsc3[:, c0:c0 + 1], 0.0), w=[t_xsc3[c]])
        ACT(lambda e: e.activation(out=xsa, in_=stg, func=AF.Square, accum_out=xsc3[:, c0:c0 + 1]), r=tk_stg, w=tk_xsa + [t_xsc3[c]])
        DVE(lambda e: e.tensor_scalar(out=xsc3[:, c0 + 1:c0 + 2], in0=xsc3[:, c0:c0 + 1], scalar1=D * EPS, scalar2=None, op0=ALU.add), r=[], w=[t_xsc3[c]])
        POOL(lambda e: e.tensor_tensor(out=xsc3[:, c0 + 2:c0 + 3], in0=xsc3[:, c0 + 1:c0 + 2], in1=mhalf[:, 0:1], op=ALU.pow), r=[t_cs[7]], w=[t_xsc3[c]])
        ACT(lambda e: e.activation(out=xsa, in_=stg, func=AF.Copy, scale=xsc3[:, c0 + 2:c0 + 3]), r=tk_stg + [t_xsc3[c]], w=tk_xsa)

    def xprep_t0_chunk_b(c):
        xsa = ysT[:, 2 * c:2 * c + 2, :].rearrange("p a b -> p (a b)")
        tk_xsa = [t_ys[2 * c], t_ys[2 * c + 1]]

        def tr(e):
            for kc in range(8):
                ins = e.transpose(out=psT[:, kc, :], in_=xsa[:, kc * 128:(kc + 1) * 128], identity=ident)
            return ins
        PE(tr, r=tk_xsa + [t_cs[1]], w=[t_psT])
        DVE(lambda e: e.tensor_tensor(out=xnTs[0][:, :, c * 128:(c + 1) * 128], in0=psT, in1=nw32.unsqueeze(2).to_broadcast([128, 8, 128]), op=ALU.mult),
            r=[t_psT, t_nw], w=[t_xnTs[0][c]])

    T0_EARLY = PC >= 6
    ONCHIP = False
    sem_oc_in = [S.new_sem("ocin0"), S.new_sem("ocin1")]
    sem_oc_st = [S.new_sem("ocst0"), S.new_sem("ocst1")]
    t_oc = {"kv": [Tok(), Tok()], "o": [Tok(), Tok()]}
    if ONCHIP:
        t_scr["kv"] = t_oc["kv"]
        t_scr["o"] = t_oc["o"]

    def onchip_convert(it):
        h_ = it % 2
        stg = hnT[:, 8 * h_:8 * h_ + 8, :].rearrange("p a b -> p (a b)").bitcast(F32)
        obf = ysT[:, 4 * h_:4 * h_ + 4, :].rearrange("p a b -> p (a b)")
        tk_stg = [t_hn[jj][cc] for jj in range(8 * h_, 8 * h_ + 8) for cc in range(4)]
        tk_obf = [t_ys[4 * h_ + q_] for q_ in range(4)]
        if it < 16:
            kb, hf = it // 2, it % 2
            c0 = OFF_K + 1536 * hf
            src = w_in[kb * 128:(kb + 1) * 128, c0:c0 + 1536]
            dst = s_in[kb * 128:(kb + 1) * 128, c0:c0 + 1536]
            ncol, name = 1536, "kv"
        else:
            kb = 2 * (it - 16)
            src = wo[kb * 128:(kb + 2) * 128, :].rearrange("(a p) c -> p a c", p=128)
            dst = s_o[kb * 128:(kb + 2) * 128, :].rearrange("(a p) c -> p a c", p=128)
            ncol, name = 2048, "o"
        sv = stg[:, 0:ncol] if it < 16 else stg[:, 0:ncol].rearrange("p (a c) -> p a c", a=2)
        ov = obf[:, 0:ncol] if it < 16 else obf[:, 0:ncol].rearrange("p (a c) -> p a c", a=2)
        S.dma("sp", lambda e, sem: e.dma_start(out=sv, in_=src).then_inc(sem, 16), sem_oc_in[h_], writes=tk_stg)
        ACT(lambda e: e.activation(out=obf[:, 0:ncol], in_=stg[:, 0:ncol], func=AF.Copy), r=tk_stg, w=tk_obf)
        S.dma("sp", lambda e, sem: e.dma_start(out=dst, in_=ov).then_inc(sem, 16), sem_oc_st[h_], reads=tk_obf, writes=[t_oc[name][h_]])

    oc_it = [0]
    gates_of = {}
    kx_of = {0: xprep_a(xp[0:128, :], "sp")}
    if PC > 1:
        kx_of[1] = xprep_a(xp[128:256, :], "sp")
    xprep_b(kx_of[0], slice(0, 128), XC.t[0], also_halo=(PC == 1))

    def prefix_A(pc):
        ci = pc % 4
        xsl = slice(ci * 128, (ci + 1) * 128)
        if pc + 2 < PC:
            kx_of[pc + 2] = xprep_a(xp[(pc + 2) * 128:(pc + 3) * 128, :], "sp")
        yield
        for nb in range(2):
            tokmajor_proj(kvs[nb][0], kvs[nb][1], xsl, XC.t[ci], ktok[pc % 2][:, nb * 512:(nb + 1) * 512], t_ktok[pc % 2][2 * nb:2 * nb + 2], on_dve=True)
            yield
        if pc + 1 < PC:
            cn = (pc + 1) % 4
            xprep_b(kx_of[pc + 1], slice(cn * 128, (cn + 1) * 128), XC.t[cn], also_halo=(pc + 1 == PC - 1))
        for nb in range(4):
            tokmajor_proj(kvs[2 + nb][0], kvs[2 + nb][1], xsl, XC.t[ci], vtok[ci][:, nb * 512:(nb + 1) * 512], [t_vtok[ci][nb]], on_dve=(nb == 3))
            yield
        gates_of[pc] = gates(1, lambda c, xsl=xsl: xsl, [XC.t[ci]])
        yield

    def prefix_B(pc):
        ci = pc % 4
        c_ = chunk_gate_scalars(gates_of[pc], 0)
        yield
        yield from state_update(c_, ktok[pc % 2], t_ktok[pc % 2], vtok[ci], t_vtok[ci], cast=(pc == PC - 1))

    for _ in prefix_A(0):
        pass
    for pc in range(PC):
        if T0_EARLY and 0 <= pc - (PC - 6) < 4:
            xprep_t0_chunk(pc - (PC - 6))
        if T0_EARLY and 0 <= pc - (PC - 5) < 4:
            xprep_t0_chunk_b(pc - (PC - 5))
        for _ in range(2):
            if conv_list:
                convert(*conv_list.pop(0))
        if ONCHIP and pc < PC - 5:
            for _ in range(2):
                if oc_it[0] < 20:
                    onchip_convert(oc_it[0])
                    oc_it[0] += 1
        ga = prefix_A(pc + 1) if pc + 1 < PC else iter(())
        gb = prefix_B(pc)
        for _ in range(3):
            next(ga, None)
        alive = True
        while alive:
            alive = False
            for g in (ga, gb):
                try:
                    next(g)
                    alive = True
                except StopIteration:
                    pass
    nbig[0] = 4
    while conv_list:
        convert(*conv_list.pop(0))

    def norm_batch(c):
        cols = slice(c * 128, (c + 1) * 128)
        DVE(lambda e: e.tensor_scalar(out=mall, in0=mall, scalar1=1.0, scalar2=None, op0=ALU.max), r=[], w=list(t_mall))
        DVE(lambda e: e.scalar_tensor_tensor(out=mall, in0=mall, scalar=512.0 * EPS, in1=mall, op0=ALU.mult, op1=ALU.mult), r=[], w=list(t_mall))
        DVE(lambda e: e.tensor_tensor(out=v5all, in0=v5all, in1=mall, op=ALU.add), r=list(t_mall), w=list(t_v5all))
        ACT(lambda e: e.activation(out=v5all, in_=v5all, func=AF.Sqrt), r=[], w=list(t_v5all))
        DVE(lambda e: e.reciprocal(out=v5all, in_=v5all), r=[], w=list(t_v5all))
        DVE(lambda e: e.tensor_tensor(out=hnT[:, :, cols].rearrange("p (h j) t -> p h j t", j=4), in0=hball,
                                      in1=v5all.unsqueeze(2).to_broadcast([128, 4, 4, 128]), op=ALU.mult),
            r=list(t_hball) + list(t_v5all), w=[t_hn[jj][c] for jj in range(16)])

    def mlstm_chunk(g_, c, deferred=None):
        cols = slice(c * 128, (c + 1) * 128)
        c_ = chunk_gate_scalars(g_, c)
        for h in range(4):
            DVE(lambda e, h=h: e.tensor_scalar(out=A_[h], in0=triB, scalar1=g_.lf[:, c, h:h + 1], scalar2=None, op0=ALU.mult),
                r=[g_.t_lf, t_cs[3]], w=[t_A[h]])
        yield
        for h in range(4):
            i = h % 2

            def mm1(e, h=h, i=i):
                e.matmul(psm[:, 0:128], lhsT=maskA, rhs=A_[h], start=True, stop=False)
                e.matmul(psm[:, 0:128], lhsT=ident, rhs=negm, start=False, stop=True)
                e.matmul(psm[:, 128:256], lhsT=onesf, rhs=A_[h], start=True, stop=True)
                e.matmul(psm[:, 256:384], lhsT=kT[:, 2 * h, cols], rhs=qT[:, 2 * h, cols], start=True, stop=False)
                return e.matmul(psm[:, 256:384], lhsT=kT[:, 2 * h + 1, cols], rhs=qT[:, 2 * h + 1, cols], start=False, stop=True)
            PE(mm1, r=[t_A[h], t_cs[2], t_cs[5], t_cs[1], t_cs[4], t_kT[2 * h], t_kT[2 * h + 1], t_qT[2 * h], t_qT[2 * h + 1]],
               w=[t_psm[0]])
            ACT(lambda e, h=h, i=i: e.activation(out=E_[i], in_=psm[:, 0:128], func=AF.Exp, bias=g_.li[:, c, h:h + 1], scale=1.0),
                r=[t_psm[0], g_.t_li], w=[t_E[i]])
            ACT(lambda e, i=i: e.activation(out=G_[i], in_=psm[:, 128:256], func=AF.Exp), r=[t_psm[1]], w=[t_G[i]])
            DVE(lambda e, i=i: e.tensor_tensor(out=ST_[i], in0=psm[:, 256:384], in1=E_[i], op=ALU.mult), r=[t_psm[2], t_E[i]], w=[t_ST[i]])
            DVE(lambda e, h=h, i=i: e.tensor_tensor(out=qG_[i], in0=qT[:, 2 * h:2 * h + 2, cols],
                                                    in1=G_[i].unsqueeze(1).to_broadcast([128, 2, 128]), op=ALU.mult),
                r=[t_qT[2 * h], t_qT[2 * h + 1], t_G[i]], w=[t_qG[i]])
            if h == 0 and deferred is not None:
                deferred()
            yield

            def mm2(e, h=h, i=i):
                for j in range(4):
                    e.matmul(psh[:, j, :], lhsT=vtok[c][:, h * 512 + j * 128:h * 512 + (j + 1) * 128], rhs=ST_[i], start=True, stop=False)
                    e.matmul(psh[:, j, :], lhsT=Cbf[:, h, 0, j * 128:(j + 1) * 128], rhs=qG_[i][:, 0, :], start=False, stop=False)
                    e.matmul(psh[:, j, :], lhsT=Cbf[:, h, 1, j * 128:(j + 1) * 128], rhs=qG_[i][:, 1, :], start=False, stop=True)
                e.matmul(psm[:, 384:512], lhsT=onesb, rhs=ST_[i], start=True, stop=False)
                e.matmul(psm[:, 384:512], lhsT=nbc[:, 2 * h, :], rhs=qG_[i][:, 0, :], start=False, stop=False)
                return e.matmul(psm[:, 384:512], lhsT=nbc[:, 2 * h + 1, :], rhs=qG_[i][:, 1, :], start=False, stop=True)
            PE(mm2, r=[t_vtok[c][h], t_ST[i], t_qG[i], t_Cbf[h][0], t_Cbf[h][1], t_nbc, t_cs[6]], w=[t_psh, t_psm[3]])
            ACT(lambda e, i=i: e.activation(out=sq_[i], in_=psh, func=AF.Square), r=[t_psh], w=[t_sq[i]])
            DVE(lambda e, h=h: e.tensor_copy(out=hball[:, h], in_=psh), r=[t_psh], w=[t_hball[h]])
            ACT(lambda e, h=h: e.activation(out=mall[:, h, :], in_=psm[:, 384:512], func=AF.Abs), r=[t_psm[3]], w=[t_mall[h]])
            yield

            def mm3(e, i=i):
                for j in range(4):
                    ins = e.matmul(psx[:, 0:128], lhsT=onesb, rhs=sq_[i][:, j, :], start=(j == 0), stop=(j == 3))
                return ins
            PE(mm3, r=[t_sq[i], t_cs[6]], w=[t_pss])
            DVE(lambda e, h=h: e.tensor_copy(out=v5all[:, h, :], in_=psx[:, 0:128]), r=[t_pss], w=[t_v5all[h]])
            yield
        yield from state_update(c_, ktok[c % 2], t_ktok[c % 2], vtok[c], t_vtok[c], kcols=cols)

    for tile in range(TM):
        r0 = tile * 512
        XC.x = xnTs[tile % 2]
        XC.t = t_xnTs[tile % 2]
        Xn, Tn = xnTs[(tile + 1) % 2], t_xnTs[(tile + 1) % 2]
        if tile == 0 and not T0_EARLY:
            xprep4(r0)
        slkv = kvs if tile == 0 else get_slabs(6)
        for sidx in range(2):
            slab, tslab = slkv[sidx]
            for l in range(4):
                fc = sidx * 4 + l
                b = nextbig()

                def mm(e, slab=slab, l=l, b=b, X=XC.x):
                    for kc in range(8):
                        ins = e.matmul(big[b], lhsT=slab[:, kc, l * 128:(l + 1) * 128], rhs=X[:, kc, :], start=(kc == 0), stop=(kc == 7))
                    return ins
                PE(mm, r=[tslab] + XC.t, w=[t_big[b]])
                ACT(lambda e, fc=fc, b=b: e.activation(out=kT[:, fc, :], in_=big[b], func=AF.Copy), r=[t_big[b]], w=[t_kT[fc]])
        for c in range(4):
            cs = slice(c * 128, (c + 1) * 128)
            for nb in range(4):
                tokmajor_proj(slkv[2 + nb][0], slkv[2 + nb][1], cs, XC.t[c], vtok[c][:, nb * 512:(nb + 1) * 512], [t_vtok[c][nb]], on_dve=(nb % 2 == 1))
        release(6)
        g_tile = gates(4, lambda c: slice(c * 128, (c + 1) * 128), XC.t)
        slq = get_slabs(2)
        for sidx in range(2):
            slab, tslab = slq[sidx]
            for l in range(4):
                fc = sidx * 4 + l
                b = nextbig()

                def mm(e, slab=slab, l=l, b=b, X=XC.x):
                    for kc in range(8):
                        ins = e.matmul(big[b], lhsT=slab[:, kc, l * 128:(l + 1) * 128], rhs=X[:, kc, :], start=(kc == 0), stop=(kc == 7))
                    return ins
                PE(mm, r=[tslab] + XC.t, w=[t_big[b]])
                ACT(lambda e, fc=fc, b=b: e.activation(out=qT[:, fc, :], in_=big[b], func=AF.Copy, scale=1.0 / 16.0),
                    r=[t_big[b]], w=[t_qT[fc], t_mg[fc]])
        release(2)

        def conv_units():
            for g in range(4):
                sl4 = get_slabs(4)
                for l in range(4):
                    cc = 4 * g + l
                    bs = []
                    for which in range(4):
                        slab, tslab = sl4[which]
                        b = nextbig()
                        bs.append(b)

                        def mm(e, slab=slab, l=l, b=b, X=XC.x):
                            for kc in range(8):
                                ins = e.matmul(big[b], lhsT=slab[:, kc, l * 128:(l + 1) * 128], rhs=X[:, kc, :], start=(kc == 0), stop=(kc == 7))
                            return ins
                        PE(mm, r=[tslab] + XC.t, w=[t_big[b]])
                        if tile == 0 and which < 2:
                            def mmh(e, slab=slab, l=l, which=which):
                                for kc in range(8):
                                    ins = e.matmul(psx[:, 208 + 2 * which:210 + 2 * which], lhsT=slab[:, kc, l * 128:(l + 1) * 128], rhs=xh[:, kc, :],
                                                   start=(kc == 0), stop=(kc == 7))
                                return ins
                            PE(mmh, r=[tslab, t_xh], w=[t_phalo])
                        if which == 0:
                            ACT(lambda e, b=b: e.activation(out=tmp[0][:, 0:512], in_=big[b], func=AF.Copy), r=[t_big[b]], w=[t_tmp[0]])
                        elif which == 1:
                            if tile == 0:
                                ACT(lambda e: e.activation(out=tmp[5][:, 0:2], in_=psx[:, 208:210], func=AF.Copy), r=[t_phalo], w=[t_tmp[5]])
                                DVE(lambda e, cc=cc: e.tensor_tensor(out=uh[:, cc, :], in0=psx[:, 210:212], in1=tmp[5][:, 0:2], op=ALU.mult),
                                    r=[t_phalo, t_tmp[5]], w=[t_uh[cc]])
                            DVE(lambda e, b=b: e.tensor_tensor(out=tmp[1][:, 2:514], in0=big[b], in1=tmp[0][:, 0:512], op=ALU.mult),
                                r=[t_big[b], t_tmp[0]], w=[t_tmp[1]])
                            DVE(lambda e, cc=cc: e.tensor_copy(out=tmp[1][:, 0:2], in_=uh[:, cc, :]), r=[t_uh[cc]], w=[t_tmp[1]])
                            POOL(lambda e, cc=cc: e.tensor_tensor(out=tmp[2][:, 0:512], in0=tmp[1][:, 0:512], in1=cwt[:, cc, 0:1].to_broadcast([128, 512]), op=ALU.mult),
                                 r=[t_tmp[1], t_cw], w=[t_tmp[2]])
                            POOL(lambda e, cc=cc: e.tensor_tensor(out=tmp[0][:, 0:512], in0=tmp[1][:, 1:513], in1=cwt[:, cc, 1:2].to_broadcast([128, 512]), op=ALU.mult),
                                 r=[t_tmp[1], t_cw], w=[t_tmp[0]])
                            POOL(lambda e: e.tensor_tensor(out=tmp[2][:, 0:512], in0=tmp[2][:, 0:512], in1=tmp[0][:, 0:512], op=ALU.add),
                                 r=[t_tmp[0]], w=[t_tmp[2]])
                            POOL(lambda e, cc=cc: e.tensor_tensor(out=tmp[0][:, 0:512], in0=tmp[1][:, 2:514], in1=cwt[:, cc, 2:3].to_broadcast([128, 512]), op=ALU.mult),
                                 r=[t_tmp[1], t_cw], w=[t_tmp[0]])
                            POOL(lambda e: e.tensor_tensor(out=tmp[2][:, 0:512], in0=tmp[2][:, 0:512], in1=tmp[0][:, 0:512], op=ALU.add),
                                 r=[t_tmp[0]], w=[t_tmp[2]])
                            DVE(lambda e, cc=cc: e.tensor_copy(out=uh[:, cc, :], in_=tmp[1][:, 512:514]), r=[t_tmp[1]], w=[t_uh[cc]])
                        elif which == 2:
                            ACT(lambda e, b=b: e.activation(out=tmp[3][:, 0:512], in_=big[b], func=AF.Tanh, scale=0.5), r=[t_big[b]], w=[t_tmp[3]])
                            DVE(lambda e, b=b: e.scalar_tensor_tensor(out=tmp[3][:, 0:512], in0=tmp[3][:, 0:512], scalar=1.0, in1=big[b],
                                                                      op0=ALU.add, op1=ALU.mult), r=[t_big[b]], w=[t_tmp[3]])
                        else:
                            DVE(lambda e, b=b: e.tensor_tensor(out=tmp[2][:, 0:512], in0=big[b], in1=tmp[2][:, 0:512], op=ALU.mult),
                                r=[t_big[b]], w=[t_tmp[2]])
                            POOL(lambda e, cc=cc: e.tensor_tensor(out=ysT[:, cc, :], in0=tmp[3][:, 0:512], in1=tmp[2][:, 0:512], op=ALU.mult),
                                 r=[t_tmp[2], t_tmp[3]], w=[t_ys[cc]])
                        yield
                release(4)

        cu = conv_units()
        li_next = xprep4_loads(r0 + 512) if tile + 1 < TM else None
        if tile == 0:
            for c_ in conv_late["t0a"]:
                convert(*c_)
        for c in range(4):
            if tile == 0 and c == 2:
                for c_ in conv_late["t0b"]:
                    convert(*c_)
            dfr = (lambda cc=c - 1: norm_batch(cc)) if c > 0 else None
            for yi, _ in enumerate(mlstm_chunk(g_tile, c, dfr)):
                if yi % 8 != 7:
                    next(cu, None)
        norm_batch(3)
        if tile == 0:
            for c_ in conv_late["t0c"]:
                convert(*c_)
        ks_next = None
        if tile + 1 < TM:
            ks_next = {0: xprep_a(None, i=li_next[0])}
        for _ in cu:
            pass

        for g in range(4):
            (so, tso), (sz, tsz) = get_slabs(2)
            for l in range(4):
                j = 4 * g + l
                bo, bz = nextbig(), nextbig()

                def mmo(e, so=so, l=l, bo=bo, X=XC.x):
                    for kc in range(8):
                        ins = e.matmul(big[bo], lhsT=so[:, kc, l * 128:(l + 1) * 128], rhs=X[:, kc, :], start=(kc == 0), stop=(kc == 7))
                    return ins

                def mmz(e, sz=sz, l=l, bz=bz, X=XC.x):
                    for kc in range(8):
                        ins = e.matmul(big[bz], lhsT=sz[:, kc, l * 128:(l + 1) * 128], rhs=X[:, kc, :], start=(kc == 0), stop=(kc == 7))
                    return ins
                PE(mmo, r=[tso] + XC.t, w=[t_big[bo]])
                PE(mmz, r=[tsz] + XC.t, w=[t_big[bz]])
                ACT(lambda e, bo=bo: e.activation(out=tmp[0][:, 0:512], in_=big[bo], func=AF.Tanh, scale=0.5), r=[t_big[bo]], w=[t_tmp[0]])
                ACT(lambda e, bz=bz: e.activation(out=tmp[1][:, 0:512], in_=big[bz], func=AF.Tanh, scale=0.5), r=[t_big[bz]], w=[t_tmp[1]])
                DVE(lambda e, bz=bz: e.scalar_tensor_tensor(out=tmp[1][:, 0:512], in0=tmp[1][:, 0:512], scalar=1.0, in1=big[bz], op0=ALU.add, op1=ALU.mult),
                    r=[t_big[bz]], w=[t_tmp[1]])
                DVE(lambda e: e.scalar_tensor_tensor(out=tmp[1][:, 0:512], in0=tmp[0][:, 0:512], scalar=1.0, in1=tmp[1][:, 0:512], op0=ALU.add, op1=ALU.mult),
                    r=[t_tmp[0]], w=[t_tmp[1]])
                DVE(lambda e, j=j: e.scalar_tensor_tensor(out=hnT[:, j, :], in0=hnT[:, j, :], scalar=mhw[:, j:j + 1], in1=tmp[1][:, 0:512],
                                                          op0=ALU.mult, op1=ALU.mult),
                    r=[t_tmp[1], t_mh] + t_hn[j], w=[t_a[j]])
            release(2)
            if ks_next is not None:
                if g + 1 < 4:
                    ks_next[g + 1] = xprep_a(None, i=li_next[g + 1])
                xprep_b(ks_next[g], slice(g * 128, (g + 1) * 128), Tn[g], X=Xn)
                if g + 2 < 4:
                    li_next.append(xload(xm[r0 + 512 + (g + 2) * 128:r0 + 512 + (g + 3) * 128, :]))

        gsl = None
        for cb in range(4):
            (sa, tsa), (sbb, tsb) = get_slabs(2)
            if cb % 2 == 0:
                gsl = get_slabs(2)
            for c2 in range(2):
                cch = 2 * cb + c2
                lg = cch % 4
                bpa, bpb, bga, bgb = nextbig(), nextbig(), nextbig(), nextbig()

                def mmp(e, slab, src, b, c2=c2):
                    for kc in range(16):
                        ins = e.matmul(big[b], lhsT=slab[:, kc, c2 * 128:(c2 + 1) * 128], rhs=src[:, kc, :], start=(kc == 0), stop=(kc == 15))
                    return ins

                def mmg(e, slab, b, lg=lg, X=XC.x):
                    for kc in range(8):
                        ins = e.matmul(big[b], lhsT=slab[:, kc, lg * 128:(lg + 1) * 128], rhs=X[:, kc, :], start=(kc == 0), stop=(kc == 7))
                    return ins
                PE(lambda e, sa=sa, bpa=bpa, mmp=mmp: mmp(e, sa, hnT, bpa), r=[tsa] + t_a, w=[t_big[bpa]])
                PE(lambda e, gs=gsl[0][0], bga=bga, mmg=mmg: mmg(e, gs, bga), r=[gsl[0][1]] + XC.t, w=[t_big[bga]])
                PE(lambda e, sbb=sbb, bpb=bpb, mmp=mmp: mmp(e, sbb, ysT, bpb), r=[tsb] + t_ys, w=[t_big[bpb]])
                PE(lambda e, gs=gsl[1][0], bgb=bgb, mmg=mmg: mmg(e, gs, bgb), r=[gsl[1][1]] + XC.t, w=[t_big[bgb]])
                ACT(lambda e, bga=bga: e.activation(out=tmp[2][:, 0:512], in_=big[bga], func=AF.Tanh, scale=0.5), r=[t_big[bga]], w=[t_tmp[2]])
                ACT(lambda e, bgb=bgb: e.activation(out=tmp[3][:, 0:512], in_=big[bgb], func=AF.Tanh, scale=0.5), r=[t_big[bgb]], w=[t_tmp[3]])
                DVE(lambda e, bpa=bpa: e.scalar_tensor_tensor(out=tmp[2][:, 0:512], in0=tmp[2][:, 0:512], scalar=1.0, in1=big[bpa], op0=ALU.add, op1=ALU.mult),
                    r=[t_big[bpa]], w=[t_tmp[2]])
                DVE(lambda e, bpb=bpb: e.scalar_tensor_tensor(out=tmp[3][:, 0:512], in0=tmp[3][:, 0:512], scalar=1.0, in1=big[bpb], op0=ALU.add, op1=ALU.mult),
                    r=[t_big[bpb]], w=[t_tmp[3]])
                POOL(lambda e, cch=cch: e.tensor_tensor(out=qT[:, cch, :], in0=tmp[2][:, 0:512], in1=tmp[3][:, 0:512], op=ALU.add),
                     r=[t_tmp[2], t_tmp[3]], w=[t_mg[cch], t_qT[cch]])
            release(2 if cb % 2 == 0 else 4)

        (so0, tso0), (so1, tso1) = get_slabs(2)
        for c in range(4):
            cs = slice(c * 128, (c + 1) * 128)
            i = xctr[0] % 2
            xctr[0] += 1
            S.dma("act", lambda e, sem, i=i, c=c, r0=r0: e.dma_start(out=xst[i], in_=xm[r0 + c * 128:r0 + (c + 1) * 128, :]).then_inc(sem, 16),
                  sem_x[i], writes=[t_xst[i]])
            bsl = []
            for nb, (slab, tslab) in enumerate(((so0, tso0), (so1, tso1))):
                b = nextbig()
                bsl.append(b)

                def mm(e, slab=slab, b=b, cs=cs):
                    for kc in range(8):
                        ins = e.matmul(big[b], lhsT=qT[:, kc, cs], rhs=slab[:, kc, :], start=(kc == 0), stop=(kc == 7))
                    return ins
                PE(mm, r=[tslab] + t_mg, w=[t_big[b]])
            for nb in range(2):
                DVE(lambda e, nb=nb, b=bsl[nb], i=i: e.scalar_tensor_tensor(out=xst[i][:, nb * 512:(nb + 1) * 512], in0=big[b], scalar=0.5,
                                                                            in1=xst[i][:, nb * 512:(nb + 1) * 512], op0=ALU.mult, op1=ALU.add),
                    r=[t_big[bsl[nb]]], w=[t_xst[i]])
            kj = xk[0] % 2
            cj = 4 + (c % 2) * 2
            POOL(lambda e, cj=cj: e.memset(xsc2[:, cj - 4:cj - 3], 0.0), w=[t_xsc2[c % 2]])
            ACT(lambda e, kj=kj, i=i, cj=cj: e.activation(out=xs[kj], in_=xst[i], func=AF.Square, accum_out=xsc2[:, cj - 4:cj - 3]),
                r=[t_xst[i]], w=[t_xs[kj], t_xsc2[c % 2]])
            DVE(lambda e, cj=cj: e.tensor_scalar(out=xsc2[:, cj - 3:cj - 2], in0=xsc2[:, cj - 4:cj - 3], scalar1=D * EPS, scalar2=None, op0=ALU.add),
                r=[], w=[t_xsc2[c % 2]])
            POOL(lambda e, cj=cj: e.tensor_tensor(out=xsc2[:, cj - 3:cj - 2], in0=xsc2[:, cj - 3:cj - 2], in1=mhalf[:, 0:1], op=ALU.pow),
                 r=[t_cs[7]], w=[t_xsc2[c % 2]])
            DVE(lambda e, i=i, cj=cj: e.scalar_tensor_tensor(out=xst[i], in0=xst[i], scalar=xsc2[:, cj - 3:cj - 2], in1=fnw, op0=ALU.mult, op1=ALU.mult),
                r=[t_xsc2[c % 2], t_fn], w=[t_xst[i]])
            S.dma("sp", lambda e, sem, c=c, r0=r0, i=i: e.dma_start(out=y_out[r0 + c * 128:r0 + (c + 1) * 128, :], in_=xst[i]).then_inc(sem, 16),
                  sem_out[i], reads=[t_xst[i]], writes=[])
        release(2)
    fins = []
    for so in sem_out:
        f_ = Tok()
        f_.w = (so, so.val)
        fins.append(f_)
    S.wait_all("sp", fins)
    S.emit()
    return nc


_NC_CACHE = {}


def kernel(x, meta_tokens, norm_w, w_in, b_igate, b_fgate, mh_norm_w, conv_w, w_proj_a, w_proj_b, w_out, final_norm_w):
    x = np.asarray(x, dtype=np.float32)
    B, SEQ, Dm = x.shape
    half = SEQ // 2
    TM = half // 512
    PC = 1 + half // 128
    key = (TM, PC)
    if key not in _NC_CACHE:
        _NC_CACHE[key] = build(TM, PC)
    nc = _NC_CACHE[key]
    meta = np.asarray(meta_tokens, dtype=np.float32)
    common = {
        "w_in": np.ascontiguousarray(np.asarray(w_in, np.float32)[0]),
        "wa": np.ascontiguousarray(np.asarray(w_proj_a, np.float32)[0]),
        "wb": np.ascontiguousarray(np.asarray(w_proj_b, np.float32)[0]),
        "wo": np.ascontiguousarray(np.asarray(w_out, np.float32)[0]),
        "norm_w": np.ascontiguousarray(np.asarray(norm_w, np.float32)[0].reshape(8, 128).T),
        "bias8": np.ascontiguousarray(np.broadcast_to(np.concatenate([np.asarray(b_igate, np.float32)[0],
                                                                       np.asarray(b_fgate, np.float32)[0]])[None, :], (128, 8))),
        "mhw": np.ascontiguousarray(np.asarray(mh_norm_w, np.float32)[0].reshape(16, 128).T),
        "cw": np.ascontiguousarray(np.asarray(conv_w, np.float32)[0].reshape(3, 16, 128).transpose(2, 1, 0).reshape(128, 48)),
        "fnw": np.ascontiguousarray(np.broadcast_to(np.asarray(final_norm_w, np.float32)[None, :], (128, Dm))),
    }
    in_maps = []
    for b in range(B):
        for s in range(2):
            xmain = np.ascontiguousarray(x[b, s * half:(s + 1) * half])
            xpre = np.zeros((PC * 128, Dm), np.float32)
            if s == 0:
                xpre[PC * 128 - 16:] = meta
            else:
                xpre[112:128] = meta
                xpre[128:] = x[b, 0:half]
            m = dict(common)
            m["xm"] = xmain
            m["xp"] = xpre
            in_maps.append(m)
    res = run_bass_kernel_spmd(nc, in_maps, core_ids=list(range(len(in_maps))))
    out = np.empty((B, SEQ, Dm), np.float32)
    for b in range(B):
        for s in range(2):
            out[b, s * half:(s + 1) * half] = res.results[2 * b + s]["y"]
    return out
```

```python
import numpy as np
import concourse.bass as bass
import concourse.mybir as mybir
from concourse.bass_utils import run_bass_kernel_spmd

F32 = mybir.dt.float32
BF16 = mybir.dt.bfloat16
ALU = mybir.AluOpType
AF = mybir.ActivationFunctionType

D = 1024
NIN = 18440
EPS = 1e-6
OFF_Q, OFF_K, OFF_V, OFF_O, OFF_ZA, OFF_I = 0, 1024, 2048, 4096, 6144, 8192
OFF_SH, OFF_SB, OFF_SC, OFF_ZS, OFF_GA, OFF_GB = 8200, 10248, 12296, 14344, 16392, 17416

SELF_SYNC = {"pool", "dve", "act"}


class Tok:
    __slots__ = ("w", "r", "excl")

    def __init__(self, excl=False):
        self.w = None
        self.r = []
        self.excl = excl


class SemC:
    __slots__ = ("sem", "val", "owner")

    def __init__(self, sem, owner=None):
        self.sem = sem
        self.val = 0
        self.owner = owner


class EngQ:
    def __init__(self, name, semc):
        self.name = name
        self.semc = semc
        semc.owner = self
        self.seen = {}
        self.prog = []


class Sched:
    def __init__(self, nc):
        self.nc = nc
        self.q = {}
        for n in ("pe", "act", "dve", "pool", "sp"):
            self.q[n] = EngQ(n, SemC(nc.alloc_semaphore(name="q_" + n)))

    def new_sem(self, name):
        self.nsem = getattr(self, "nsem", 0) + 1
        return SemC(self.nc.alloc_semaphore(name=f"{name}_{self.nsem}"))

    def _waits(self, q, reads, writes):
        need = {}

        def add(d):
            if d is None:
                return
            s, v = d
            if s.owner is q and q.name not in SELF_SYNC:
                return
            if q.seen.get(s, 0) >= v:
                return
            if need.get(s, 0) < v:
                need[s] = v

        for t in reads:
            add(t.w)
        for t in writes:
            add(t.w)
            for d in t.r:
                add(d)
        for s, v in need.items():
            q.seen[s] = v
        return list(need.items())

    def op(self, qn, fn, reads=(), writes=()):
        q = self.q[qn]
        writes = list(dict.fromkeys(list(writes) + [t for t in reads if t.excl]))
        reads = [t for t in reads if not t.excl]
        waits = self._waits(q, reads, writes)
        q.semc.val += 1
        me = (q.semc, q.semc.val)
        for t in writes:
            t.w = me
            t.r = []
        for t in reads:
            t.r.append(me)
        sem = q.semc.sem

        def thunk(eng):
            for s, v in waits:
                eng.wait_ge(s.sem, v)
            fn(eng).then_inc(sem, 1)
        q.prog.append(thunk)

    def dma(self, qn, fn, semc, reads=(), writes=(), n=1):
        q = self.q[qn]
        waits = self._waits(q, reads, writes)
        semc.val += 16 * n
        me = (semc, semc.val)
        for t in writes:
            t.w = me
            t.r = []
        for t in reads:
            t.r.append(me)
        sem = semc.sem

        def thunk(eng):
            for s, v in waits:
                eng.wait_ge(s.sem, v)
            fn(eng, sem)
        q.prog.append(thunk)

    def wait_all(self, qn, toks):
        q = self.q[qn]
        waits = self._waits(q, toks, ())

        def thunk(eng):
            for s, v in waits:
                eng.wait_ge(s.sem, v)
        q.prog.append(thunk)

    def emit(self):
        nc = self.nc
        progs = self.q
        with nc.Block() as block:
            @block.tensor
            def _(e):
                for t in progs["pe"].prog:
                    t(e)

            @block.scalar
            def _(e):
                for t in progs["act"].prog:
                    t(e)

            @block.vector
            def _(e):
                for t in progs["dve"].prog:
                    t(e)

            @block.gpsimd
            def _(e):
                for t in progs["pool"].prog:
                    t(e)

            @block.sync
            def _(e):
                for t in progs["sp"].prog:
                    t(e)


def build(TM, PC):
    nc = bass.Bass("TRN2", target_bir_lowering=False)
    S = Sched(nc)
    NT = TM * 512
    NP = PC * 128

    def din(name, shape):
        return nc.dram_tensor(name, shape, F32, kind="ExternalInput").ap()

    xm = din("xm", [NT, D])
    xp = din("xp", [NP, D])
    w_in = din("w_in", [D, NIN])
    wa = din("wa", [2048, D])
    wb = din("wb", [2048, D])
    wo = din("wo", [D, D])
    norm_w = din("norm_w", [128, 8])
    bias_d = din("bias8", [128, 8])
    mhw_d = din("mhw", [128, 16])
    cw_d = din("cw", [128, 48])
    fnw_d = din("fnw", [128, D])
    y_out = nc.dram_tensor("y", [NT, D], F32, kind="ExternalOutput").ap()

    def sb(name, shape, dt):
        return nc.alloc_sbuf_tensor(name, shape, dt).ap()

    PE = lambda fn, r=(), w=(): S.op("pe", fn, r, w)
    ACT = lambda fn, r=(), w=(): S.op("act", fn, r, w)
    DVE = lambda fn, r=(), w=(): S.op("dve", fn, r, w)
    POOL = lambda fn, r=(), w=(): S.op("pool", fn, r, w)

    NSLOT = 6
    ring = [sb(f"ring{i}", [128, 4096], BF16) for i in range(NSLOT)]
    t_ring = [Tok() for _ in range(NSLOT)]
    sem_ring = [S.new_sem(f"ring{i}") for i in range(NSLOT)]
    Cst = sb("Cst", [128, 4, 2, 512], F32)
    Cbf = sb("Cbf", [128, 4, 2, 512], BF16)
    t_C = [[Tok() for _ in range(2)] for _ in range(4)]
    t_Cbf = [[Tok() for _ in range(2)] for _ in range(4)]
    nst = sb("nst", [128, 8], F32)
    nbc = sb("nbc", [128, 8, 128], BF16)
    t_n, t_nbc = Tok(), Tok()
    xnTs = [sb(f"xnT{i}", [128, 8, 512], BF16) for i in range(2)]
    t_xnTs = [[Tok() for _ in range(4)] for _ in range(2)]

    class XC:
        x = xnTs[1]
        t = t_xnTs[1]
    xh = sb("xh", [128, 8, 2], BF16)
    t_xh = Tok()
    xst = [sb(f"xst{i}", [128, D], F32) for i in range(2)]
    t_xst = [Tok() for _ in range(2)]
    sem_x = [S.new_sem(f"x{i}") for i in range(2)]
    sem_xs = [S.new_sem(f"xs{i}") for i in range(2)]
    xs = [sb(f"xs{i}", [128, D], BF16) for i in range(2)]
    t_xs = [Tok() for _ in range(2)]
    xsc = sb("xsc", [128, 8], F32)
    t_xsc = [Tok() for _ in range(2)]
    xk = [0]
    qT = sb("qT", [128, 8, 512], BF16)
    kT = sb("kT", [128, 8, 512], BF16)
    t_qT = [Tok() for _ in range(8)]
    t_kT = [Tok() for _ in range(8)]
    ktok = [sb(f"ktok{i}", [128, 1024], BF16) for i in range(2)]
    vtok = [sb(f"vtok{i}", [128, 2048], BF16) for i in range(4)]
    t_ktok = [[Tok() for _ in range(4)] for _ in range(4)]
    t_vtok = [[Tok() for _ in range(4)] for _ in range(4)]
    hnT = sb("hnT", [128, 16, 512], BF16)
    t_hn = [[Tok() for _ in range(4)] for _ in range(16)]
    t_a = [Tok() for _ in range(16)]
    ysT = sb("ysT", [128, 16, 512], BF16)
    t_ys = [Tok() for _ in range(16)]
    t_mg = [Tok() for _ in range(8)]
    NTMP = 6
    tmp = [sb(f"tmp{i}", [128, 514] if i < 4 else [128, 2], F32) for i in range(NTMP)]
    t_tmp = [Tok() for _ in range(NTMP)]
    uh = sb("uh", [128, 16, 2], F32)
    t_uh = [Tok() for _ in range(16)]
    xsc2 = sb("xsc2", [128, 4], F32)
    t_xsc2 = [Tok() for _ in range(2)]
    sem_out = [S.new_sem("out0"), S.new_sem("out1")]
    def dbl(name, shape, dt):
        return [sb(f"{name}{i}", shape, dt) for i in range(2)], [Tok() for _ in range(2)]
    A_ = [sb(f"A{i}", [128, 128], F32) for i in range(4)]
    t_A = [Tok() for _ in range(4)]
    E_, t_E = dbl("E", [128, 128], F32)
    G_, t_G = dbl("G", [128, 128], F32)
    ST_, t_ST = dbl("ST", [128, 128], BF16)
    qG_, t_qG = dbl("qG", [128, 2, 128], BF16)
    mall = sb("mall", [128, 4, 128], F32)
    t_mall = [Tok() for _ in range(4)]
    hball = sb("hball", [128, 4, 4, 128], F32)
    t_hball = [Tok() for _ in range(4)]
    v5all = sb("v5all", [128, 4, 128], F32)
    t_v5all = [Tok() for _ in range(4)]
    sq_, t_sq = dbl("sq", [128, 4, 128], BF16)
    class _NS:
        pass
    GS = []
    for gi in range(2):
        g_ = _NS()
        g_.gpre = sb(f"gpre{gi}", [128, 4, 8], F32)
        g_.gth = sb(f"gth{gi}", [128, 4, 8], F32)
        g_.li = sb(f"li{gi}", [128, 4, 4], F32)
        g_.lf = sb(f"lf{gi}", [128, 4, 4], F32)
        g_.gex = sb(f"gex{gi}", [128, 4, 4], F32)
        g_.t_gpre, g_.t_gth, g_.t_li, g_.t_lf, g_.t_gex = Tok(), Tok(), Tok(), Tok(), Tok()
        GS.append(g_)
    CSs = []
    for gi in range(2):
        c_ = _NS()
        c_.gsum = sb(f"gsum{gi}", [128, 4], F32)
        c_.gg = sb(f"gg{gi}", [128, 4], F32)
        c_.dec = sb(f"dec{gi}", [128, 4], F32)
        c_.t_gsum, c_.t_gg, c_.t_dec = Tok(), Tok(), Tok()
        CSs.append(c_)
    gctr = [0, 0]
    sc1 = sb("sc1", [128, 4], F32)
    t_sc1 = Tok()
    identf = sb("identf", [128, 128], F32)
    ident = sb("ident", [128, 128], BF16)
    maskA = sb("maskA", [128, 128], F32)
    triB = sb("triB", [128, 128], F32)
    negm = sb("negm", [128, 128], BF16)
    onesf = sb("onesf", [128, 128], F32)
    onesb = sb("onesb", [128, 128], BF16)
    mhalf = sb("mhalf", [128, 1], F32)
    wif = sb("wif", [128, 8, 8], BF16)
    nw32 = sb("nw32", [128, 8], F32)
    mhw = sb("mhwc", [128, 16], F32)
    cwt = sb("cwt", [128, 16, 3], F32)
    fnw = sb("fnwbc", [128, D], F32)
    bias8 = sb("bias8sb", [128, 8], F32)
    t_const = Tok()
    t_wif = Tok()

    big = [nc.alloc_psum_tensor(f"big{i}", [128, 512], F32).ap() for i in range(4)]
    t_big = [Tok(True) for _ in range(4)]
    psm = nc.alloc_psum_tensor("psm", [128, 512], F32).ap()
    t_psm = [Tok(True)] * 4
    psh = nc.alloc_psum_tensor("psh", [128, 4, 128], F32).ap()
    t_psh = Tok(True)
    psx = nc.alloc_psum_tensor("psx", [128, 512], F32).ap()
    t_pss = t_pnu = t_pg = t_psf = t_ptot = t_phalo = Tok(True)
    psT = nc.alloc_psum_tensor("psT", [128, 8, 128], BF16).ap()
    t_psT = Tok(True)
    big += [psm, psh.rearrange("p a b -> p (a b)")]
    t_big += [t_psm[0], t_psh]
    bigctr = [0]
    nbig = [6]

    def nextbig():
        i = bigctr[0] % nbig[0]
        bigctr[0] += 1
        return i

    s_in = nc.dram_tensor("s_in", [D, NIN], BF16).ap()
    s_a = nc.dram_tensor("s_a", [2048, D], BF16).ap()
    s_b = nc.dram_tensor("s_b", [2048, D], BF16).ap()
    s_o = nc.dram_tensor("s_o", [D, D], BF16).ap()
    t_scr = {}

    def convert(name, dst, src, cols=None):
        t = Tok()
        sc = S.new_sem("cv_" + name)
        t_scr.setdefault(name, [])
        if cols is None:
            S.dma("pool", lambda e, sem: e.dma_start(out=dst, in_=src).then_inc(sem, 16), sc, writes=[t])
        else:
            c0, c1 = cols
            S.dma("pool", lambda e, sem: e.dma_start(out=dst[:, c0:c1], in_=src[:, c0:c1]).then_inc(sem, 16), sc, writes=[t])
        t_scr[name].append(t)

    w_in_v = w_in.rearrange("(kc p) f -> p kc f", p=128)
    S.dma("pool", lambda e, sem: e.dma_start(out=wif, in_=w_in_v[:, :, OFF_I:OFF_I + 8]).then_inc(sem, 16),
          S.new_sem("wif"), writes=[t_wif])
    for i in range(NSLOT):
        off = OFF_K + 512 * i
        S.dma("pool", lambda e, sem, i=i, off=off: e.dma_start(out=ring[i].rearrange("p (kc f) -> p kc f", kc=8),
                                                                in_=w_in_v[:, :, off:off + 512]).then_inc(sem, 16),
              S.new_sem(f"pre{i}"), writes=[t_ring[i]])
    conv_list = [("q", s_in, w_in, (OFF_Q, OFF_K)), ("cv0", s_in, w_in, (OFF_SH, OFF_SC)), ("cv1", s_in, w_in, (OFF_SC, OFF_GA)),
                 ("oz", s_in, w_in, (OFF_O, OFF_I)), ("g", s_in, w_in, (OFF_GA, NIN)), ("a", s_a, wa, None), ("b", s_b, wb, None),
                 ("o", s_o, wo, None), ("kv", s_in, w_in, (OFF_K, OFF_O))]
    _cl = []
    for (name, dst, src, cols) in conv_list:
        if cols is None or cols[1] - cols[0] <= 1024:
            _cl.append((name, dst, src, cols))
        else:
            for c0 in range(cols[0], cols[1], 1024):
                _cl.append((name, dst, src, (c0, min(c0 + 1024, cols[1]))))
    conv_list = [c_ for c_ in _cl if c_[0] in ("q", "cv0", "cv1")]
    conv_late = {"t0a": [c_ for c_ in _cl if c_[0] == "oz"],
                 "t0b": [c_ for c_ in _cl if c_[0] in ("g", "a", "b")],
                 "t0c": [c_ for c_ in _cl if c_[0] in ("o", "kv")]}
    n_blocks = {}
    for c_ in _cl:
        t_scr.setdefault(c_[0], [])
        n_blocks[c_[0]] = n_blocks.get(c_[0], 0) + 1
    convert(*conv_list.pop(0))
    s_in_v = s_in.rearrange("(kc p) f -> p kc f", p=128)
    s_a_v = s_a.rearrange("(kc p) c -> p kc c", p=128)
    s_b_v = s_b.rearrange("(kc p) c -> p kc c", p=128)
    s_o_v = s_o.rearrange("(kc p) c -> p kc c", p=128)

    def slab_in(off):
        name = ("q" if off < OFF_K else "kv" if off < OFF_O else "oz" if off < OFF_I else
                "cv0" if off < OFF_SC else "cv1" if off < OFF_GA else "g")
        return (name, s_in_v[:, :, off:off + 512], 8, 512)

    def slab_ab(which, cb):
        v = s_a_v if which == "a" else s_b_v
        return (which, v[:, :, cb * 256:(cb + 1) * 256], 16, 256)

    def slab_o(nb):
        return ("o", s_o_v[:, :, nb * 512:(nb + 1) * 512], 8, 512)

    kv_slabs = [slab_in(OFF_K), slab_in(OFF_K + 512)] + [slab_in(OFF_V + 512 * i) for i in range(4)]
    tile_seq = kv_slabs + [slab_in(OFF_Q), slab_in(OFF_Q + 512)]
    for g in range(4):
        tile_seq += [slab_in(OFF_SH + 512 * g), slab_in(OFF_SC + 512 * g), slab_in(OFF_ZS + 512 * g), slab_in(OFF_SB + 512 * g)]
    for g in range(4):
        tile_seq += [slab_in(OFF_O + 512 * g), slab_in(OFF_ZA + 512 * g)]
    for cb in range(4):
        tile_seq += [slab_ab("a", cb), slab_ab("b", cb)]
        if cb % 2 == 0:
            tile_seq += [slab_in(OFF_GA + 512 * (cb // 2)), slab_in(OFF_GB + 512 * (cb // 2))]
    tile_seq += [slab_o(0), slab_o(1)]
    seq = list(tile_seq)
    for _ in range(TM - 1):
        seq += tile_seq
    st = {"loaded": NSLOT, "used": 0, "free": 0}

    def _load(pos):
        name, src, nkc, nf = seq[pos]
        slot = pos % NSLOT
        dst = ring[slot].rearrange("p (kc f) -> p kc f", kc=nkc)
        assert len(t_scr[name]) == n_blocks[name], f"slab load of {name} recorded before its conversion"
        S.dma("sp", lambda e, sem: e.dma_start(out=dst, in_=src).then_inc(sem, 16), sem_ring[slot],
              reads=(t_scr[name] if isinstance(t_scr[name], list) else [t_scr[name]]), writes=[t_ring[slot]])

    def _prefetch():
        while st["loaded"] < min(len(seq), st["free"] + NSLOT):
            _load(st["loaded"])
            st["loaded"] += 1

    def get_slabs(n):
        out = []
        first = st["used"]
        st["used"] += n
        assert st["used"] - st["free"] <= NSLOT
        _prefetch()
        for pos in range(first, first + n):
            name, src, nkc, nf = seq[pos]
            slot = pos % NSLOT
            out.append((ring[slot].rearrange("p (kc f) -> p kc f", kc=nkc), t_ring[slot]))
        return out

    def release(n):
        st["free"] += n
        assert st["free"] <= st["used"]
        _prefetch()

    def cload(dst, src, slow=False):
        S.dma("sp", lambda e, sem: e.dma_start(out=dst, in_=src, allow_slow_non_contiguous=slow).then_inc(sem, 16),
              S.new_sem("c"), writes=[t_const])
    t_cs = [Tok() for _ in range(12)]
    POOL(lambda e: e.memset(identf, 1.0), w=[t_cs[0]])
    POOL(lambda e: e.affine_select(out=identf, in_=identf, pattern=[[-1, 128]], compare_op=ALU.is_equal, fill=0.0, base=0, channel_multiplier=1), w=[t_cs[0]])
    POOL(lambda e: e.tensor_copy(out=ident, in_=identf), r=[t_cs[0]], w=[t_cs[1]])
    POOL(lambda e: e.memset(maskA, 1.0), w=[t_cs[2]])
    POOL(lambda e: e.affine_select(out=maskA, in_=maskA, pattern=[[-1, 128]], compare_op=ALU.is_gt, fill=0.0, base=0, channel_multiplier=1), w=[t_cs[2]])
    POOL(lambda e: e.memset(triB, 1.0), w=[t_cs[3]])
    POOL(lambda e: e.affine_select(out=triB, in_=triB, pattern=[[1, 128]], compare_op=ALU.is_ge, fill=0.0, base=0, channel_multiplier=-1), w=[t_cs[3]])
    POOL(lambda e: e.memset(negm, -30000.0), w=[t_cs[4]])
    POOL(lambda e: e.affine_select(out=negm, in_=negm, pattern=[[-1, 128]], compare_op=ALU.is_gt, fill=0.0, base=0, channel_multiplier=1), w=[t_cs[4]])
    POOL(lambda e: e.memset(onesf, 1.0), w=[t_cs[5]])
    POOL(lambda e: e.memset(onesb, 1.0), w=[t_cs[6]])
    POOL(lambda e: e.memset(mhalf, -0.5), w=[t_cs[7]])
    POOL(lambda e: e.memset(Cst.rearrange("p a b c -> p (a b c)"), 0.0), w=[t_C[h][d] for h in range(4) for d in range(2)])
    POOL(lambda e: e.memset(Cbf.rearrange("p a b c -> p (a b c)"), 0.0), w=[t_Cbf[h][d] for h in range(4) for d in range(2)])
    POOL(lambda e: e.memset(nst, 0.0), w=[t_n])
    POOL(lambda e: e.memset(nbc.rearrange("p a b -> p (a b)"), 0.0), w=[t_nbc])
    t_all_const = [t_cs[i] for i in range(8)]
    t_nw, t_mh, t_cw, t_fn, t_b8 = Tok(), Tok(), Tok(), Tok(), Tok()
    def pl(dst, src, tk, slow=True):
        S.dma("sp", lambda e, sem: e.dma_start(out=dst, in_=src, allow_slow_non_contiguous=slow).then_inc(sem, 16),
              S.new_sem("p"), writes=[tk])
    pl(nw32, norm_w, t_nw, slow=False)
    pl(mhw, mhw_d, t_mh, slow=False)
    pl(cwt.rearrange("p a b -> p (a b)"), cw_d, t_cw, slow=False)
    pl(fnw, fnw_d, t_fn, slow=False)
    pl(bias8, bias_d, t_b8, slow=False)
    DVE(lambda e: e.tensor_scalar(out=nw32, in0=nw32, scalar1=32.0, scalar2=None, op0=ALU.mult), r=[], w=[t_nw])
    DVE(lambda e: e.tensor_scalar(out=fnw, in0=fnw, scalar1=32.0, scalar2=None, op0=ALU.mult), r=[], w=[t_fn])
    DVE(lambda e: e.tensor_scalar(out=mhw, in0=mhw, scalar1=float(0.25 * np.sqrt(512.0)), scalar2=None, op0=ALU.mult), r=[], w=[t_mh])
    DVE(lambda e: e.tensor_scalar(out=cwt.rearrange("p a b -> p (a b)"), in0=cwt.rearrange("p a b -> p (a b)"), scalar1=0.5, scalar2=None, op0=ALU.mult), r=[], w=[t_cw])

    xctr = [0]

    def xload(src_rows, qn="act"):
        i = xctr[0] % 2
        xctr[0] += 1
        S.dma(qn, lambda e, sem: e.dma_start(out=xst[i], in_=src_rows).then_inc(sem, 16), (sem_x if qn == "act" else sem_xs)[i], writes=[t_xst[i]])
        return i

    def xprep_a(src_rows, qn="act", i=None):
        if i is None:
            i = xload(src_rows, qn)
        k = xk[0] % 2
        xk[0] += 1
        POOL(lambda e: e.memset(xsc[:, 4 * k:4 * k + 1], 0.0), w=[t_xsc[k]])
        ACT(lambda e: e.activation(out=xs[k], in_=xst[i], func=AF.Square, accum_out=xsc[:, 4 * k:4 * k + 1]), r=[t_xst[i]], w=[t_xs[k], t_xsc[k]])
        DVE(lambda e: e.tensor_scalar(out=xsc[:, 4 * k + 1:4 * k + 2], in0=xsc[:, 4 * k:4 * k + 1], scalar1=D * EPS, scalar2=None, op0=ALU.add),
            r=[], w=[t_xsc[k]])
        POOL(lambda e: e.tensor_tensor(out=xsc[:, 4 * k + 2:4 * k + 3], in0=xsc[:, 4 * k + 1:4 * k + 2], in1=mhalf[:, 0:1], op=ALU.pow),
             r=[t_cs[7]], w=[t_xsc[k]])
        ACT(lambda e: e.activation(out=xs[k], in_=xst[i], func=AF.Copy, scale=xsc[:, 4 * k + 2:4 * k + 3]), r=[t_xst[i], t_xsc[k]], w=[t_xs[k]])
        return k

    def xprep_b(k, dst_cols, tk_dst, also_halo=False, X=None):
        X = XC.x if X is None else X
        def tr(e):
            for kc in range(8):
                ins = e.transpose(out=psT[:, kc, :], in_=xs[k][:, kc * 128:(kc + 1) * 128], identity=ident)
            return ins
        PE(tr, r=[t_xs[k], t_cs[1]], w=[t_psT])
        DVE(lambda e: e.tensor_tensor(out=X[:, :, dst_cols], in0=psT, in1=nw32.unsqueeze(2).to_broadcast([128, 8, 128]), op=ALU.mult),
            r=[t_psT, t_nw], w=[tk_dst])
        if also_halo:
            DVE(lambda e: e.tensor_tensor(out=xh, in0=psT[:, :, 126:128], in1=nw32.unsqueeze(2).to_broadcast([128, 8, 2]), op=ALU.mult),
                r=[t_psT, t_nw], w=[t_xh])

    def xprep4_loads(row0):
        return [xload(xm[row0:row0 + 128, :]), xload(xm[row0 + 128:row0 + 256, :])]

    def xprep4_head(row0, li=None):
        if li is None:
            li = [None, None]
        return {0: xprep_a(xm[row0:row0 + 128, :], i=li[0]), 1: xprep_a(xm[row0 + 128:row0 + 256, :], i=li[1])}

    def xprep4(row0, ks=None, X=None, T=None):
        T = XC.t if T is None else T
        if ks is None:
            ks = xprep4_head(row0)
        for c in range(4):
            xprep_b(ks[c], slice(c * 128, (c + 1) * 128), T[c], X=X)
            if c + 2 < 4:
                ks[c + 2] = xprep_a(xm[row0 + (c + 2) * 128:row0 + (c + 3) * 128, :])

    def tokmajor_proj(slab, tslab, xcols, t_x, dst, t_dst, on_dve=False):
        b = nextbig()
        X = XC.x

        def mm(e):
            for kc in range(8):
                ins = e.matmul(big[b], lhsT=X[:, kc, xcols], rhs=slab[:, kc, :], start=(kc == 0), stop=(kc == 7))
            return ins
        PE(mm, r=[tslab, t_x], w=[t_big[b]])
        if on_dve:
            DVE(lambda e: e.tensor_copy(out=dst, in_=big[b]), r=[t_big[b]], w=list(t_dst))
        else:
            ACT(lambda e: e.activation(out=dst, in_=big[b], func=AF.Copy), r=[t_big[b]], w=list(t_dst))

    def gates(nch, xcol_fn, t_xs_list):
        g_ = GS[gctr[0] % 2]
        gctr[0] += 1
        X = XC.x

        def mm(e):
            for c in range(nch):
                for kc in range(8):
                    ins = e.matmul(psx[:, 160 + 8 * c:168 + 8 * c], lhsT=X[:, kc, xcol_fn(c)], rhs=wif[:, kc, :],
                                   start=(kc == 0), stop=(kc == 7))
            return ins
        PE(mm, r=[t_wif] + t_xs_list, w=[t_pg])
        pg = psx[:, 160:160 + 8 * nch].rearrange("p (c g) -> p c g", g=8)
        DVE(lambda e: e.tensor_tensor(out=g_.gpre[:, 0:nch, :], in0=pg, in1=bias8.unsqueeze(1).to_broadcast([128, nch, 8]), op=ALU.add),
            r=[t_pg, t_b8], w=[g_.t_gpre])
        ACT(lambda e: e.activation(out=g_.gth[:, 0:nch, :], in_=g_.gpre[:, 0:nch, :], func=AF.Tanh, scale=1.0 / 15.0), r=[g_.t_gpre], w=[g_.t_gth])
        DVE(lambda e: e.tensor_scalar(out=g_.li[:, 0:nch, :], in0=g_.gth[:, 0:nch, 0:4], scalar1=15.0, scalar2=None, op0=ALU.mult), r=[g_.t_gth], w=[g_.t_li])
        ACT(lambda e: e.activation(out=g_.gex[:, 0:nch, :], in_=g_.gth[:, 0:nch, 4:8], func=AF.Exp, scale=-15.0), r=[g_.t_gth], w=[g_.t_gex])
        ACT(lambda e: e.activation(out=g_.gex[:, 0:nch, :], in_=g_.gex[:, 0:nch, :], func=AF.Ln, bias=1.0, scale=1.0), r=[], w=[g_.t_gex])
        DVE(lambda e: e.tensor_scalar(out=g_.lf[:, 0:nch, :], in0=g_.gex[:, 0:nch, :], scalar1=-1.0, scalar2=None, op0=ALU.mult), r=[g_.t_gex], w=[g_.t_lf])
        return g_

    def chunk_gate_scalars(g_, c):
        c_ = CSs[gctr[1] % 2]
        gctr[1] += 1

        def mm(e):
            e.matmul(psx[:, 192:196], lhsT=maskA, rhs=g_.lf[:, c, :], start=True, stop=True)
            return e.matmul(psx[:, 200:204], lhsT=onesf, rhs=g_.lf[:, c, :], start=True, stop=True)
        PE(mm, r=[g_.t_lf, t_cs[2], t_cs[5]], w=[t_psf, t_ptot])
        DVE(lambda e: e.tensor_tensor(out=c_.gsum, in0=psx[:, 192:196], in1=g_.li[:, c, :], op=ALU.add), r=[t_psf, g_.t_li], w=[c_.t_gsum])
        ACT(lambda e: e.activation(out=c_.gg, in_=c_.gsum, func=AF.Exp), r=[c_.t_gsum], w=[c_.t_gg])
        ACT(lambda e: e.activation(out=c_.dec, in_=psx[:, 200:204], func=AF.Exp), r=[t_ptot], w=[c_.t_dec])
        return c_

    def state_update(c_, kt, tkt, vt, tvt, cast=True, kcols=None):
        if kcols is None:
            for h in range(4):
                DVE(lambda e, h=h: e.tensor_scalar(out=kt[:, h * 256:(h + 1) * 256], in0=kt[:, h * 256:(h + 1) * 256],
                                                   scalar1=c_.gg[:, h:h + 1], scalar2=None, op0=ALU.mult),
                    r=[c_.t_gg], w=[tkt[h]])
        else:
            def trk(e):
                for hd in range(8):
                    ins = e.transpose(out=psT[:, hd, :], in_=kT[:, hd, kcols], identity=ident)
                return ins
            PE(trk, r=list(t_kT) + [t_cs[1]], w=[t_psT])
            for h in range(4):
                DVE(lambda e, h=h: e.tensor_scalar(out=kt[:, h * 256:(h + 1) * 256].rearrange("p (a b) -> p a b", a=2), in0=psT[:, 2 * h:2 * h + 2, :],
                                                   scalar1=c_.gg[:, h:h + 1], scalar2=None, op0=ALU.mult),
                    r=[t_psT, c_.t_gg], w=[tkt[h]])
        yield
        for h in range(4):
            for dc in range(2):
                b = nextbig()
                PE(lambda e, h=h, dc=dc, b=b: e.matmul(big[b], lhsT=kt[:, h * 256 + dc * 128:h * 256 + (dc + 1) * 128],
                                                        rhs=vt[:, h * 512:(h + 1) * 512], start=True, stop=True),
                   r=[tkt[h], tvt[h]], w=[t_big[b]])
                DVE(lambda e, h=h, dc=dc, b=b: e.scalar_tensor_tensor(out=Cst[:, h, dc, :], in0=Cst[:, h, dc, :], scalar=c_.dec[:, h:h + 1],
                                                                      in1=big[b], op0=ALU.mult, op1=ALU.add),
                    r=[c_.t_dec, t_big[b]], w=[t_C[h][dc]])
                if cast:
                    ACT(lambda e, h=h, dc=dc: e.activation(out=Cbf[:, h, dc, :], in_=Cst[:, h, dc, :], func=AF.Copy), r=[t_C[h][dc]], w=[t_Cbf[h][dc]])
            yield

        def nmm(e):
            for h in range(4):
                for dc in range(2):
                    ins = e.matmul(psx[:, 128 + 2 * h + dc:129 + 2 * h + dc], lhsT=kt[:, h * 256 + dc * 128:h * 256 + (dc + 1) * 128],
                                   rhs=onesb[:, 0:1], start=True, stop=True)
            return ins
        PE(nmm, r=list(tkt) + [t_cs[6]], w=[t_pnu])
        DVE(lambda e: e.tensor_tensor(out=nst.rearrange("p (h d) -> p h d", d=2), in0=nst.rearrange("p (h d) -> p h d", d=2),
                                      in1=c_.dec.unsqueeze(2).to_broadcast([128, 4, 2]), op=ALU.mult), r=[c_.t_dec], w=[t_n])
        DVE(lambda e: e.tensor_tensor(out=nst, in0=nst, in1=psx[:, 128:136], op=ALU.add), r=[t_pnu], w=[t_n])
        if cast:
            DVE(lambda e: e.tensor_tensor(out=nbc, in0=onesb.unsqueeze(1).to_broadcast([128, 8, 128]),
                                          in1=nst.unsqueeze(2).to_broadcast([128, 8, 128]), op=ALU.mult), r=[t_n, t_cs[6]], w=[t_nbc])
        yield

    kvs = get_slabs(6)

    xsc3 = sb("xsc3", [128, 16], F32)
    t_xsc3 = [Tok() for _ in range(4)]

    def xprep_t0_chunk(c):
        stg = hnT[:, 4 * c:4 * c + 4, :].rearrange("p a b -> p (a b)").bitcast(F32)
        xsa = ysT[:, 2 * c:2 * c + 2, :].rearrange("p a b -> p (a b)")
        tk_stg = [t_hn[jj][cc] for jj in range(4 * c, 4 * c + 4) for cc in range(4)]
        tk_xsa = [t_ys[2 * c], t_ys[2 * c + 1]]
        S.dma("sp", lambda e, sem: e.dma_start(out=stg, in_=xm[c * 128:(c + 1) * 128, :]).then_inc(sem, 16), S.new_sem(f"t0x{c}"), writes=tk_stg)
        c0 = 4 * c
        POOL(lambda e: e.memset(xsc3[:, c0:c0 + 1], 0.0), w=[t_xsc3[c]])
        ACT(lambda e: e.activation(out=xsa, in_=stg, func=AF.Square, accum_out=xsc3[:, c0:c0 + 1]), r=tk_stg, w=tk_xsa + [t_xsc3[c]])
        DVE(lambda e: e.tensor_scalar(out=xsc3[:, c0 + 1:c0 + 2], in0=xsc3[:, c0:c0 + 1], scalar1=D * EPS, scalar2=None, op0=ALU.add), r=[], w=[t_xsc3[c]])
        POOL(lambda e: e.tensor_tensor(out=xsc3[:, c0 + 2:c0 + 3], in0=xsc3[:, c0 + 1:c0 + 2], in1=mhalf[:, 0:1], op=ALU.pow), r=[t_cs[7]], w=[t_xsc3[c]])
        ACT(lambda e: e.activation(out=xsa, in_=stg, func=AF.Copy, scale=xsc3[:, c0 + 2:c0 + 3]), r=tk_stg + [t_xsc3[c]], w=tk_xsa)

    def xprep_t0_chunk_b(c):
        xsa = ysT[:, 2 * c:2 * c + 2, :].rearrange("p a b -> p (a b)")
        tk_xsa = [t_ys[2 * c], t_ys[2 * c + 1]]

        def tr(e):
            for kc in range(8):
                ins = e.transpose(out=psT[:, kc, :], in_=xsa[:, kc * 128:(kc + 1) * 128], identity=ident)
            return ins
        PE(tr, r=tk_xsa + [t_cs[1]], w=[t_psT])
        DVE(lambda e: e.tensor_tensor(out=xnTs[0][:, :, c * 128:(c + 1) * 128], in0=psT, in1=nw32.unsqueeze(2).to_broadcast([128, 8, 128]), op=ALU.mult),
            r=[t_psT, t_nw], w=[t_xnTs[0][c]])

    T0_EARLY = PC >= 6
    ONCHIP = False
    sem_oc_in = [S.new_sem("ocin0"), S.new_sem("ocin1")]
    sem_oc_st = [S.new_sem("ocst0"), S.new_sem("ocst1")]
    t_oc = {"kv": [Tok(), Tok()], "o": [Tok(), Tok()]}
    if ONCHIP:
        t_scr["kv"] = t_oc["kv"]
        t_scr["o"] = t_oc["o"]

    def onchip_convert(it):
        h_ = it % 2
        stg = hnT[:, 8 * h_:8 * h_ + 8, :].rearrange("p a b -> p (a b)").bitcast(F32)
        obf = ysT[:, 4 * h_:4 * h_ + 4, :].rearrange("p a b -> p (a b)")
        tk_stg = [t_hn[jj][cc] for jj in range(8 * h_, 8 * h_ + 8) for cc in range(4)]
        tk_obf = [t_ys[4 * h_ + q_] for q_ in range(4)]
        if it < 16:
            kb, hf = it // 2, it % 2
            c0 = OFF_K + 1536 * hf
            src = w_in[kb * 128:(kb + 1) * 128, c0:c0 + 1536]
            dst = s_in[kb * 128:(kb + 1) * 128, c0:c0 + 1536]
            ncol, name = 1536, "kv"
        else:
            kb = 2 * (it - 16)
            src = wo[kb * 128:(kb + 2) * 128, :].rearrange("(a p) c -> p a c", p=128)
            dst = s_o[kb * 128:(kb + 2) * 128, :].rearrange("(a p) c -> p a c", p=128)
            ncol, name = 2048, "o"
        sv = stg[:, 0:ncol] if it < 16 else stg[:, 0:ncol].rearrange("p (a c) -> p a c", a=2)
        ov = obf[:, 0:ncol] if it < 16 else obf[:, 0:ncol].rearrange("p (a c) -> p a c", a=2)
        S.dma("sp", lambda e, sem: e.dma_start(out=sv, in_=src).then_inc(sem, 16), sem_oc_in[h_], writes=tk_stg)
        ACT(lambda e: e.activation(out=obf[:, 0:ncol], in_=stg[:, 0:ncol], func=AF.Copy), r=tk_stg, w=tk_obf)
        S.dma("sp", lambda e, sem: e.dma_start(out=dst, in_=ov).then_inc(sem, 16), sem_oc_st[h_], reads=tk_obf, writes=[t_oc[name][h_]])

    oc_it = [0]
    gates_of = {}
    kx_of = {0: xprep_a(xp[0:128, :], "sp")}
    if PC > 1:
        kx_of[1] = xprep_a(xp[128:256, :], "sp")
    xprep_b(kx_of[0], slice(0, 128), XC.t[0], also_halo=(PC == 1))

    def prefix_A(pc):
        ci = pc % 4
        xsl = slice(ci * 128, (ci + 1) * 128)
        if pc + 2 < PC:
            kx_of[pc + 2] = xprep_a(xp[(pc + 2) * 128:(pc + 3) * 128, :], "sp")
        yield
        for nb in range(2):
            tokmajor_proj(kvs[nb][0], kvs[nb][1], xsl, XC.t[ci], ktok[pc % 2][:, nb * 512:(nb + 1) * 512], t_ktok[pc % 2][2 * nb:2 * nb + 2], on_dve=True)
            yield
        if pc + 1 < PC:
            cn = (pc + 1) % 4
            xprep_b(kx_of[pc + 1], slice(cn * 128, (cn + 1) * 128), XC.t[cn], also_halo=(pc + 1 == PC - 1))
        for nb in range(4):
            tokmajor_proj(kvs[2 + nb][0], kvs[2 + nb][1], xsl, XC.t[ci], vtok[ci][:, nb * 512:(nb + 1) * 512], [t_vtok[ci][nb]], on_dve=(nb == 3))
            yield
        gates_of[pc] = gates(1, lambda c, xsl=xsl: xsl, [XC.t[ci]])
        yield

    def prefix_B(pc):
        ci = pc % 4
        c_ = chunk_gate_scalars(gates_of[pc], 0)
        yield
        yield from state_update(c_, ktok[pc % 2], t_ktok[pc % 2], vtok[ci], t_vtok[ci], cast=(pc == PC - 1))

    for _ in prefix_A(0):
        pass
    for pc in range(PC):
        if T0_EARLY and 0 <= pc - (PC - 6) < 4:
            xprep_t0_chunk(pc - (PC - 6))
        if T0_EARLY and 0 <= pc - (PC - 5) < 4:
            xprep_t0_chunk_b(pc - (PC - 5))
        for _ in range(2):
            if conv_list:
                convert(*conv_list.pop(0))
        if ONCHIP and pc < PC - 5:
            for _ in range(2):
                if oc_it[0] < 20:
                    onchip_convert(oc_it[0])
                    oc_it[0] += 1
        ga = prefix_A(pc + 1) if pc + 1 < PC else iter(())
        gb = prefix_B(pc)
        for _ in range(3):
            next(ga, None)
        alive = True
        while alive:
            alive = False
            for g in (ga, gb):
                try:
                    next(g)
                    alive = True
                except StopIteration:
                    pass
    nbig[0] = 4
    while conv_list:
        convert(*conv_list.pop(0))

    def norm_batch(c):
        cols = slice(c * 128, (c + 1) * 128)
        DVE(lambda e: e.tensor_scalar(out=mall, in0=mall, scalar1=1.0, scalar2=None, op0=ALU.max), r=[], w=list(t_mall))
        DVE(lambda e: e.scalar_tensor_tensor(out=mall, in0=mall, scalar=512.0 * EPS, in1=mall, op0=ALU.mult, op1=ALU.mult), r=[], w=list(t_mall))
        DVE(lambda e: e.tensor_tensor(out=v5all, in0=v5all, in1=mall, op=ALU.add), r=list(t_mall), w=list(t_v5all))
        ACT(lambda e: e.activation(out=v5all, in_=v5all, func=AF.Sqrt), r=[], w=list(t_v5all))
        DVE(lambda e: e.reciprocal(out=v5all, in_=v5all), r=[], w=list(t_v5all))
        DVE(lambda e: e.tensor_tensor(out=hnT[:, :, cols].rearrange("p (h j) t -> p h j t", j=4), in0=hball,
                                      in1=v5all.unsqueeze(2).to_broadcast([128, 4, 4, 128]), op=ALU.mult),
            r=list(t_hball) + list(t_v5all), w=[t_hn[jj][c] for jj in range(16)])

    def mlstm_chunk(g_, c, deferred=None):
        cols = slice(c * 128, (c + 1) * 128)
        c_ = chunk_gate_scalars(g_, c)
        for h in range(4):
            DVE(lambda e, h=h: e.tensor_scalar(out=A_[h], in0=triB, scalar1=g_.lf[:, c, h:h + 1], scalar2=None, op0=ALU.mult),
                r=[g_.t_lf, t_cs[3]], w=[t_A[h]])
        yield
        for h in range(4):
            i = h % 2

            def mm1(e, h=h, i=i):
                e.matmul(psm[:, 0:128], lhsT=maskA, rhs=A_[h], start=True, stop=False)
                e.matmul(psm[:, 0:128], lhsT=ident, rhs=negm, start=False, stop=True)
                e.matmul(psm[:, 128:256], lhsT=onesf, rhs=A_[h], start=True, stop=True)
                e.matmul(psm[:, 256:384], lhsT=kT[:, 2 * h, cols], rhs=qT[:, 2 * h, cols], start=True, stop=False)
                return e.matmul(psm[:, 256:384], lhsT=kT[:, 2 * h + 1, cols], rhs=qT[:, 2 * h + 1, cols], start=False, stop=True)
            PE(mm1, r=[t_A[h], t_cs[2], t_cs[5], t_cs[1], t_cs[4], t_kT[2 * h], t_kT[2 * h + 1], t_qT[2 * h], t_qT[2 * h + 1]],
               w=[t_psm[0]])
            ACT(lambda e, h=h, i=i: e.activation(out=E_[i], in_=psm[:, 0:128], func=AF.Exp, bias=g_.li[:, c, h:h + 1], scale=1.0),
                r=[t_psm[0], g_.t_li], w=[t_E[i]])
            ACT(lambda e, i=i: e.activation(out=G_[i], in_=psm[:, 128:256], func=AF.Exp), r=[t_psm[1]], w=[t_G[i]])
            DVE(lambda e, i=i: e.tensor_tensor(out=ST_[i], in0=psm[:, 256:384], in1=E_[i], op=ALU.mult), r=[t_psm[2], t_E[i]], w=[t_ST[i]])
            DVE(lambda e, h=h, i=i: e.tensor_tensor(out=qG_[i], in0=qT[:, 2 * h:2 * h + 2, cols],
                                                    in1=G_[i].unsqueeze(1).to_broadcast([128, 2, 128]), op=ALU.mult),
                r=[t_qT[2 * h], t_qT[2 * h + 1], t_G[i]], w=[t_qG[i]])
            if h == 0 and deferred is not None:
                deferred()
            yield

            def mm2(e, h=h, i=i):
                for j in range(4):
                    e.matmul(psh[:, j, :], lhsT=vtok[c][:, h * 512 + j * 128:h * 512 + (j + 1) * 128], rhs=ST_[i], start=True, stop=False)
                    e.matmul(psh[:, j, :], lhsT=Cbf[:, h, 0, j * 128:(j + 1) * 128], rhs=qG_[i][:, 0, :], start=False, stop=False)
                    e.matmul(psh[:, j, :], lhsT=Cbf[:, h, 1, j * 128:(j + 1) * 128], rhs=qG_[i][:, 1, :], start=False, stop=True)
                e.matmul(psm[:, 384:512], lhsT=onesb, rhs=ST_[i], start=True, stop=False)
                e.matmul(psm[:, 384:512], lhsT=nbc[:, 2 * h, :], rhs=qG_[i][:, 0, :], start=False, stop=False)
                return e.matmul(psm[:, 384:512], lhsT=nbc[:, 2 * h + 1, :], rhs=qG_[i][:, 1, :], start=False, stop=True)
            PE(mm2, r=[t_vtok[c][h], t_ST[i], t_qG[i], t_Cbf[h][0], t_Cbf[h][1], t_nbc, t_cs[6]], w=[t_psh, t_psm[3]])
            ACT(lambda e, i=i: e.activation(out=sq_[i], in_=psh, func=AF.Square), r=[t_psh], w=[t_sq[i]])
            DVE(lambda e, h=h: e.tensor_copy(out=hball[:, h], in_=psh), r=[t_psh], w=[t_hball[h]])
            ACT(lambda e, h=h: e.activation(out=mall[:, h, :], in_=psm[:, 384:512], func=AF.Abs), r=[t_psm[3]], w=[t_mall[h]])
            yield

            def mm3(e, i=i):
                for j in range(4):
                    ins = e.matmul(psx[:, 0:128], lhsT=onesb, rhs=sq_[i][:, j, :], start=(j == 0), stop=(j == 3))
                return ins
            PE(mm3, r=[t_sq[i], t_cs[6]], w=[t_pss])
            DVE(lambda e, h=h: e.tensor_copy(out=v5all[:, h, :], in_=psx[:, 0:128]), r=[t_pss], w=[t_v5all[h]])
            yield
        yield from state_update(c_, ktok[c % 2], t_ktok[c % 2], vtok[c], t_vtok[c], kcols=cols)

    for tile in range(TM):
        r0 = tile * 512
        XC.x = xnTs[tile % 2]
        XC.t = t_xnTs[tile % 2]
        Xn, Tn = xnTs[(tile + 1) % 2], t_xnTs[(tile + 1) % 2]
        if tile == 0 and not T0_EARLY:
            xprep4(r0)
        slkv = kvs if tile == 0 else get_slabs(6)
        for sidx in range(2):
            slab, tslab = slkv[sidx]
            for l in range(4):
                fc = sidx * 4 + l
                b = nextbig()

                def mm(e, slab=slab, l=l, b=b, X=XC.x):
                    for kc in range(8):
                        ins = e.matmul(big[b], lhsT=slab[:, kc, l * 128:(l + 1) * 128], rhs=X[:, kc, :], start=(kc == 0), stop=(kc == 7))
                    return ins
                PE(mm, r=[tslab] + XC.t, w=[t_big[b]])
                ACT(lambda e, fc=fc, b=b: e.activation(out=kT[:, fc, :], in_=big[b], func=AF.Copy), r=[t_big[b]], w=[t_kT[fc]])
        for c in range(4):
            cs = slice(c * 128, (c + 1) * 128)
            for nb in range(4):
                tokmajor_proj(slkv[2 + nb][0], slkv[2 + nb][1], cs, XC.t[c], vtok[c][:, nb * 512:(nb + 1) * 512], [t_vtok[c][nb]], on_dve=(nb % 2 == 1))
        release(6)
        g_tile = gates(4, lambda c: slice(c * 128, (c + 1) * 128), XC.t)
        slq = get_slabs(2)
        for sidx in range(2):
            slab, tslab = slq[sidx]
            for l in range(4):
                fc = sidx * 4 + l
                b = nextbig()

                def mm(e, slab=slab, l=l, b=b, X=XC.x):
                    for kc in range(8):
                        ins = e.matmul(big[b], lhsT=slab[:, kc, l * 128:(l + 1) * 128], rhs=X[:, kc, :], start=(kc == 0), stop=(kc == 7))
                    return ins
                PE(mm, r=[tslab] + XC.t, w=[t_big[b]])
                ACT(lambda e, fc=fc, b=b: e.activation(out=qT[:, fc, :], in_=big[b], func=AF.Copy, scale=1.0 / 16.0),
                    r=[t_big[b]], w=[t_qT[fc], t_mg[fc]])
        release(2)

        def conv_units():
            for g in range(4):
                sl4 = get_slabs(4)
                for l in range(4):
                    cc = 4 * g + l
                    bs = []
                    for which in range(4):
                        slab, tslab = sl4[which]
                        b = nextbig()
                        bs.append(b)

                        def mm(e, slab=slab, l=l, b=b, X=XC.x):
                            for kc in range(8):
                                ins = e.matmul(big[b], lhsT=slab[:, kc, l * 128:(l + 1) * 128], rhs=X[:, kc, :], start=(kc == 0), stop=(kc == 7))
                            return ins
                        PE(mm, r=[tslab] + XC.t, w=[t_big[b]])
                        if tile == 0 and which < 2:
                            def mmh(e, slab=slab, l=l, which=which):
                                for kc in range(8):
                                    ins = e.matmul(psx[:, 208 + 2 * which:210 + 2 * which], lhsT=slab[:, kc, l * 128:(l + 1) * 128], rhs=xh[:, kc, :],
                                                   start=(kc == 0), stop=(kc == 7))
                                return ins
                            PE(mmh, r=[tslab, t_xh], w=[t_phalo])
                        if which == 0:
                            ACT(lambda e, b=b: e.activation(out=tmp[0][:, 0:512], in_=big[b], func=AF.Copy), r=[t_big[b]], w=[t_tmp[0]])
                        elif which == 1:
                            if tile == 0:
                                ACT(lambda e: e.activation(out=tmp[5][:, 0:2], in_=psx[:, 208:210], func=AF.Copy), r=[t_phalo], w=[t_tmp[5]])
                                DVE(lambda e, cc=cc: e.tensor_tensor(out=uh[:, cc, :], in0=psx[:, 210:212], in1=tmp[5][:, 0:2], op=ALU.mult),
                                    r=[t_phalo, t_tmp[5]], w=[t_uh[cc]])
                            DVE(lambda e, b=b: e.tensor_tensor(out=tmp[1][:, 2:514], in0=big[b], in1=tmp[0][:, 0:512], op=ALU.mult),
                                r=[t_big[b], t_tmp[0]], w=[t_tmp[1]])
                            DVE(lambda e, cc=cc: e.tensor_copy(out=tmp[1][:, 0:2], in_=uh[:, cc, :]), r=[t_uh[cc]], w=[t_tmp[1]])
                            POOL(lambda e, cc=cc: e.tensor_tensor(out=tmp[2][:, 0:512], in0=tmp[1][:, 0:512], in1=cwt[:, cc, 0:1].to_broadcast([128, 512]), op=ALU.mult),
                                 r=[t_tmp[1], t_cw], w=[t_tmp[2]])
                            POOL(lambda e, cc=cc: e.tensor_tensor(out=tmp[0][:, 0:512], in0=tmp[1][:, 1:513], in1=cwt[:, cc, 1:2].to_broadcast([128, 512]), op=ALU.mult),
                                 r=[t_tmp[1], t_cw], w=[t_tmp[0]])
                            POOL(lambda e: e.tensor_tensor(out=tmp[2][:, 0:512], in0=tmp[2][:, 0:512], in1=tmp[0][:, 0:512], op=ALU.add),
                                 r=[t_tmp[0]], w=[t_tmp[2]])
                            POOL(lambda e, cc=cc: e.tensor_tensor(out=tmp[0][:, 0:512], in0=tmp[1][:, 2:514], in1=cwt[:, cc, 2:3].to_broadcast([128, 512]), op=ALU.mult),
                                 r=[t_tmp[1], t_cw], w=[t_tmp[0]])
                            POOL(lambda e: e.tensor_tensor(out=tmp[2][:, 0:512], in0=tmp[2][:, 0:512], in1=tmp[0][:, 0:512], op=ALU.add),
                                 r=[t_tmp[0]], w=[t_tmp[2]])
                            DVE(lambda e, cc=cc: e.tensor_copy(out=uh[:, cc, :], in_=tmp[1][:, 512:514]), r=[t_tmp[1]], w=[t_uh[cc]])
                        elif which == 2:
                            ACT(lambda e, b=b: e.activation(out=tmp[3][:, 0:512], in_=big[b], func=AF.Tanh, scale=0.5), r=[t_big[b]], w=[t_tmp[3]])
                            DVE(lambda e, b=b: e.scalar_tensor_tensor(out=tmp[3][:, 0:512], in0=tmp[3][:, 0:512], scalar=1.0, in1=big[b],
                                                                      op0=ALU.add, op1=ALU.mult), r=[t_big[b]], w=[t_tmp[3]])
                        else:
                            DVE(lambda e, b=b: e.tensor_tensor(out=tmp[2][:, 0:512], in0=big[b], in1=tmp[2][:, 0:512], op=ALU.mult),
                                r=[t_big[b]], w=[t_tmp[2]])
                            POOL(lambda e, cc=cc: e.tensor_tensor(out=ysT[:, cc, :], in0=tmp[3][:, 0:512], in1=tmp[2][:, 0:512], op=ALU.mult),
                                 r=[t_tmp[2], t_tmp[3]], w=[t_ys[cc]])
                        yield
                release(4)

        cu = conv_units()
        li_next = xprep4_loads(r0 + 512) if tile + 1 < TM else None
        if tile == 0:
            for c_ in conv_late["t0a"]:
                convert(*c_)
        for c in range(4):
            if tile == 0 and c == 2:
                for c_ in conv_late["t0b"]:
                    convert(*c_)
            dfr = (lambda cc=c - 1: norm_batch(cc)) if c > 0 else None
            gen = mlstm_chunk(g_tile, c, dfr)
            yi = 0
            while True:
                if yi % 8 != 7:
                    next(cu, None)
                try:
                    next(gen)
                except StopIteration:
                    break
                yi += 1
        norm_batch(3)
        if tile == 0:
            for c_ in conv_late["t0c"]:
                convert(*c_)
        ks_next = None
        if tile + 1 < TM:
            ks_next = {0: xprep_a(None, i=li_next[0])}
        for _ in cu:
            pass

        for g in range(4):
            (so, tso), (sz, tsz) = get_slabs(2)
            for l in range(4):
                j = 4 * g + l
                bo, bz = nextbig(), nextbig()

                def mmo(e, so=so, l=l, bo=bo, X=XC.x):
                    for kc in range(8):
                        ins = e.matmul(big[bo], lhsT=so[:, kc, l * 128:(l + 1) * 128], rhs=X[:, kc, :], start=(kc == 0), stop=(kc == 7))
                    return ins

                def mmz(e, sz=sz, l=l, bz=bz, X=XC.x):
                    for kc in range(8):
                        ins = e.matmul(big[bz], lhsT=sz[:, kc, l * 128:(l + 1) * 128], rhs=X[:, kc, :], start=(kc == 0), stop=(kc == 7))
                    return ins
                PE(mmo, r=[tso] + XC.t, w=[t_big[bo]])
                PE(mmz, r=[tsz] + XC.t, w=[t_big[bz]])
                ACT(lambda e, bo=bo: e.activation(out=tmp[0][:, 0:512], in_=big[bo], func=AF.Tanh, scale=0.5), r=[t_big[bo]], w=[t_tmp[0]])
                ACT(lambda e, bz=bz: e.activation(out=tmp[1][:, 0:512], in_=big[bz], func=AF.Tanh, scale=0.5), r=[t_big[bz]], w=[t_tmp[1]])
                DVE(lambda e, bz=bz: e.scalar_tensor_tensor(out=tmp[1][:, 0:512], in0=tmp[1][:, 0:512], scalar=1.0, in1=big[bz], op0=ALU.add, op1=ALU.mult),
                    r=[t_big[bz]], w=[t_tmp[1]])
                DVE(lambda e: e.scalar_tensor_tensor(out=tmp[1][:, 0:512], in0=tmp[0][:, 0:512], scalar=1.0, in1=tmp[1][:, 0:512], op0=ALU.add, op1=ALU.mult),
                    r=[t_tmp[0]], w=[t_tmp[1]])
                DVE(lambda e, j=j: e.scalar_tensor_tensor(out=hnT[:, j, :], in0=hnT[:, j, :], scalar=mhw[:, j:j + 1], in1=tmp[1][:, 0:512],
                                                          op0=ALU.mult, op1=ALU.mult),
                    r=[t_tmp[1], t_mh] + t_hn[j], w=[t_a[j]])
            release(2)
            if ks_next is not None:
                if g + 1 < 4:
                    ks_next[g + 1] = xprep_a(None, i=li_next[g + 1])
                xprep_b(ks_next[g], slice(g * 128, (g + 1) * 128), Tn[g], X=Xn)
                if g + 2 < 4:
                    li_next.append(xload(xm[r0 + 512 + (g + 2) * 128:r0 + 512 + (g + 3) * 128, :]))

        gsl = None
        for cb in range(4):
            (sa, tsa), (sbb, tsb) = get_slabs(2)
            if cb % 2 == 0:
                gsl = get_slabs(2)
            for c2 in range(2):
                cch = 2 * cb + c2
                lg = cch % 4
                bpa, bpb, bga, bgb = nextbig(), nextbig(), nextbig(), nextbig()

                def mmp(e, slab, src, b, c2=c2):
                    for kc in range(16):
                        ins = e.matmul(big[b], lhsT=slab[:, kc, c2 * 128:(c2 + 1) * 128], rhs=src[:, kc, :], start=(kc == 0), stop=(kc == 15))
                    return ins

                def mmg(e, slab, b, lg=lg, X=XC.x):
                    for kc in range(8):
                        ins = e.matmul(big[b], lhsT=slab[:, kc, lg * 128:(lg + 1) * 128], rhs=X[:, kc, :], start=(kc == 0), stop=(kc == 7))
                    return ins
                PE(lambda e, sa=sa, bpa=bpa, mmp=mmp: mmp(e, sa, hnT, bpa), r=[tsa] + t_a, w=[t_big[bpa]])
                PE(lambda e, gs=gsl[0][0], bga=bga, mmg=mmg: mmg(e, gs, bga), r=[gsl[0][1]] + XC.t, w=[t_big[bga]])
                PE(lambda e, sbb=sbb, bpb=bpb, mmp=mmp: mmp(e, sbb, ysT, bpb), r=[tsb] + t_ys, w=[t_big[bpb]])
                PE(lambda e, gs=gsl[1][0], bgb=bgb, mmg=mmg: mmg(e, gs, bgb), r=[gsl[1][1]] + XC.t, w=[t_big[bgb]])
                ACT(lambda e, bga=bga: e.activation(out=tmp[2][:, 0:512], in_=big[bga], func=AF.Tanh, scale=0.5), r=[t_big[bga]], w=[t_tmp[2]])
                ACT(lambda e, bgb=bgb: e.activation(out=tmp[3][:, 0:512], in_=big[bgb], func=AF.Tanh, scale=0.5), r=[t_big[bgb]], w=[t_tmp[3]])
                DVE(lambda e, bpa=bpa: e.scalar_tensor_tensor(out=tmp[2][:, 0:512], in0=tmp[2][:, 0:512], scalar=1.0, in1=big[bpa], op0=ALU.add, op1=ALU.mult),
                    r=[t_big[bpa]], w=[t_tmp[2]])
                DVE(lambda e, bpb=bpb: e.scalar_tensor_tensor(out=tmp[3][:, 0:512], in0=tmp[3][:, 0:512], scalar=1.0, in1=big[bpb], op0=ALU.add, op1=ALU.mult),
                    r=[t_big[bpb]], w=[t_tmp[3]])
                POOL(lambda e, cch=cch: e.tensor_tensor(out=qT[:, cch, :], in0=tmp[2][:, 0:512], in1=tmp[3][:, 0:512], op=ALU.add),
                     r=[t_tmp[2], t_tmp[3]], w=[t_mg[cch], t_qT[cch]])
            release(2 if cb % 2 == 0 else 4)

        (so0, tso0), (so1, tso1) = get_slabs(2)
        for c in range(4):
            cs = slice(c * 128, (c + 1) * 128)
            i = xctr[0] % 2
            xctr[0] += 1
            S.dma("act", lambda e, sem, i=i, c=c, r0=r0: e.dma_start(out=xst[i], in_=xm[r0 + c * 128:r0 + (c + 1) * 128, :]).then_inc(sem, 16),
                  sem_x[i], writes=[t_xst[i]])
            bsl = []
            for nb, (slab, tslab) in enumerate(((so0, tso0), (so1, tso1))):
                b = nextbig()
                bsl.append(b)

                def mm(e, slab=slab, b=b, cs=cs):
                    for kc in range(8):
                        ins = e.matmul(big[b], lhsT=qT[:, kc, cs], rhs=slab[:, kc, :], start=(kc == 0), stop=(kc == 7))
                    return ins
                PE(mm, r=[tslab] + t_mg, w=[t_big[b]])
            for nb in range(2):
                DVE(lambda e, nb=nb, b=bsl[nb], i=i: e.scalar_tensor_tensor(out=xst[i][:, nb * 512:(nb + 1) * 512], in0=big[b], scalar=0.5,
                                                                            in1=xst[i][:, nb * 512:(nb + 1) * 512], op0=ALU.mult, op1=ALU.add),
                    r=[t_big[bsl[nb]]], w=[t_xst[i]])
            kj = xk[0] % 2
            cj = 4 + (c % 2) * 2
            POOL(lambda e, cj=cj: e.memset(xsc2[:, cj - 4:cj - 3], 0.0), w=[t_xsc2[c % 2]])
            ACT(lambda e, kj=kj, i=i, cj=cj: e.activation(out=xs[kj], in_=xst[i], func=AF.Square, accum_out=xsc2[:, cj - 4:cj - 3]),
                r=[t_xst[i]], w=[t_xs[kj], t_xsc2[c % 2]])
            DVE(lambda e, cj=cj: e.tensor_scalar(out=xsc2[:, cj - 3:cj - 2], in0=xsc2[:, cj - 4:cj - 3], scalar1=D * EPS, scalar2=None, op0=ALU.add),
                r=[], w=[t_xsc2[c % 2]])
            POOL(lambda e, cj=cj: e.tensor_tensor(out=xsc2[:, cj - 3:cj - 2], in0=xsc2[:, cj - 3:cj - 2], in1=mhalf[:, 0:1], op=ALU.pow),
                 r=[t_cs[7]], w=[t_xsc2[c % 2]])
            DVE(lambda e, i=i, cj=cj: e.scalar_tensor_tensor(out=xst[i], in0=xst[i], scalar=xsc2[:, cj - 3:cj - 2], in1=fnw, op0=ALU.mult, op1=ALU.mult),
                r=[t_xsc2[c % 2], t_fn], w=[t_xst[i]])
            S.dma("sp", lambda e, sem, c=c, r0=r0, i=i: e.dma_start(out=y_out[r0 + c * 128:r0 + (c + 1) * 128, :], in_=xst[i]).then_inc(sem, 16),
                  sem_out[i], reads=[t_xst[i]], writes=[])
        release(2)
    fins = []
    for so in sem_out:
        f_ = Tok()
        f_.w = (so, so.val)
        fins.append(f_)
    S.wait_all("sp", fins)
    S.emit()
    return nc


_NC_CACHE = {}


def kernel(x, meta_tokens, norm_w, w_in, b_igate, b_fgate, mh_norm_w, conv_w, w_proj_a, w_proj_b, w_out, final_norm_w):
    x = np.asarray(x, dtype=np.float32)
    B, SEQ, Dm = x.shape
    half = SEQ // 2
    TM = half // 512
    PC = 1 + half // 128
    key = (TM, PC)
    if key not in _NC_CACHE:
        _NC_CACHE[key] = build(TM, PC)
    nc = _NC_CACHE[key]
    meta = np.asarray(meta_tokens, dtype=np.float32)
    common = {
        "w_in": np.ascontiguousarray(np.asarray(w_in, np.float32)[0]),
        "wa": np.ascontiguousarray(np.asarray(w_proj_a, np.float32)[0]),
        "wb": np.ascontiguousarray(np.asarray(w_proj_b, np.float32)[0]),
        "wo": np.ascontiguousarray(np.asarray(w_out, np.float32)[0]),
        "norm_w": np.ascontiguousarray(np.asarray(norm_w, np.float32)[0].reshape(8, 128).T),
        "bias8": np.ascontiguousarray(np.broadcast_to(np.concatenate([np.asarray(b_igate, np.float32)[0],
                                                                       np.asarray(b_fgate, np.float32)[0]])[None, :], (128, 8))),
        "mhw": np.ascontiguousarray(np.asarray(mh_norm_w, np.float32)[0].reshape(16, 128).T),
        "cw": np.ascontiguousarray(np.asarray(conv_w, np.float32)[0].reshape(3, 16, 128).transpose(2, 1, 0).reshape(128, 48)),
        "fnw": np.ascontiguousarray(np.broadcast_to(np.asarray(final_norm_w, np.float32)[None, :], (128, Dm))),
    }
    in_maps = []
    for b in range(B):
        for s in range(2):
            xmain = np.ascontiguousarray(x[b, s * half:(s + 1) * half])
            xpre = np.zeros((PC * 128, Dm), np.float32)
            if s == 0:
                xpre[PC * 128 - 16:] = meta
            else:
                xpre[112:128] = meta
                xpre[128:] = x[b, 0:half]
            m = dict(common)
            m["xm"] = xmain
            m["xp"] = xpre
            in_maps.append(m)
    res = run_bass_kernel_spmd(nc, in_maps, core_ids=list(range(len(in_maps))))
    out = np.empty((B, SEQ, Dm), np.float32)
    for b in range(B):
        for s in range(2):
            out[b, s * half:(s + 1) * half] = res.results[2 * b + s]["y"]
    return out
```

```python
import numpy as np
import concourse.bass as bass
import concourse.mybir as mybir
from concourse.bass_utils import run_bass_kernel_spmd

F32 = mybir.dt.float32
BF16 = mybir.dt.bfloat16
ALU = mybir.AluOpType
AF = mybir.ActivationFunctionType

D = 1024
NIN = 18440
EPS = 1e-6
OFF_Q, OFF_K, OFF_V, OFF_O, OFF_ZA, OFF_I = 0, 1024, 2048, 4096, 6144, 8192
OFF_SH, OFF_SB, OFF_SC, OFF_ZS, OFF_GA, OFF_GB = 8200, 10248, 12296, 14344, 16392, 17416

SELF_SYNC = {"pool", "dve", "act"}


class Tok:
    __slots__ = ("w", "r", "excl")

    def __init__(self, excl=False):
        self.w = None
        self.r = []
        self.excl = excl


class SemC:
    __slots__ = ("sem", "val", "owner")

    def __init__(self, sem, owner=None):
        self.sem = sem
        self.val = 0
        self.owner = owner


class EngQ:
    def __init__(self, name, semc):
        self.name = name
        self.semc = semc
        semc.owner = self
        self.seen = {}
        self.prog = []


class Sched:
    def __init__(self, nc):
        self.nc = nc
        self.q = {}
        for n in ("pe", "act", "dve", "pool", "sp"):
            self.q[n] = EngQ(n, SemC(nc.alloc_semaphore(name="q_" + n)))

    def new_sem(self, name):
        self.nsem = getattr(self, "nsem", 0) + 1
        return SemC(self.nc.alloc_semaphore(name=f"{name}_{self.nsem}"))

    def _waits(self, q, reads, writes):
        need = {}

        def add(d):
            if d is None:
                return
            s, v = d
            if s.owner is q and q.name not in SELF_SYNC:
                return
            if q.seen.get(s, 0) >= v:
                return
            if need.get(s, 0) < v:
                need[s] = v

        for t in reads:
            add(t.w)
        for t in writes:
            add(t.w)
            for d in t.r:
                add(d)
        for s, v in need.items():
            q.seen[s] = v
        return list(need.items())

    def op(self, qn, fn, reads=(), writes=()):
        q = self.q[qn]
        writes = list(dict.fromkeys(list(writes) + [t for t in reads if t.excl]))
        reads = [t for t in reads if not t.excl]
        waits = self._waits(q, reads, writes)
        q.semc.val += 1
        me = (q.semc, q.semc.val)
        for t in writes:
            t.w = me
            t.r = []
        for t in reads:
            t.r.append(me)
        sem = q.semc.sem

        def thunk(eng):
            for s, v in waits:
                eng.wait_ge(s.sem, v)
            fn(eng).then_inc(sem, 1)
        q.prog.append(thunk)

    def dma(self, qn, fn, semc, reads=(), writes=(), n=1):
        q = self.q[qn]
        waits = self._waits(q, reads, writes)
        semc.val += 16 * n
        me = (semc, semc.val)
        for t in writes:
            t.w = me
            t.r = []
        for t in reads:
            t.r.append(me)
        sem = semc.sem

        def thunk(eng):
            for s, v in waits:
                eng.wait_ge(s.sem, v)
            fn(eng, sem)
        q.prog.append(thunk)

    def wait_all(self, qn, toks):
        q = self.q[qn]
        waits = self._waits(q, toks, ())

        def thunk(eng):
            for s, v in waits:
                eng.wait_ge(s.sem, v)
        q.prog.append(thunk)

    def emit(self):
        nc = self.nc
        progs = self.q
        with nc.Block() as block:
            @block.tensor
            def _(e):
                for t in progs["pe"].prog:
                    t(e)

            @block.scalar
            def _(e):
                for t in progs["act"].prog:
                    t(e)

            @block.vector
            def _(e):
                for t in progs["dve"].prog:
                    t(e)

            @block.gpsimd
            def _(e):
                for t in progs["pool"].prog:
                    t(e)

            @block.sync
            def _(e):
                for t in progs["sp"].prog:
                    t(e)


def build(TM, PC):
    nc = bass.Bass("TRN2", target_bir_lowering=False)
    S = Sched(nc)
    NT = TM * 512
    NP = PC * 128

    def din(name, shape):
        return nc.dram_tensor(name, shape, F32, kind="ExternalInput").ap()

    xm = din("xm", [NT, D])
    xp = din("xp", [NP, D])
    w_in = din("w_in", [D, NIN])
    wa = din("wa", [2048, D])
    wb = din("wb", [2048, D])
    wo = din("wo", [D, D])
    norm_w = din("norm_w", [128, 8])
    bias_d = din("bias8", [128, 8])
    mhw_d = din("mhw", [128, 16])
    cw_d = din("cw", [128, 48])
    fnw_d = din("fnw", [128, D])
    y_out = nc.dram_tensor("y", [NT, D], F32, kind="ExternalOutput").ap()

    def sb(name, shape, dt):
        return nc.alloc_sbuf_tensor(name, shape, dt).ap()

    PE = lambda fn, r=(), w=(): S.op("pe", fn, r, w)
    ACT = lambda fn, r=(), w=(): S.op("act", fn, r, w)
    DVE = lambda fn, r=(), w=(): S.op("dve", fn, r, w)
    POOL = lambda fn, r=(), w=(): S.op("pool", fn, r, w)

    NSLOT = 6
    ring = [sb(f"ring{i}", [128, 4096], BF16) for i in range(NSLOT)]
    t_ring = [Tok() for _ in range(NSLOT)]
    sem_ring = [S.new_sem(f"ring{i}") for i in range(NSLOT)]
    Cst = sb("Cst", [128, 4, 2, 512], F32)
    Cbf = sb("Cbf", [128, 4, 2, 512], BF16)
    t_C = [[Tok() for _ in range(2)] for _ in range(4)]
    t_Cbf = [[Tok() for _ in range(2)] for _ in range(4)]
    nst = sb("nst", [128, 8], F32)
    nbc = sb("nbc", [128, 8, 128], BF16)
    t_n, t_nbc = Tok(), Tok()
    xnTs = [sb(f"xnT{i}", [128, 8, 512], BF16) for i in range(2)]
    t_xnTs = [[Tok() for _ in range(4)] for _ in range(2)]

    class XC:
        x = xnTs[1]
        t = t_xnTs[1]
    xh = sb("xh", [128, 8, 2], BF16)
    t_xh = Tok()
    xst = [sb(f"xst{i}", [128, D], F32) for i in range(2)]
    t_xst = [Tok() for _ in range(2)]
    sem_x = [S.new_sem(f"x{i}") for i in range(2)]
    sem_xs = [S.new_sem(f"xs{i}") for i in range(2)]
    xs = [sb(f"xs{i}", [128, D], BF16) for i in range(2)]
    t_xs = [Tok() for _ in range(2)]
    xsc = sb("xsc", [128, 8], F32)
    t_xsc = [Tok() for _ in range(2)]
    xk = [0]
    qT = sb("qT", [128, 8, 512], BF16)
    kT = sb("kT", [128, 8, 512], BF16)
    t_qT = [Tok() for _ in range(8)]
    t_kT = [Tok() for _ in range(8)]
    ktok = [sb(f"ktok{i}", [128, 1024], BF16) for i in range(2)]
    vtok = [sb(f"vtok{i}", [128, 2048], BF16) for i in range(4)]
    t_ktok = [[Tok() for _ in range(4)] for _ in range(4)]
    t_vtok = [[Tok() for _ in range(4)] for _ in range(4)]
    hnT = sb("hnT", [128, 16, 512], BF16)
    t_hn = [[Tok() for _ in range(4)] for _ in range(16)]
    t_a = [Tok() for _ in range(16)]
    ysT = sb("ysT", [128, 16, 512], BF16)
    t_ys = [Tok() for _ in range(16)]
    t_mg = [Tok() for _ in range(8)]
    NTMP = 6
    tmp = [sb(f"tmp{i}", [128, 514] if i < 4 else [128, 2], F32) for i in range(NTMP)]
    t_tmp = [Tok() for _ in range(NTMP)]
    uh = sb("uh", [128, 16, 2], F32)
    t_uh = [Tok() for _ in range(16)]
    xsc2 = sb("xsc2", [128, 4], F32)
    t_xsc2 = [Tok() for _ in range(2)]
    sem_out = [S.new_sem("out0"), S.new_sem("out1")]
    def dbl(name, shape, dt):
        return [sb(f"{name}{i}", shape, dt) for i in range(2)], [Tok() for _ in range(2)]
    A_ = [sb(f"A{i}", [128, 128], F32) for i in range(4)]
    t_A = [Tok() for _ in range(4)]
    E_, t_E = dbl("E", [128, 128], F32)
    G_, t_G = dbl("G", [128, 128], F32)
    ST_, t_ST = dbl("ST", [128, 128], BF16)
    qG_, t_qG = dbl("qG", [128, 2, 128], BF16)
    mall = sb("mall", [128, 4, 128], F32)
    t_mall = [Tok() for _ in range(4)]
    hball = sb("hball", [128, 4, 4, 128], F32)
    t_hball = [Tok() for _ in range(4)]
    v5all = sb("v5all", [128, 4, 128], F32)
    t_v5all = [Tok() for _ in range(4)]
    sq_, t_sq = dbl("sq", [128, 4, 128], BF16)
    class _NS:
        pass
    GS = []
    for gi in range(2):
        g_ = _NS()
        g_.gpre = sb(f"gpre{gi}", [128, 4, 8], F32)
        g_.gth = sb(f"gth{gi}", [128, 4, 8], F32)
        g_.li = sb(f"li{gi}", [128, 4, 4], F32)
        g_.lf = sb(f"lf{gi}", [128, 4, 4], F32)
        g_.gex = sb(f"gex{gi}", [128, 4, 4], F32)
        g_.t_gpre, g_.t_gth, g_.t_li, g_.t_lf, g_.t_gex = Tok(), Tok(), Tok(), Tok(), Tok()
        GS.append(g_)
    CSs = []
    for gi in range(2):
        c_ = _NS()
        c_.gsum = sb(f"gsum{gi}", [128, 4], F32)
        c_.gg = sb(f"gg{gi}", [128, 4], F32)
        c_.dec = sb(f"dec{gi}", [128, 4], F32)
        c_.t_gsum, c_.t_gg, c_.t_dec = Tok(), Tok(), Tok()
        CSs.append(c_)
    gctr = [0, 0]
    sc1 = sb("sc1", [128, 4], F32)
    t_sc1 = Tok()
    identf = sb("identf", [128, 128], F32)
    ident = sb("ident", [128, 128], BF16)
    maskA = sb("maskA", [128, 128], F32)
    triB = sb("triB", [128, 128], F32)
    negm = sb("negm", [128, 128], BF16)
    onesf = sb("onesf", [128, 128], F32)
    onesb = sb("onesb", [128, 128], BF16)
    mhalf = sb("mhalf", [128, 1], F32)
    wif = sb("wif", [128, 8, 8], BF16)
    nw32 = sb("nw32", [128, 8], F32)
    mhw = sb("mhwc", [128, 16], F32)
    cwt = sb("cwt", [128, 16, 3], F32)
    fnw = sb("fnwbc", [128, D], F32)
    bias8 = sb("bias8sb", [128, 8], F32)
    t_const = Tok()
    t_wif = Tok()

    big = [nc.alloc_psum_tensor(f"big{i}", [128, 512], F32).ap() for i in range(4)]
    t_big = [Tok(True) for _ in range(4)]
    psm = nc.alloc_psum_tensor("psm", [128, 512], F32).ap()
    t_psm = [Tok(True)] * 4
    psh = nc.alloc_psum_tensor("psh", [128, 4, 128], F32).ap()
    t_psh = Tok(True)
    psx = nc.alloc_psum_tensor("psx", [128, 512], F32).ap()
    t_pss = t_pnu = t_pg = t_psf = t_ptot = t_phalo = Tok(True)
    psT = nc.alloc_psum_tensor("psT", [128, 8, 128], BF16).ap()
    t_psT = Tok(True)
    big += [psm, psh.rearrange("p a b -> p (a b)")]
    t_big += [t_psm[0], t_psh]
    bigctr = [0]
    nbig = [6]

    def nextbig():
        i = bigctr[0] % nbig[0]
        bigctr[0] += 1
        return i

    s_in = nc.dram_tensor("s_in", [D, NIN], BF16).ap()
    s_a = nc.dram_tensor("s_a", [2048, D], BF16).ap()
    s_b = nc.dram_tensor("s_b", [2048, D], BF16).ap()
    s_o = nc.dram_tensor("s_o", [D, D], BF16).ap()
    t_scr = {}

    def convert(name, dst, src, cols=None):
        t = Tok()
        sc = S.new_sem("cv_" + name)
        t_scr.setdefault(name, [])
        if cols is None:
            S.dma("pool", lambda e, sem: e.dma_start(out=dst, in_=src).then_inc(sem, 16), sc, writes=[t])
        else:
            c0, c1 = cols
            S.dma("pool", lambda e, sem: e.dma_start(out=dst[:, c0:c1], in_=src[:, c0:c1]).then_inc(sem, 16), sc, writes=[t])
        t_scr[name].append(t)

    w_in_v = w_in.rearrange("(kc p) f -> p kc f", p=128)
    S.dma("pool", lambda e, sem: e.dma_start(out=wif, in_=w_in_v[:, :, OFF_I:OFF_I + 8]).then_inc(sem, 16),
          S.new_sem("wif"), writes=[t_wif])
    for i in range(NSLOT):
        off = OFF_K + 512 * i
        S.dma("pool", lambda e, sem, i=i, off=off: e.dma_start(out=ring[i].rearrange("p (kc f) -> p kc f", kc=8),
                                                                in_=w_in_v[:, :, off:off + 512]).then_inc(sem, 16),
              S.new_sem(f"pre{i}"), writes=[t_ring[i]])
    conv_list = [("q", s_in, w_in, (OFF_Q, OFF_K)), ("cv0", s_in, w_in, (OFF_SH, OFF_SC)), ("cv1", s_in, w_in, (OFF_SC, OFF_GA)),
                 ("oz", s_in, w_in, (OFF_O, OFF_I)), ("g", s_in, w_in, (OFF_GA, NIN)), ("a", s_a, wa, None), ("b", s_b, wb, None),
                 ("o", s_o, wo, None), ("kv", s_in, w_in, (OFF_K, OFF_O))]
    _cl = []
    for (name, dst, src, cols) in conv_list:
        if cols is None or cols[1] - cols[0] <= 1024:
            _cl.append((name, dst, src, cols))
        else:
            for c0 in range(cols[0], cols[1], 1024):
                _cl.append((name, dst, src, (c0, min(c0 + 1024, cols[1]))))
    conv_list = [c_ for c_ in _cl if c_[0] in ("q", "cv0", "cv1")]
    conv_late = {"t0a": [c_ for c_ in _cl if c_[0] == "oz"],
                 "t0b": [c_ for c_ in _cl if c_[0] in ("g", "a", "b")],
                 "t0c": [c_ for c_ in _cl if c_[0] in ("o", "kv")]}
    n_blocks = {}
    for c_ in _cl:
        t_scr.setdefault(c_[0], [])
        n_blocks[c_[0]] = n_blocks.get(c_[0], 0) + 1
    convert(*conv_list.pop(0))
    s_in_v = s_in.rearrange("(kc p) f -> p kc f", p=128)
    s_a_v = s_a.rearrange("(kc p) c -> p kc c", p=128)
    s_b_v = s_b.rearrange("(kc p) c -> p kc c", p=128)
    s_o_v = s_o.rearrange("(kc p) c -> p kc c", p=128)

    def slab_in(off):
        name = ("q" if off < OFF_K else "kv" if off < OFF_O else "oz" if off < OFF_I else
                "cv0" if off < OFF_SC else "cv1" if off < OFF_GA else "g")
        return (name, s_in_v[:, :, off:off + 512], 8, 512)

    def slab_ab(which, cb):
        v = s_a_v if which == "a" else s_b_v
        return (which, v[:, :, cb * 256:(cb + 1) * 256], 16, 256)

    def slab_o(nb):
        return ("o", s_o_v[:, :, nb * 512:(nb + 1) * 512], 8, 512)

    kv_slabs = [slab_in(OFF_K), slab_in(OFF_K + 512)] + [slab_in(OFF_V + 512 * i) for i in range(4)]
    tile_seq = kv_slabs + [slab_in(OFF_Q), slab_in(OFF_Q + 512)]
    for g in range(4):
        tile_seq += [slab_in(OFF_SH + 512 * g), slab_in(OFF_SC + 512 * g), slab_in(OFF_ZS + 512 * g), slab_in(OFF_SB + 512 * g)]
    for g in range(4):
        tile_seq += [slab_in(OFF_O + 512 * g), slab_in(OFF_ZA + 512 * g)]
    for cb in range(4):
        tile_seq += [slab_ab("a", cb), slab_ab("b", cb)]
        if cb % 2 == 0:
            tile_seq += [slab_in(OFF_GA + 512 * (cb // 2)), slab_in(OFF_GB + 512 * (cb // 2))]
    tile_seq += [slab_o(0), slab_o(1)]
    seq = list(tile_seq)
    for _ in range(TM - 1):
        seq += tile_seq
    st = {"loaded": NSLOT, "used": 0, "free": 0}

    def _load(pos):
        name, src, nkc, nf = seq[pos]
        slot = pos % NSLOT
        dst = ring[slot].rearrange("p (kc f) -> p kc f", kc=nkc)
        assert len(t_scr[name]) == n_blocks[name], f"slab load of {name} recorded before its conversion"
        S.dma("sp", lambda e, sem: e.dma_start(out=dst, in_=src).then_inc(sem, 16), sem_ring[slot],
              reads=(t_scr[name] if isinstance(t_scr[name], list) else [t_scr[name]]), writes=[t_ring[slot]])

    def _prefetch():
        while st["loaded"] < min(len(seq), st["free"] + NSLOT):
            _load(st["loaded"])
            st["loaded"] += 1

    def get_slabs(n):
        out = []
        first = st["used"]
        st["used"] += n
        assert st["used"] - st["free"] <= NSLOT
        _prefetch()
        for pos in range(first, first + n):
            name, src, nkc, nf = seq[pos]
            slot = pos % NSLOT
            out.append((ring[slot].rearrange("p (kc f) -> p kc f", kc=nkc), t_ring[slot]))
        return out

    def release(n):
        st["free"] += n
        assert st["free"] <= st["used"]
        _prefetch()

    def cload(dst, src, slow=False):
        S.dma("sp", lambda e, sem: e.dma_start(out=dst, in_=src, allow_slow_non_contiguous=slow).then_inc(sem, 16),
              S.new_sem("c"), writes=[t_const])
    t_cs = [Tok() for _ in range(12)]
    POOL(lambda e: e.memset(identf, 1.0), w=[t_cs[0]])
    POOL(lambda e: e.affine_select(out=identf, in_=identf, pattern=[[-1, 128]], compare_op=ALU.is_equal, fill=0.0, base=0, channel_multiplier=1), w=[t_cs[0]])
    POOL(lambda e: e.tensor_copy(out=ident, in_=identf), r=[t_cs[0]], w=[t_cs[1]])
    POOL(lambda e: e.memset(maskA, 1.0), w=[t_cs[2]])
    POOL(lambda e: e.affine_select(out=maskA, in_=maskA, pattern=[[-1, 128]], compare_op=ALU.is_gt, fill=0.0, base=0, channel_multiplier=1), w=[t_cs[2]])
    POOL(lambda e: e.memset(triB, 1.0), w=[t_cs[3]])
    POOL(lambda e: e.affine_select(out=triB, in_=triB, pattern=[[1, 128]], compare_op=ALU.is_ge, fill=0.0, base=0, channel_multiplier=-1), w=[t_cs[3]])
    POOL(lambda e: e.memset(negm, -30000.0), w=[t_cs[4]])
    POOL(lambda e: e.affine_select(out=negm, in_=negm, pattern=[[-1, 128]], compare_op=ALU.is_gt, fill=0.0, base=0, channel_multiplier=1), w=[t_cs[4]])
    POOL(lambda e: e.memset(onesf, 1.0), w=[t_cs[5]])
    POOL(lambda e: e.memset(onesb, 1.0), w=[t_cs[6]])
    POOL(lambda e: e.memset(mhalf, -0.5), w=[t_cs[7]])
    POOL(lambda e: e.memset(Cst.rearrange("p a b c -> p (a b c)"), 0.0), w=[t_C[h][d] for h in range(4) for d in range(2)])
    POOL(lambda e: e.memset(Cbf.rearrange("p a b c -> p (a b c)"), 0.0), w=[t_Cbf[h][d] for h in range(4) for d in range(2)])
    POOL(lambda e: e.memset(nst, 0.0), w=[t_n])
    POOL(lambda e: e.memset(nbc.rearrange("p a b -> p (a b)"), 0.0), w=[t_nbc])
    t_all_const = [t_cs[i] for i in range(8)]
    t_nw, t_mh, t_cw, t_fn, t_b8 = Tok(), Tok(), Tok(), Tok(), Tok()
    def pl(dst, src, tk, slow=True):
        S.dma("sp", lambda e, sem: e.dma_start(out=dst, in_=src, allow_slow_non_contiguous=slow).then_inc(sem, 16),
              S.new_sem("p"), writes=[tk])
    pl(nw32, norm_w, t_nw, slow=False)
    pl(mhw, mhw_d, t_mh, slow=False)
    pl(cwt.rearrange("p a b -> p (a b)"), cw_d, t_cw, slow=False)
    pl(fnw, fnw_d, t_fn, slow=False)
    pl(bias8, bias_d, t_b8, slow=False)
    DVE(lambda e: e.tensor_scalar(out=nw32, in0=nw32, scalar1=32.0, scalar2=None, op0=ALU.mult), r=[], w=[t_nw])
    DVE(lambda e: e.tensor_scalar(out=fnw, in0=fnw, scalar1=32.0, scalar2=None, op0=ALU.mult), r=[], w=[t_fn])
    DVE(lambda e: e.tensor_scalar(out=mhw, in0=mhw, scalar1=float(0.25 * np.sqrt(512.0)), scalar2=None, op0=ALU.mult), r=[], w=[t_mh])
    DVE(lambda e: e.tensor_scalar(out=cwt.rearrange("p a b -> p (a b)"), in0=cwt.rearrange("p a b -> p (a b)"), scalar1=0.5, scalar2=None, op0=ALU.mult), r=[], w=[t_cw])

    xctr = [0]

    def xload(src_rows, qn="act"):
        i = xctr[0] % 2
        xctr[0] += 1
        S.dma(qn, lambda e, sem: e.dma_start(out=xst[i], in_=src_rows).then_inc(sem, 16), (sem_x if qn == "act" else sem_xs)[i], writes=[t_xst[i]])
        return i

    def xprep_a(src_rows, qn="act", i=None):
        if i is None:
            i = xload(src_rows, qn)
        k = xk[0] % 2
        xk[0] += 1
        POOL(lambda e: e.memset(xsc[:, 4 * k:4 * k + 1], 0.0), w=[t_xsc[k]])
        ACT(lambda e: e.activation(out=xs[k], in_=xst[i], func=AF.Square, accum_out=xsc[:, 4 * k:4 * k + 1]), r=[t_xst[i]], w=[t_xs[k], t_xsc[k]])
        DVE(lambda e: e.tensor_scalar(out=xsc[:, 4 * k + 1:4 * k + 2], in0=xsc[:, 4 * k:4 * k + 1], scalar1=D * EPS, scalar2=None, op0=ALU.add),
            r=[], w=[t_xsc[k]])
        POOL(lambda e: e.tensor_tensor(out=xsc[:, 4 * k + 2:4 * k + 3], in0=xsc[:, 4 * k + 1:4 * k + 2], in1=mhalf[:, 0:1], op=ALU.pow),
             r=[t_cs[7]], w=[t_xsc[k]])
        ACT(lambda e: e.activation(out=xs[k], in_=xst[i], func=AF.Copy, scale=xsc[:, 4 * k + 2:4 * k + 3]), r=[t_xst[i], t_xsc[k]], w=[t_xs[k]])
        return k

    def xprep_b(k, dst_cols, tk_dst, also_halo=False, X=None):
        X = XC.x if X is None else X
        def tr(e):
            for kc in range(8):
                ins = e.transpose(out=psT[:, kc, :], in_=xs[k][:, kc * 128:(kc + 1) * 128], identity=ident)
            return ins
        PE(tr, r=[t_xs[k], t_cs[1]], w=[t_psT])
        DVE(lambda e: e.tensor_tensor(out=X[:, :, dst_cols], in0=psT, in1=nw32.unsqueeze(2).to_broadcast([128, 8, 128]), op=ALU.mult),
            r=[t_psT, t_nw], w=[tk_dst])
        if also_halo:
            DVE(lambda e: e.tensor_tensor(out=xh, in0=psT[:, :, 126:128], in1=nw32.unsqueeze(2).to_broadcast([128, 8, 2]), op=ALU.mult),
                r=[t_psT, t_nw], w=[t_xh])

    def xprep4_loads(row0):
        return [xload(xm[row0:row0 + 128, :]), xload(xm[row0 + 128:row0 + 256, :])]

    def xprep4_head(row0, li=None):
        if li is None:
            li = [None, None]
        return {0: xprep_a(xm[row0:row0 + 128, :], i=li[0]), 1: xprep_a(xm[row0 + 128:row0 + 256, :], i=li[1])}

    def xprep4(row0, ks=None, X=None, T=None):
        T = XC.t if T is None else T
        if ks is None:
            ks = xprep4_head(row0)
        for c in range(4):
            xprep_b(ks[c], slice(c * 128, (c + 1) * 128), T[c], X=X)
            if c + 2 < 4:
                ks[c + 2] = xprep_a(xm[row0 + (c + 2) * 128:row0 + (c + 3) * 128, :])

    def tokmajor_proj(slab, tslab, xcols, t_x, dst, t_dst, on_dve=False):
        b = nextbig()
        X = XC.x

        def mm(e):
            for kc in range(8):
                ins = e.matmul(big[b], lhsT=X[:, kc, xcols], rhs=slab[:, kc, :], start=(kc == 0), stop=(kc == 7))
            return ins
        PE(mm, r=[tslab, t_x], w=[t_big[b]])
        if on_dve:
            DVE(lambda e: e.tensor_copy(out=dst, in_=big[b]), r=[t_big[b]], w=list(t_dst))
        else:
            ACT(lambda e: e.activation(out=dst, in_=big[b], func=AF.Copy), r=[t_big[b]], w=list(t_dst))

    def gates(nch, xcol_fn, t_xs_list):
        g_ = GS[gctr[0] % 2]
        gctr[0] += 1
        X = XC.x

        def mm(e):
            for c in range(nch):
                for kc in range(8):
                    ins = e.matmul(psx[:, 160 + 8 * c:168 + 8 * c], lhsT=X[:, kc, xcol_fn(c)], rhs=wif[:, kc, :],
                                   start=(kc == 0), stop=(kc == 7))
            return ins
        PE(mm, r=[t_wif] + t_xs_list, w=[t_pg])
        pg = psx[:, 160:160 + 8 * nch].rearrange("p (c g) -> p c g", g=8)
        DVE(lambda e: e.tensor_tensor(out=g_.gpre[:, 0:nch, :], in0=pg, in1=bias8.unsqueeze(1).to_broadcast([128, nch, 8]), op=ALU.add),
            r=[t_pg, t_b8], w=[g_.t_gpre])
        ACT(lambda e: e.activation(out=g_.gth[:, 0:nch, :], in_=g_.gpre[:, 0:nch, :], func=AF.Tanh, scale=1.0 / 15.0), r=[g_.t_gpre], w=[g_.t_gth])
        DVE(lambda e: e.tensor_scalar(out=g_.li[:, 0:nch, :], in0=g_.gth[:, 0:nch, 0:4], scalar1=15.0, scalar2=None, op0=ALU.mult), r=[g_.t_gth], w=[g_.t_li])
        ACT(lambda e: e.activation(out=g_.gex[:, 0:nch, :], in_=g_.gth[:, 0:nch, 4:8], func=AF.Exp, scale=-15.0), r=[g_.t_gth], w=[g_.t_gex])
        ACT(lambda e: e.activation(out=g_.gex[:, 0:nch, :], in_=g_.gex[:, 0:nch, :], func=AF.Ln, bias=1.0, scale=1.0), r=[], w=[g_.t_gex])
        DVE(lambda e: e.tensor_scalar(out=g_.lf[:, 0:nch, :], in0=g_.gex[:, 0:nch, :], scalar1=-1.0, scalar2=None, op0=ALU.mult), r=[g_.t_gex], w=[g_.t_lf])
        return g_

    def chunk_gate_scalars(g_, c):
        c_ = CSs[gctr[1] % 2]
        gctr[1] += 1

        def mm(e):
            e.matmul(psx[:, 192:196], lhsT=maskA, rhs=g_.lf[:, c, :], start=True, stop=True)
            return e.matmul(psx[:, 200:204], lhsT=onesf, rhs=g_.lf[:, c, :], start=True, stop=True)
        PE(mm, r=[g_.t_lf, t_cs[2], t_cs[5]], w=[t_psf, t_ptot])
        DVE(lambda e: e.tensor_tensor(out=c_.gsum, in0=psx[:, 192:196], in1=g_.li[:, c, :], op=ALU.add), r=[t_psf, g_.t_li], w=[c_.t_gsum])
        ACT(lambda e: e.activation(out=c_.gg, in_=c_.gsum, func=AF.Exp), r=[c_.t_gsum], w=[c_.t_gg])
        ACT(lambda e: e.activation(out=c_.dec, in_=psx[:, 200:204], func=AF.Exp), r=[t_ptot], w=[c_.t_dec])
        return c_

    def state_update(c_, kt, tkt, vt, tvt, cast=True, kcols=None):
        if kcols is None:
            for h in range(4):
                DVE(lambda e, h=h: e.tensor_scalar(out=kt[:, h * 256:(h + 1) * 256], in0=kt[:, h * 256:(h + 1) * 256],
                                                   scalar1=c_.gg[:, h:h + 1], scalar2=None, op0=ALU.mult),
                    r=[c_.t_gg], w=[tkt[h]])
        else:
            def trk(e):
                for hd in range(8):
                    ins = e.transpose(out=psT[:, hd, :], in_=kT[:, hd, kcols], identity=ident)
                return ins
            PE(trk, r=list(t_kT) + [t_cs[1]], w=[t_psT])
            for h in range(4):
                DVE(lambda e, h=h: e.tensor_scalar(out=kt[:, h * 256:(h + 1) * 256].rearrange("p (a b) -> p a b", a=2), in0=psT[:, 2 * h:2 * h + 2, :],
                                                   scalar1=c_.gg[:, h:h + 1], scalar2=None, op0=ALU.mult),
                    r=[t_psT, c_.t_gg], w=[tkt[h]])
        yield
        for h in range(4):
            for dc in range(2):
                b = nextbig()
                PE(lambda e, h=h, dc=dc, b=b: e.matmul(big[b], lhsT=kt[:, h * 256 + dc * 128:h * 256 + (dc + 1) * 128],
                                                        rhs=vt[:, h * 512:(h + 1) * 512], start=True, stop=True),
                   r=[tkt[h], tvt[h]], w=[t_big[b]])
                DVE(lambda e, h=h, dc=dc, b=b: e.scalar_tensor_tensor(out=Cst[:, h, dc, :], in0=Cst[:, h, dc, :], scalar=c_.dec[:, h:h + 1],
                                                                      in1=big[b], op0=ALU.mult, op1=ALU.add),
                    r=[c_.t_dec, t_big[b]], w=[t_C[h][dc]])
                if cast:
                    ACT(lambda e, h=h, dc=dc: e.activation(out=Cbf[:, h, dc, :], in_=Cst[:, h, dc, :], func=AF.Copy), r=[t_C[h][dc]], w=[t_Cbf[h][dc]])
            yield

        def nmm(e):
            for h in range(4):
                for dc in range(2):
                    ins = e.matmul(psx[:, 128 + 2 * h + dc:129 + 2 * h + dc], lhsT=kt[:, h * 256 + dc * 128:h * 256 + (dc + 1) * 128],
                                   rhs=onesb[:, 0:1], start=True, stop=True)
            return ins
        PE(nmm, r=list(tkt) + [t_cs[6]], w=[t_pnu])
        DVE(lambda e: e.tensor_tensor(out=nst.rearrange("p (h d) -> p h d", d=2), in0=nst.rearrange("p (h d) -> p h d", d=2),
                                      in1=c_.dec.unsqueeze(2).to_broadcast([128, 4, 2]), op=ALU.mult), r=[c_.t_dec], w=[t_n])
        DVE(lambda e: e.tensor_tensor(out=nst, in0=nst, in1=psx[:, 128:136], op=ALU.add), r=[t_pnu], w=[t_n])
        if cast:
            DVE(lambda e: e.tensor_tensor(out=nbc, in0=onesb.unsqueeze(1).to_broadcast([128, 8, 128]),
                                          in1=nst.unsqueeze(2).to_broadcast([128, 8, 128]), op=ALU.mult), r=[t_n, t_cs[6]], w=[t_nbc])
        yield

    kvs = get_slabs(6)

    xsc3 = sb("xsc3", [128, 16], F32)
    t_xsc3 = [Tok() for _ in range(4)]

    def xprep_t0_chunk(c):
        stg = hnT[:, 4 * c:4 * c + 4, :].rearrange("p a b -> p (a b)").bitcast(F32)
        xsa = ysT[:, 2 * c:2 * c + 2, :].rearrange("p a b -> p (a b)")
        tk_stg = [t_hn[jj][cc] for jj in range(4 * c, 4 * c + 4) for cc in range(4)]
        tk_xsa = [t_ys[2 * c], t_ys[2 * c + 1]]
        S.dma("sp", lambda e, sem: e.dma_start(out=stg, in_=xm[c * 128:(c + 1) * 128, :]).then_inc(sem, 16), S.new_sem(f"t0x{c}"), writes=tk_stg)
        c0 = 4 * c
        POOL(lambda e: e.memset(xsc3[:, c0:c0 + 1], 0.0), w=[t_xsc3[c]])
        ACT(lambda e: e.activation(out=xsa, in_=stg, func=AF.Square, accum_out=xsc3[:, c0:c0 + 1]), r=tk_stg, w=tk_xsa + [t_xsc3[c]])
        DVE(lambda e: e.tensor_scalar(out=xsc3[:, c0 + 1:c0 + 2], in0=xsc3[:, c0:c0 + 1], scalar1=D * EPS, scalar2=None, op0=ALU.add), r=[], w=[t_xsc3[c]])
        POOL(lambda e: e.tensor_tensor(out=xsc3[:, c0 + 2:c0 + 3], in0=xsc3[:, c0 + 1:c0 + 2], in1=mhalf[:, 0:1], op=ALU.pow), r=[t_cs[7]], w=[t_xsc3[c]])
        ACT(lambda e: e.activation(out=xsa, in_=stg, func=AF.Copy, scale=xsc3[:, c0 + 2:c0 + 3]), r=tk_stg + [t_xsc3[c]], w=tk_xsa)

    def xprep_t0_chunk_b(c):
        xsa = ysT[:, 2 * c:2 * c + 2, :].rearrange("p a b -> p (a b)")
        tk_xsa = [t_ys[2 * c], t_ys[2 * c + 1]]

        def tr(e):
            for kc in range(8):
                ins = e.transpose(out=psT[:, kc, :], in_=xsa[:, kc * 128:(kc + 1) * 128], identity=ident)
            return ins
        PE(tr, r=tk_xsa + [t_cs[1]], w=[t_psT])
        DVE(lambda e: e.tensor_tensor(out=xnTs[0][:, :, c * 128:(c + 1) * 128], in0=psT, in1=nw32.unsqueeze(2).to_broadcast([128, 8, 128]), op=ALU.mult),
            r=[t_psT, t_nw], w=[t_xnTs[0][c]])

    T0_EARLY = PC >= 6
    ONCHIP = False
    sem_oc_in = [S.new_sem("ocin0"), S.new_sem("ocin1")]
    sem_oc_st = [S.new_sem("ocst0"), S.new_sem("ocst1")]
    t_oc = {"kv": [Tok(), Tok()], "o": [Tok(), Tok()]}
    if ONCHIP:
        t_scr["kv"] = t_oc["kv"]
        t_scr["o"] = t_oc["o"]

    def onchip_convert(it):
        h_ = it % 2
        stg = hnT[:, 8 * h_:8 * h_ + 8, :].rearrange("p a b -> p (a b)").bitcast(F32)
        obf = ysT[:, 4 * h_:4 * h_ + 4, :].rearrange("p a b -> p (a b)")
        tk_stg = [t_hn[jj][cc] for jj in range(8 * h_, 8 * h_ + 8) for cc in range(4)]
        tk_obf = [t_ys[4 * h_ + q_] for q_ in range(4)]
        if it < 16:
            kb, hf = it // 2, it % 2
            c0 = OFF_K + 1536 * hf
            src = w_in[kb * 128:(kb + 1) * 128, c0:c0 + 1536]
            dst = s_in[kb * 128:(kb + 1) * 128, c0:c0 + 1536]
            ncol, name = 1536, "kv"
        else:
            kb = 2 * (it - 16)
            src = wo[kb * 128:(kb + 2) * 128, :].rearrange("(a p) c -> p a c", p=128)
            dst = s_o[kb * 128:(kb + 2) * 128, :].rearrange("(a p) c -> p a c", p=128)
            ncol, name = 2048, "o"
        sv = stg[:, 0:ncol] if it < 16 else stg[:, 0:ncol].rearrange("p (a c) -> p a c", a=2)
        ov = obf[:, 0:ncol] if it < 16 else obf[:, 0:ncol].rearrange("p (a c) -> p a c", a=2)
        S.dma("sp", lambda e, sem: e.dma_start(out=sv, in_=src).then_inc(sem, 16), sem_oc_in[h_], writes=tk_stg)
        ACT(lambda e: e.activation(out=obf[:, 0:ncol], in_=stg[:, 0:ncol], func=AF.Copy), r=tk_stg, w=tk_obf)
        S.dma("sp", lambda e, sem: e.dma_start(out=dst, in_=ov).then_inc(sem, 16), sem_oc_st[h_], reads=tk_obf, writes=[t_oc[name][h_]])

    oc_it = [0]
    gates_of = {}
    kx_of = {0: xprep_a(xp[0:128, :], "sp")}
    if PC > 1:
        kx_of[1] = xprep_a(xp[128:256, :], "sp")
    xprep_b(kx_of[0], slice(0, 128), XC.t[0], also_halo=(PC == 1))

    def prefix_A(pc):
        ci = pc % 4
        xsl = slice(ci * 128, (ci + 1) * 128)
        if pc + 2 < PC:
            kx_of[pc + 2] = xprep_a(xp[(pc + 2) * 128:(pc + 3) * 128, :], "sp")
        yield
        for nb in range(2):
            tokmajor_proj(kvs[nb][0], kvs[nb][1], xsl, XC.t[ci], ktok[pc % 2][:, nb * 512:(nb + 1) * 512], t_ktok[pc % 2][2 * nb:2 * nb + 2], on_dve=True)
            yield
        if pc + 1 < PC:
            cn = (pc + 1) % 4
            xprep_b(kx_of[pc + 1], slice(cn * 128, (cn + 1) * 128), XC.t[cn], also_halo=(pc + 1 == PC - 1))
        for nb in range(4):
            tokmajor_proj(kvs[2 + nb][0], kvs[2 + nb][1], xsl, XC.t[ci], vtok[ci][:, nb * 512:(nb + 1) * 512], [t_vtok[ci][nb]], on_dve=(nb == 3))
            yield
        gates_of[pc] = gates(1, lambda c, xsl=xsl: xsl, [XC.t[ci]])
        yield

    def prefix_B(pc):
        ci = pc % 4
        c_ = chunk_gate_scalars(gates_of[pc], 0)
        yield
        yield from state_update(c_, ktok[pc % 2], t_ktok[pc % 2], vtok[ci], t_vtok[ci], cast=(pc == PC - 1))

    for _ in prefix_A(0):
        pass
    for pc in range(PC):
        if T0_EARLY and 0 <= pc - (PC - 6) < 4:
            xprep_t0_chunk(pc - (PC - 6))
        if T0_EARLY and 0 <= pc - (PC - 5) < 4:
            xprep_t0_chunk_b(pc - (PC - 5))
        for _ in range(2):
            if conv_list:
                convert(*conv_list.pop(0))
        if ONCHIP and pc < PC - 5:
            for _ in range(2):
                if oc_it[0] < 20:
                    onchip_convert(oc_it[0])
                    oc_it[0] += 1
        ga = prefix_A(pc + 1) if pc + 1 < PC else iter(())
        gb = prefix_B(pc)
        for _ in range(3):
            next(ga, None)
        alive = True
        while alive:
            alive = False
            for g in (ga, gb):
                try:
                    next(g)
                    alive = True
                except StopIteration:
                    pass
    nbig[0] = 4
    while conv_list:
        convert(*conv_list.pop(0))

    def norm_batch(c):
        cols = slice(c * 128, (c + 1) * 128)
        DVE(lambda e: e.tensor_scalar(out=mall, in0=mall, scalar1=1.0, scalar2=None, op0=ALU.max), r=[], w=list(t_mall))
        DVE(lambda e: e.scalar_tensor_tensor(out=mall, in0=mall, scalar=512.0 * EPS, in1=mall, op0=ALU.mult, op1=ALU.mult), r=[], w=list(t_mall))
        DVE(lambda e: e.tensor_tensor(out=v5all, in0=v5all, in1=mall, op=ALU.add), r=list(t_mall), w=list(t_v5all))
        ACT(lambda e: e.activation(out=v5all, in_=v5all, func=AF.Sqrt), r=[], w=list(t_v5all))
        DVE(lambda e: e.reciprocal(out=v5all, in_=v5all), r=[], w=list(t_v5all))
        DVE(lambda e: e.tensor_tensor(out=hnT[:, :, cols].rearrange("p (h j) t -> p h j t", j=4), in0=hball,
                                      in1=v5all.unsqueeze(2).to_broadcast([128, 4, 4, 128]), op=ALU.mult),
            r=list(t_hball) + list(t_v5all), w=[t_hn[jj][c] for jj in range(16)])

    def mlstm_chunk(g_, c, deferred=None):
        cols = slice(c * 128, (c + 1) * 128)
        c_ = chunk_gate_scalars(g_, c)
        for h in range(4):
            DVE(lambda e, h=h: e.tensor_scalar(out=A_[h], in0=triB, scalar1=g_.lf[:, c, h:h + 1], scalar2=None, op0=ALU.mult),
                r=[g_.t_lf, t_cs[3]], w=[t_A[h]])
        yield
        for h in range(4):
            i = h % 2

            def mm1(e, h=h, i=i):
                e.matmul(psm[:, 0:128], lhsT=maskA, rhs=A_[h], start=True, stop=False)
                e.matmul(psm[:, 0:128], lhsT=ident, rhs=negm, start=False, stop=True)
                e.matmul(psm[:, 128:256], lhsT=onesf, rhs=A_[h], start=True, stop=True)
                e.matmul(psm[:, 256:384], lhsT=kT[:, 2 * h, cols], rhs=qT[:, 2 * h, cols], start=True, stop=False)
                return e.matmul(psm[:, 256:384], lhsT=kT[:, 2 * h + 1, cols], rhs=qT[:, 2 * h + 1, cols], start=False, stop=True)
            PE(mm1, r=[t_A[h], t_cs[2], t_cs[5], t_cs[1], t_cs[4], t_kT[2 * h], t_kT[2 * h + 1], t_qT[2 * h], t_qT[2 * h + 1]],
               w=[t_psm[0]])
            ACT(lambda e, h=h, i=i: e.activation(out=E_[i], in_=psm[:, 0:128], func=AF.Exp, bias=g_.li[:, c, h:h + 1], scale=1.0),
                r=[t_psm[0], g_.t_li], w=[t_E[i]])
            ACT(lambda e, i=i: e.activation(out=G_[i], in_=psm[:, 128:256], func=AF.Exp), r=[t_psm[1]], w=[t_G[i]])
            DVE(lambda e, i=i: e.tensor_tensor(out=ST_[i], in0=psm[:, 256:384], in1=E_[i], op=ALU.mult), r=[t_psm[2], t_E[i]], w=[t_ST[i]])
            DVE(lambda e, h=h, i=i: e.tensor_tensor(out=qG_[i], in0=qT[:, 2 * h:2 * h + 2, cols],
                                                    in1=G_[i].unsqueeze(1).to_broadcast([128, 2, 128]), op=ALU.mult),
                r=[t_qT[2 * h], t_qT[2 * h + 1], t_G[i]], w=[t_qG[i]])
            if h == 0 and deferred is not None:
                deferred()
            yield

            def mm2(e, h=h, i=i):
                for j in range(4):
                    e.matmul(psh[:, j, :], lhsT=vtok[c][:, h * 512 + j * 128:h * 512 + (j + 1) * 128], rhs=ST_[i], start=True, stop=False)
                    e.matmul(psh[:, j, :], lhsT=Cbf[:, h, 0, j * 128:(j + 1) * 128], rhs=qG_[i][:, 0, :], start=False, stop=False)
                    e.matmul(psh[:, j, :], lhsT=Cbf[:, h, 1, j * 128:(j + 1) * 128], rhs=qG_[i][:, 1, :], start=False, stop=True)
                e.matmul(psm[:, 384:512], lhsT=onesb, rhs=ST_[i], start=True, stop=False)
                e.matmul(psm[:, 384:512], lhsT=nbc[:, 2 * h, :], rhs=qG_[i][:, 0, :], start=False, stop=False)
                return e.matmul(psm[:, 384:512], lhsT=nbc[:, 2 * h + 1, :], rhs=qG_[i][:, 1, :], start=False, stop=True)
            PE(mm2, r=[t_vtok[c][h], t_ST[i], t_qG[i], t_Cbf[h][0], t_Cbf[h][1], t_nbc, t_cs[6]], w=[t_psh, t_psm[3]])
            ACT(lambda e, i=i: e.activation(out=sq_[i], in_=psh, func=AF.Square), r=[t_psh], w=[t_sq[i]])
            DVE(lambda e, h=h: e.tensor_copy(out=hball[:, h], in_=psh), r=[t_psh], w=[t_hball[h]])
            ACT(lambda e, h=h: e.activation(out=mall[:, h, :], in_=psm[:, 384:512], func=AF.Abs), r=[t_psm[3]], w=[t_mall[h]])
            yield

            def mm3(e, i=i):
                for j in range(4):
                    ins = e.matmul(psx[:, 0:128], lhsT=onesb, rhs=sq_[i][:, j, :], start=(j == 0), stop=(j == 3))
                return ins
            PE(mm3, r=[t_sq[i], t_cs[6]], w=[t_pss])
            DVE(lambda e, h=h: e.tensor_copy(out=v5all[:, h, :], in_=psx[:, 0:128]), r=[t_pss], w=[t_v5all[h]])
            yield
        yield from state_update(c_, ktok[c % 2], t_ktok[c % 2], vtok[c], t_vtok[c], kcols=cols)

    for tile in range(TM):
        r0 = tile * 512
        XC.x = xnTs[tile % 2]
        XC.t = t_xnTs[tile % 2]
        Xn, Tn = xnTs[(tile + 1) % 2], t_xnTs[(tile + 1) % 2]
        if tile == 0 and not T0_EARLY:
            xprep4(r0)
        slkv = kvs if tile == 0 else get_slabs(6)
        for sidx in range(2):
            slab, tslab = slkv[sidx]
            for l in range(4):
                fc = sidx * 4 + l
                b = nextbig()

                def mm(e, slab=slab, l=l, b=b, X=XC.x):
                    for kc in range(8):
                        ins = e.matmul(big[b], lhsT=slab[:, kc, l * 128:(l + 1) * 128], rhs=X[:, kc, :], start=(kc == 0), stop=(kc == 7))
                    return ins
                PE(mm, r=[tslab] + XC.t, w=[t_big[b]])
                ACT(lambda e, fc=fc, b=b: e.activation(out=kT[:, fc, :], in_=big[b], func=AF.Copy), r=[t_big[b]], w=[t_kT[fc]])
        for c in range(4):
            cs = slice(c * 128, (c + 1) * 128)
            for nb in range(4):
                tokmajor_proj(slkv[2 + nb][0], slkv[2 + nb][1], cs, XC.t[c], vtok[c][:, nb * 512:(nb + 1) * 512], [t_vtok[c][nb]], on_dve=(nb % 2 == 1))
        release(6)
        g_tile = gates(4, lambda c: slice(c * 128, (c + 1) * 128), XC.t)
        slq = get_slabs(2)
        for sidx in range(2):
            slab, tslab = slq[sidx]
            for l in range(4):
                fc = sidx * 4 + l
                b = nextbig()

                def mm(e, slab=slab, l=l, b=b, X=XC.x):
                    for kc in range(8):
                        ins = e.matmul(big[b], lhsT=slab[:, kc, l * 128:(l + 1) * 128], rhs=X[:, kc, :], start=(kc == 0), stop=(kc == 7))
                    return ins
                PE(mm, r=[tslab] + XC.t, w=[t_big[b]])
                ACT(lambda e, fc=fc, b=b: e.activation(out=qT[:, fc, :], in_=big[b], func=AF.Copy, scale=1.0 / 16.0),
                    r=[t_big[b]], w=[t_qT[fc], t_mg[fc]])
        release(2)

        def conv_units():
            for g in range(4):
                sl4 = get_slabs(4)
                for l in range(4):
                    cc = 4 * g + l
                    bs = []
                    for which in range(4):
                        slab, tslab = sl4[which]
                        b = nextbig()
                        bs.append(b)

                        def mm(e, slab=slab, l=l, b=b, X=XC.x):
                            for kc in range(8):
                                ins = e.matmul(big[b], lhsT=slab[:, kc, l * 128:(l + 1) * 128], rhs=X[:, kc, :], start=(kc == 0), stop=(kc == 7))
                            return ins
                        PE(mm, r=[tslab] + XC.t, w=[t_big[b]])
                        if tile == 0 and which < 2:
                            def mmh(e, slab=slab, l=l, which=which):
                                for kc in range(8):
                                    ins = e.matmul(psx[:, 208 + 2 * which:210 + 2 * which], lhsT=slab[:, kc, l * 128:(l + 1) * 128], rhs=xh[:, kc, :],
                                                   start=(kc == 0), stop=(kc == 7))
                                return ins
                            PE(mmh, r=[tslab, t_xh], w=[t_phalo])
                        if which == 0:
                            ACT(lambda e, b=b: e.activation(out=tmp[0][:, 0:512], in_=big[b], func=AF.Copy), r=[t_big[b]], w=[t_tmp[0]])
                        elif which == 1:
                            if tile == 0:
                                ACT(lambda e: e.activation(out=tmp[5][:, 0:2], in_=psx[:, 208:210], func=AF.Copy), r=[t_phalo], w=[t_tmp[5]])
                                DVE(lambda e, cc=cc: e.tensor_tensor(out=uh[:, cc, :], in0=psx[:, 210:212], in1=tmp[5][:, 0:2], op=ALU.mult),
                                    r=[t_phalo, t_tmp[5]], w=[t_uh[cc]])
                            DVE(lambda e, b=b: e.tensor_tensor(out=tmp[1][:, 2:514], in0=big[b], in1=tmp[0][:, 0:512], op=ALU.mult),
                                r=[t_big[b], t_tmp[0]], w=[t_tmp[1]])
                            DVE(lambda e, cc=cc: e.tensor_copy(out=tmp[1][:, 0:2], in_=uh[:, cc, :]), r=[t_uh[cc]], w=[t_tmp[1]])
                            POOL(lambda e, cc=cc: e.tensor_tensor(out=tmp[2][:, 0:512], in0=tmp[1][:, 0:512], in1=cwt[:, cc, 0:1].to_broadcast([128, 512]), op=ALU.mult),
                                 r=[t_tmp[1], t_cw], w=[t_tmp[2]])
                            POOL(lambda e, cc=cc: e.tensor_tensor(out=tmp[0][:, 0:512], in0=tmp[1][:, 1:513], in1=cwt[:, cc, 1:2].to_broadcast([128, 512]), op=ALU.mult),
                                 r=[t_tmp[1], t_cw], w=[t_tmp[0]])
                            POOL(lambda e: e.tensor_tensor(out=tmp[2][:, 0:512], in0=tmp[2][:, 0:512], in1=tmp[0][:, 0:512], op=ALU.add),
                                 r=[t_tmp[0]], w=[t_tmp[2]])
                            POOL(lambda e, cc=cc: e.tensor_tensor(out=tmp[0][:, 0:512], in0=tmp[1][:, 2:514], in1=cwt[:, cc, 2:3].to_broadcast([128, 512]), op=ALU.mult),
                                 r=[t_tmp[1], t_cw], w=[t_tmp[0]])
                            POOL(lambda e: e.tensor_tensor(out=tmp[2][:, 0:512], in0=tmp[2][:, 0:512], in1=tmp[0][:, 0:512], op=ALU.add),
                                 r=[t_tmp[0]], w=[t_tmp[2]])
                            DVE(lambda e, cc=cc: e.tensor_copy(out=uh[:, cc, :], in_=tmp[1][:, 512:514]), r=[t_tmp[1]], w=[t_uh[cc]])
                        elif which == 2:
                            ACT(lambda e, b=b: e.activation(out=tmp[3][:, 0:512], in_=big[b], func=AF.Tanh, scale=0.5), r=[t_big[b]], w=[t_tmp[3]])
                            DVE(lambda e, b=b: e.scalar_tensor_tensor(out=tmp[3][:, 0:512], in0=tmp[3][:, 0:512], scalar=1.0, in1=big[b],
                                                                      op0=ALU.add, op1=ALU.mult), r=[t_big[b]], w=[t_tmp[3]])
                        else:
                            DVE(lambda e, b=b: e.tensor_tensor(out=tmp[2][:, 0:512], in0=big[b], in1=tmp[2][:, 0:512], op=ALU.mult),
                                r=[t_big[b]], w=[t_tmp[2]])
                            POOL(lambda e, cc=cc: e.tensor_tensor(out=ysT[:, cc, :], in0=tmp[3][:, 0:512], in1=tmp[2][:, 0:512], op=ALU.mult),
                                 r=[t_tmp[2], t_tmp[3]], w=[t_ys[cc]])
                        yield
                release(4)

        cu = conv_units()
        li_next = xprep4_loads(r0 + 512) if tile + 1 < TM else None
        if tile == 0:
            for c_ in conv_late["t0a"]:
                convert(*c_)
        for c in range(4):
            if tile == 0 and c == 1:
                for c_ in conv_late["t0b"]:
                    convert(*c_)
            dfr = (lambda cc=c - 1: norm_batch(cc)) if c > 0 else None
            gen = mlstm_chunk(g_tile, c, dfr)
            yi = 0
            while True:
                if yi % 8 != 7:
                    next(cu, None)
                try:
                    next(gen)
                except StopIteration:
                    break
                yi += 1
        norm_batch(3)
        if tile == 0:
            for c_ in conv_late["t0c"]:
                convert(*c_)
        ks_next = None
        if tile + 1 < TM:
            ks_next = {0: xprep_a(None, i=li_next[0])}
        for _ in cu:
            pass

        for g in range(4):
            (so, tso), (sz, tsz) = get_slabs(2)
            for l in range(4):
                j = 4 * g + l
                bo, bz = nextbig(), nextbig()

                def mmo(e, so=so, l=l, bo=bo, X=XC.x):
                    for kc in range(8):
                        ins = e.matmul(big[bo], lhsT=so[:, kc, l * 128:(l + 1) * 128], rhs=X[:, kc, :], start=(kc == 0), stop=(kc == 7))
                    return ins

                def mmz(e, sz=sz, l=l, bz=bz, X=XC.x):
                    for kc in range(8):
                        ins = e.matmul(big[bz], lhsT=sz[:, kc, l * 128:(l + 1) * 128], rhs=X[:, kc, :], start=(kc == 0), stop=(kc == 7))
                    return ins
                PE(mmo, r=[tso] + XC.t, w=[t_big[bo]])
                PE(mmz, r=[tsz] + XC.t, w=[t_big[bz]])
                ACT(lambda e, bo=bo: e.activation(out=tmp[0][:, 0:512], in_=big[bo], func=AF.Tanh, scale=0.5), r=[t_big[bo]], w=[t_tmp[0]])
                ACT(lambda e, bz=bz: e.activation(out=tmp[1][:, 0:512], in_=big[bz], func=AF.Tanh, scale=0.5), r=[t_big[bz]], w=[t_tmp[1]])
                DVE(lambda e, bz=bz: e.scalar_tensor_tensor(out=tmp[1][:, 0:512], in0=tmp[1][:, 0:512], scalar=1.0, in1=big[bz], op0=ALU.add, op1=ALU.mult),
                    r=[t_big[bz]], w=[t_tmp[1]])
                DVE(lambda e: e.scalar_tensor_tensor(out=tmp[1][:, 0:512], in0=tmp[0][:, 0:512], scalar=1.0, in1=tmp[1][:, 0:512], op0=ALU.add, op1=ALU.mult),
                    r=[t_tmp[0]], w=[t_tmp[1]])
                DVE(lambda e, j=j: e.scalar_tensor_tensor(out=hnT[:, j, :], in0=hnT[:, j, :], scalar=mhw[:, j:j + 1], in1=tmp[1][:, 0:512],
                                                          op0=ALU.mult, op1=ALU.mult),
                    r=[t_tmp[1], t_mh] + t_hn[j], w=[t_a[j]])
            release(2)
            if ks_next is not None:
                if g + 1 < 4:
                    ks_next[g + 1] = xprep_a(None, i=li_next[g + 1])
                xprep_b(ks_next[g], slice(g * 128, (g + 1) * 128), Tn[g], X=Xn)
                if g + 2 < 4:
                    li_next.append(xload(xm[r0 + 512 + (g + 2) * 128:r0 + 512 + (g + 3) * 128, :]))

        gsl = None
        for cb in range(4):
            (sa, tsa), (sbb, tsb) = get_slabs(2)
            if cb % 2 == 0:
                gsl = get_slabs(2)
            for c2 in range(2):
                cch = 2 * cb + c2
                lg = cch % 4
                bpa, bpb, bga, bgb = nextbig(), nextbig(), nextbig(), nextbig()

                def mmp(e, slab, src, b, c2=c2):
                    for kc in range(16):
                        ins = e.matmul(big[b], lhsT=slab[:, kc, c2 * 128:(c2 + 1) * 128], rhs=src[:, kc, :], start=(kc == 0), stop=(kc == 15))
                    return ins

                def mmg(e, slab, b, lg=lg, X=XC.x):
                    for kc in range(8):
                        ins = e.matmul(big[b], lhsT=slab[:, kc, lg * 128:(lg + 1) * 128], rhs=X[:, kc, :], start=(kc == 0), stop=(kc == 7))
                    return ins
                PE(lambda e, sa=sa, bpa=bpa, mmp=mmp: mmp(e, sa, hnT, bpa), r=[tsa] + t_a, w=[t_big[bpa]])
                PE(lambda e, gs=gsl[0][0], bga=bga, mmg=mmg: mmg(e, gs, bga), r=[gsl[0][1]] + XC.t, w=[t_big[bga]])
                PE(lambda e, sbb=sbb, bpb=bpb, mmp=mmp: mmp(e, sbb, ysT, bpb), r=[tsb] + t_ys, w=[t_big[bpb]])
                PE(lambda e, gs=gsl[1][0], bgb=bgb, mmg=mmg: mmg(e, gs, bgb), r=[gsl[1][1]] + XC.t, w=[t_big[bgb]])
                ACT(lambda e, bga=bga: e.activation(out=tmp[2][:, 0:512], in_=big[bga], func=AF.Tanh, scale=0.5), r=[t_big[bga]], w=[t_tmp[2]])
                ACT(lambda e, bgb=bgb: e.activation(out=tmp[3][:, 0:512], in_=big[bgb], func=AF.Tanh, scale=0.5), r=[t_big[bgb]], w=[t_tmp[3]])
                DVE(lambda e, bpa=bpa: e.scalar_tensor_tensor(out=tmp[2][:, 0:512], in0=tmp[2][:, 0:512], scalar=1.0, in1=big[bpa], op0=ALU.add, op1=ALU.mult),
                    r=[t_big[bpa]], w=[t_tmp[2]])
                DVE(lambda e, bpb=bpb: e.scalar_tensor_tensor(out=tmp[3][:, 0:512], in0=tmp[3][:, 0:512], scalar=1.0, in1=big[bpb], op0=ALU.add, op1=ALU.mult),
                    r=[t_big[bpb]], w=[t_tmp[3]])
                POOL(lambda e, cch=cch: e.tensor_tensor(out=qT[:, cch, :], in0=tmp[2][:, 0:512], in1=tmp[3][:, 0:512], op=ALU.add),
                     r=[t_tmp[2], t_tmp[3]], w=[t_mg[cch], t_qT[cch]])
            release(2 if cb % 2 == 0 else 4)

        (so0, tso0), (so1, tso1) = get_slabs(2)
        for c in range(4):
            cs = slice(c * 128, (c + 1) * 128)
            i = xctr[0] % 2
            xctr[0] += 1
            S.dma("act", lambda e, sem, i=i, c=c, r0=r0: e.dma_start(out=xst[i], in_=xm[r0 + c * 128:r0 + (c + 1) * 128, :]).then_inc(sem, 16),
                  sem_x[i], writes=[t_xst[i]])
            bsl = []
            for nb, (slab, tslab) in enumerate(((so0, tso0), (so1, tso1))):
                b = nextbig()
                bsl.append(b)

                def mm(e, slab=slab, b=b, cs=cs):
                    for kc in range(8):
                        ins = e.matmul(big[b], lhsT=qT[:, kc, cs], rhs=slab[:, kc, :], start=(kc == 0), stop=(kc == 7))
                    return ins
                PE(mm, r=[tslab] + t_mg, w=[t_big[b]])
            for nb in range(2):
                DVE(lambda e, nb=nb, b=bsl[nb], i=i: e.scalar_tensor_tensor(out=xst[i][:, nb * 512:(nb + 1) * 512], in0=big[b], scalar=0.5,
                                                                            in1=xst[i][:, nb * 512:(nb + 1) * 512], op0=ALU.mult, op1=ALU.add),
                    r=[t_big[bsl[nb]]], w=[t_xst[i]])
            kj = xk[0] % 2
            cj = 4 + (c % 2) * 2
            POOL(lambda e, cj=cj: e.memset(xsc2[:, cj - 4:cj - 3], 0.0), w=[t_xsc2[c % 2]])
            ACT(lambda e, kj=kj, i=i, cj=cj: e.activation(out=xs[kj], in_=xst[i], func=AF.Square, accum_out=xsc2[:, cj - 4:cj - 3]),
                r=[t_xst[i]], w=[t_xs[kj], t_xsc2[c % 2]])
            DVE(lambda e, cj=cj: e.tensor_scalar(out=xsc2[:, cj - 3:cj - 2], in0=xsc2[:, cj - 4:cj - 3], scalar1=D * EPS, scalar2=None, op0=ALU.add),
                r=[], w=[t_xsc2[c % 2]])
            POOL(lambda e, cj=cj: e.tensor_tensor(out=xsc2[:, cj - 3:cj - 2], in0=xsc2[:, cj - 3:cj - 2], in1=mhalf[:, 0:1], op=ALU.pow),
                 r=[t_cs[7]], w=[t_xsc2[c % 2]])
            DVE(lambda e, i=i, cj=cj: e.scalar_tensor_tensor(out=xst[i], in0=xst[i], scalar=xsc2[:, cj - 3:cj - 2], in1=fnw, op0=ALU.mult, op1=ALU.mult),
                r=[t_xsc2[c % 2], t_fn], w=[t_xst[i]])
            S.dma("sp", lambda e, sem, c=c, r0=r0, i=i: e.dma_start(out=y_out[r0 + c * 128:r0 + (c + 1) * 128, :], in_=xst[i]).then_inc(sem, 16),
                  sem_out[i], reads=[t_xst[i]], writes=[])
        release(2)
    fins = []
    for so in sem_out:
        f_ = Tok()
        f_.w = (so, so.val)
        fins.append(f_)
    S.wait_all("sp", fins)
    S.emit()
    return nc


_NC_CACHE = {}


def kernel(x, meta_tokens, norm_w, w_in, b_igate, b_fgate, mh_norm_w, conv_w, w_proj_a, w_proj_b, w_out, final_norm_w):
    x = np.asarray(x, dtype=np.float32)
    B, SEQ, Dm = x.shape
    half = SEQ // 2
    TM = half // 512
    PC = 1 + half // 128
    key = (TM, PC)
    if key not in _NC_CACHE:
        _NC_CACHE[key] = build(TM, PC)
    nc = _NC_CACHE[key]
    meta = np.asarray(meta_tokens, dtype=np.float32)
    common = {
        "w_in": np.ascontiguousarray(np.asarray(w_in, np.float32)[0]),
        "wa": np.ascontiguousarray(np.asarray(w_proj_a, np.float32)[0]),
        "wb": np.ascontiguousarray(np.asarray(w_proj_b, np.float32)[0]),
        "wo": np.ascontiguousarray(np.asarray(w_out, np.float32)[0]),
        "norm_w": np.ascontiguousarray(np.asarray(norm_w, np.float32)[0].reshape(8, 128).T),
        "bias8": np.ascontiguousarray(np.broadcast_to(np.concatenate([np.asarray(b_igate, np.float32)[0],
                                                                       np.asarray(b_fgate, np.float32)[0]])[None, :], (128, 8))),
        "mhw": np.ascontiguousarray(np.asarray(mh_norm_w, np.float32)[0].reshape(16, 128).T),
        "cw": np.ascontiguousarray(np.asarray(conv_w, np.float32)[0].reshape(3, 16, 128).transpose(2, 1, 0).reshape(128, 48)),
        "fnw": np.ascontiguousarray(np.broadcast_to(np.asarray(final_norm_w, np.float32)[None, :], (128, Dm))),
    }
    in_maps = []
    for b in range(B):
        for s in range(2):
            xmain = np.ascontiguousarray(x[b, s * half:(s + 1) * half])
            xpre = np.zeros((PC * 128, Dm), np.float32)
            if s == 0:
                xpre[PC * 128 - 16:] = meta
            else:
                xpre[112:128] = meta
                xpre[128:] = x[b, 0:half]
            m = dict(common)
            m["xm"] = xmain
            m["xp"] = xpre
            in_maps.append(m)
    res = run_bass_kernel_spmd(nc, in_maps, core_ids=list(range(len(in_maps))))
    out = np.empty((B, SEQ, Dm), np.float32)
    for b in range(B):
        for s in range(2):
            out[b, s * half:(s + 1) * half] = res.results[2 * b + s]["y"]
    return out
```
